# Optimizing a Trainium2 kernel written in Bass

```python
import math
import jax
import jax.numpy as jnp
from jax import lax
import numpy as np


D_MODEL = 1024
BATCH = 8
SEQ = 2048
DEPTH = 4

GRID_W = 64
CTX_LEN = 256
Q_BLOCK = 128
ROPE_BASE = 10000.0
EPS = 1e-6

A_HEADS = 4
A_HD = 64
A_VD = 2 * A_HD
B_HEADS = 8
B_NOPE = 64
B_ROPE = 32
B_QK = B_NOPE + B_ROPE
B_VD = 64
B_QLORA = 384
B_KVLORA = 256
POOL_WINDOWS = (2, 4, 8, 16)
POOL_GROUP = 128
C_WIDTH = len(POOL_WINDOWS) * POOL_GROUP
D_WIDTH = 512
N_BRANCH = 4
BRANCH_W = 512
D_FF = 4 * D_MODEL

IN_SIZES = (2 * A_HEADS * A_HD, 2 * A_HEADS * A_HD, A_HEADS * A_VD,
            B_QLORA, B_KVLORA, B_ROPE,
            C_WIDTH,
            D_WIDTH, D_WIDTH, D_WIDTH,
            N_BRANCH * D_MODEL)
D_IN = sum(IN_SIZES)

kernel_name = 'hybrid_parallel_gated_diffusion_block'


def rmsnorm(x, g):
    xf = x.astype(jnp.float32)
    y = xf * lax.rsqrt(jnp.mean(xf * xf, axis=-1, keepdims=True) + EPS)
    return (y * g.astype(jnp.float32)).astype(x.dtype)


def modulate(h, shift, scale):
    return h * (1.0 + scale) + shift


def rope_tables(n_tok, rot_dim):
    rows = n_tok // GRID_W
    row = jnp.repeat(jnp.arange(rows, dtype=jnp.float32), GRID_W)
    col = jnp.tile(jnp.arange(GRID_W, dtype=jnp.float32), rows)
    n_freq = rot_dim // 4
    inv = ROPE_BASE ** (-jnp.arange(n_freq, dtype=jnp.float32) / n_freq)
    ang = jnp.concatenate([row[:, None] * inv, col[:, None] * inv], axis=-1)
    return jnp.cos(ang), jnp.sin(ang)


def apply_rope(t, cos, sin):
    half = t.shape[-1] // 2
    tf = t.astype(jnp.float32)
    t1, t2 = tf[..., :half], tf[..., half:]
    return jnp.concatenate([t1 * cos - t2 * sin, t1 * sin + t2 * cos], axis=-1).astype(t.dtype)


def split_in(p):
    outs = []
    off = 0
    for n in IN_SIZES:
        outs.append(p[..., off:off + n])
        off += n
    return outs


def to_heads(t, n_heads):
    b, s, _ = t.shape
    return t.reshape(b, s, n_heads, -1).transpose(0, 2, 1, 3)


def from_heads(t):
    b, h, s, d = t.shape
    return t.transpose(0, 2, 1, 3).reshape(b, s, h * d)


def attend(qs, ks, v, coef):
    m, b, h, s, dk = qs.shape
    nb = s // Q_BLOCK
    qb = jnp.moveaxis(qs.reshape(m, b, h, nb, Q_BLOCK, dk), 3, 0)
    scale = dk ** -0.5

    def block(qi):
        sc = jnp.einsum('mbhqd,mbhkd->mbhqk', qi, ks, preferred_element_type=jnp.float32) * scale
        p = jnp.einsum('m,mbhqk->bhqk', coef, jax.nn.softmax(sc, axis=-1))
        return jnp.einsum('bhqk,bhkd->bhqd', p.astype(v.dtype), v)

    o = lax.map(block, qb)
    return jnp.moveaxis(o, 0, 2).reshape(b, h, s, v.shape[-1])


def diff_qkv(pq, pk, pv, gq, gk, rope):
    b, s, _ = pq.shape
    q = rmsnorm(pq.reshape(b, s, A_HEADS, 2, A_HD), gq).transpose(3, 0, 2, 1, 4)
    k = rmsnorm(pk.reshape(b, s, A_HEADS, 2, A_HD), gk).transpose(3, 0, 2, 1, 4)
    v = to_heads(pv, A_HEADS)
    if rope is not None:
        q = apply_rope(q, *rope)
        k = apply_rope(k, *rope)
    return q, k, v


def diff_post(o, g_sub, lam_init):
    return from_heads(rmsnorm(o, g_sub) * (1.0 - lam_init))


def mla_qkv(pcq, pckv, pkr, g_cq, w_uq, g_ckv, w_ukv, gq, gk, rope):
    b, s, _ = pcq.shape
    q = (rmsnorm(pcq, g_cq) @ w_uq).reshape(b, s, B_HEADS, B_QK)
    kv = (rmsnorm(pckv, g_ckv) @ w_ukv).reshape(b, s, B_HEADS, B_NOPE + B_VD)
    q = jnp.concatenate([rmsnorm(q[..., :B_NOPE], gq[:B_NOPE]),
                         rmsnorm(q[..., B_NOPE:], gq[B_NOPE:])], axis=-1).transpose(0, 2, 1, 3)
    k_nope = rmsnorm(kv[..., :B_NOPE], gk[:B_NOPE]).transpose(0, 2, 1, 3)
    k_rope = rmsnorm(pkr, gk[B_NOPE:])[:, None, :, :]
    v = kv[..., B_NOPE:].transpose(0, 2, 1, 3)
    if rope is not None:
        q = jnp.concatenate([q[..., :B_NOPE], apply_rope(q[..., B_NOPE:], *rope)], axis=-1)
        k_rope = apply_rope(k_rope, *rope)
    k = jnp.concatenate([k_nope, jnp.broadcast_to(k_rope, k_nope.shape[:3] + (B_ROPE,))], axis=-1)
    return q, k, v


def pool_mixer(u, w_pool, s_pool):
    b, s, _ = u.shape
    uf = u.astype(jnp.float32)
    cs = jnp.pad(jnp.cumsum(uf, axis=1), ((0, 0), (1, 0), (0, 0)))
    t = jnp.arange(s)
    outs = []
    for gi, w in enumerate(POOL_WINDOWS):
        lo = jnp.clip(t - w // 2, 0, s)
        hi = jnp.clip(t + w // 2, 0, s)
        sl = slice(gi * POOL_GROUP, (gi + 1) * POOL_GROUP)
        csg = cs[..., sl]
        mean = (csg[:, hi] - csg[:, lo]) / (hi - lo).astype(jnp.float32)[:, None]
        outs.append(mean - uf[..., sl])
    d = jnp.stack(outs, axis=2).astype(u.dtype)
    y = jnp.einsum('bsgc,gcd->bsgd', d, w_pool).reshape(b, s, C_WIDTH)
    return y * s_pool


def conv_mixer(pb, pc, px, w_conv):
    u = pc * px
    up = jnp.pad(u, ((0, 0), (1, 1), (0, 0)))
    y = up[:, :-2] * w_conv[0] + up[:, 1:-1] * w_conv[1] + up[:, 2:] * w_conv[2]
    return pb * y


def merge_branches(ys, gate_logits, w_branch, w_o):
    b, s, _ = gate_logits.shape
    y = jnp.stack(ys, axis=2)
    proj = jnp.einsum('bsnc,ncd->bsnd', y, w_branch)
    g = jax.nn.sigmoid(gate_logits.reshape(b, s, N_BRANCH, D_MODEL))
    return jnp.einsum('bsnd,bsnd->bsd', g, proj) @ w_o


def squared_relu_mlp(h, w1, w2):
    a = jax.nn.relu(h @ w1)
    return (a * a) @ w2


def setup_inputs(seed: int = 0) -> dict:
    key = jax.random.key(seed)
    ks = jax.random.split(key, 26)
    f32 = jnp.float32

    def nrm(k, shape, scale):
        return jax.random.normal(k, shape, f32) * scale

    def gain(k, shape, noise=0.05):
        return 1.0 + nrm(k, shape, noise)

    return {
        'x': nrm(ks[0], (BATCH, SEQ, D_MODEL), 1.0),
        'c': nrm(ks[1], (BATCH, D_MODEL), 1.0),
        'ctx': nrm(ks[2], (BATCH, CTX_LEN, D_MODEL), 1.0),
        'c_ctx': nrm(ks[3], (D_MODEL,), 1.0),
        'w_mod': nrm(ks[4], (DEPTH, D_MODEL, 6 * D_MODEL), D_MODEL ** -0.5),
        'b_mod': nrm(ks[5], (DEPTH, 6 * D_MODEL), 0.02),
        'g_norm1': gain(ks[6], (DEPTH, D_MODEL)),
        'g_norm2': gain(ks[7], (DEPTH, D_MODEL)),
        'w_in': nrm(ks[8], (DEPTH, D_MODEL, D_IN), D_MODEL ** -0.5),
        'gq_a': gain(ks[9], (DEPTH, A_HD)),
        'gk_a': gain(ks[10], (DEPTH, A_HD)),
        'lam_a': nrm(ks[11], (DEPTH, 4, A_HD), 0.1),
        'g_sub_a': gain(ks[12], (DEPTH, A_VD)),
        'g_cq': gain(ks[13], (DEPTH, B_QLORA)),
        'w_uq': nrm(ks[14], (DEPTH, B_QLORA, B_HEADS * B_QK), B_QLORA ** -0.5),
        'g_ckv': gain(ks[15], (DEPTH, B_KVLORA)),
        'w_ukv': nrm(ks[16], (DEPTH, B_KVLORA, B_HEADS * (B_NOPE + B_VD)), B_KVLORA ** -0.5),
        'gq_b': gain(ks[17], (DEPTH, B_QK)),
        'gk_b': gain(ks[18], (DEPTH, B_QK)),
        'w_pool': nrm(ks[19], (DEPTH, len(POOL_WINDOWS), POOL_GROUP, POOL_GROUP), POOL_GROUP ** -0.5),
        's_pool': gain(ks[20], (DEPTH, C_WIDTH), 0.1),
        'w_conv': nrm(ks[21], (DEPTH, 3, D_WIDTH), 3 ** -0.5),
        'w_branch': nrm(ks[22], (DEPTH, N_BRANCH, BRANCH_W, D_MODEL), BRANCH_W ** -0.5),
        'w_o': nrm(ks[23], (DEPTH, D_MODEL, D_MODEL), D_MODEL ** -0.5),
        'w_ff1': nrm(ks[24], (DEPTH, D_MODEL, D_FF), D_MODEL ** -0.5),
        'w_ff2': nrm(ks[25], (DEPTH, D_FF, D_MODEL), D_FF ** -0.5),
    }


def reference(x, c, ctx, c_ctx, w_mod, b_mod, g_norm1, g_norm2, w_in, gq_a, gk_a, lam_a, g_sub_a,
              g_cq, w_uq, g_ckv, w_ukv, gq_b, gk_b, w_pool, s_pool, w_conv, w_branch, w_o,
              w_ff1, w_ff2):
    n_lat = x.shape[1]
    rope_a = rope_tables(n_lat, A_HD)
    rope_b = rope_tables(n_lat, B_ROPE)
    coef_b = jnp.ones((1,), jnp.float32)
    xc = ctx
    for l in range(DEPTH):
        last = l == DEPTH - 1
        lam_init = 0.8 - 0.6 * math.exp(-0.3 * l)
        mod_x = jnp.split((jax.nn.silu(c) @ w_mod[l] + b_mod[l])[:, None, :], 6, axis=-1)
        mod_c = jnp.split(jax.nn.silu(c_ctx) @ w_mod[l] + b_mod[l], 6, axis=-1)

        px = split_in(modulate(rmsnorm(x, g_norm1[l]), mod_x[0], mod_x[1]) @ w_in[l])
        pc = split_in(modulate(rmsnorm(xc, g_norm1[l]), mod_c[0], mod_c[1]) @ w_in[l])

        la = lam_a[l].astype(jnp.float32)
        lam = jnp.exp(jnp.sum(la[0] * la[1])) - jnp.exp(jnp.sum(la[2] * la[3])) + lam_init
        coef_a = jnp.stack([jnp.ones((), jnp.float32), -lam])
        qa_x, ka_x, va_x = diff_qkv(px[0], px[1], px[2], gq_a[l], gk_a[l], rope_a)
        qa_c, ka_c, va_c = diff_qkv(pc[0], pc[1], pc[2], gq_a[l], gk_a[l], None)
        ka_all = jnp.concatenate([ka_c, ka_x], axis=3)
        va_all = jnp.concatenate([va_c, va_x], axis=2)
        ya_x = diff_post(attend(qa_x, ka_all, va_all, coef_a), g_sub_a[l], lam_init)

        qb_x, kb_x, vb_x = mla_qkv(px[3], px[4], px[5], g_cq[l], w_uq[l], g_ckv[l], w_ukv[l],
                                   gq_b[l], gk_b[l], rope_b)
        qb_c, kb_c, vb_c = mla_qkv(pc[3], pc[4], pc[5], g_cq[l], w_uq[l], g_ckv[l], w_ukv[l],
                                   gq_b[l], gk_b[l], None)
        kb_all = jnp.concatenate([kb_c, kb_x], axis=2)
        vb_all = jnp.concatenate([vb_c, vb_x], axis=2)
        yb_x = from_heads(attend(qb_x[None], kb_all[None], vb_all, coef_b))

        mix_x = merge_branches([ya_x, yb_x,
                                pool_mixer(px[6], w_pool[l], s_pool[l]),
                                conv_mixer(px[7], px[8], px[9], w_conv[l])],
                               px[10], w_branch[l], w_o[l])
        x = x + mod_x[2] * mix_x
        x = x + mod_x[5] * squared_relu_mlp(modulate(rmsnorm(x, g_norm2[l]), mod_x[3], mod_x[4]),
                                            w_ff1[l], w_ff2[l])

        if not last:
            ya_c = diff_post(attend(qa_c, ka_c, va_c, coef_a), g_sub_a[l], lam_init)
            yb_c = from_heads(attend(qb_c[None], kb_c[None], vb_c, coef_b))
            mix_c = merge_branches([ya_c, yb_c,
                                    pool_mixer(pc[6], w_pool[l], s_pool[l]),
                                    conv_mixer(pc[7], pc[8], pc[9], w_conv[l])],
                                   pc[10], w_branch[l], w_o[l])
            xc = xc + mod_c[2] * mix_c
            xc = xc + mod_c[5] * squared_relu_mlp(modulate(rmsnorm(xc, g_norm2[l]), mod_c[3], mod_c[4]),
                                                  w_ff1[l], w_ff2[l])
    return x
```

```python
import math
import os
import numpy as np
import concourse.bass as bass
import concourse.mybir as mybir
from concourse.bass_utils import run_bass_kernel_spmd

F32 = mybir.dt.float32
BF16 = mybir.dt.bfloat16
AF = mybir.ActivationFunctionType
ALU = mybir.AluOpType
AX = mybir.AxisListType
DSZ = {F32: 4, BF16: 2}

D = 1024
S_LAT = 2048
S_CTX = 256
NT = S_LAT + S_CTX
NTT = NT // 128
D_IN = 8352
EPS = 1e-6
OFF_Q, OFF_K, OFF_V, OFF_CQ, OFF_CKV, OFF_KR, OFF_U, OFF_PB, OFF_PC, OFF_PX, OFF_G = (
    0, 512, 1024, 1536, 1920, 2176, 2208, 2720, 3232, 3744, 4256)
TB_ALL = [(0, 256), (256, 512), (768, 512), (1280, 512), (1792, 512)]
PADL = 2368


def pcol(t):
    return t + 16 if t < 256 else t + 48


class Sched:
    ENG = ("tensor", "vector", "scalar", "gpsimd", "sync")

    def __init__(self, nc, ndma=12):
        self.nc = nc
        self.ops = {e: [] for e in self.ENG}
        self.track = {}
        self.ndma = ndma
        self.dma_n = {"sync": 0, "gpsimd": 0}
        self.dma_cum = {}

    @staticmethod
    def region(ap):
        if isinstance(ap, tuple):
            return ap
        dims = ap.ap
        dsz = DSZ[ap.dtype]
        row = dims[0][0]
        off = ap.offset
        sp = str(ap.space)
        if "SB" in sp or "PSUM" in sp:
            p0 = off // row
            f0 = off % row
            ext = 0
            for s, c in dims[1:]:
                ext += (c - 1) * abs(s)
            return (ap.tensor.name, p0, p0 + dims[0][1], f0 * dsz, (f0 + ext + 1) * dsz)
        ext = 0
        for s, c in dims:
            ext += (c - 1) * abs(s)
        return (ap.tensor.name, 0, 1, off * dsz, (off + ext + 1) * dsz)

    def _conf(self, r, write, deps):
        for e in self.track.get(r[0], ()):
            if (write or e[5]) and e[0] < r[2] and r[1] < e[1] and e[2] < r[4] and r[3] < e[3]:
                deps.add(e[4])

    def _record(self, r, prod, write):
        lst = self.track.setdefault(r[0], [])
        if write:
            lst[:] = [e for e in lst if not (r[1] <= e[0] and e[1] <= r[2] and r[3] <= e[2] and e[3] <= r[4])]
        else:
            lst[:] = [e for e in lst if not ((not e[5]) and e[4][0] == prod[0] and e[0] == r[1] and e[1] == r[2]
                                             and e[2] == r[3] and e[3] == r[4])]
        lst.append((r[1], r[2], r[3], r[4], prod, write))

    def add(self, eng, fn, reads, writes, dma=False):
        reads = [self.region(a) for a in reads]
        writes = [self.region(a) for a in writes]
        deps = set()
        for r in reads:
            self._conf(r, False, deps)
        for w in writes:
            self._conf(w, True, deps)
        if dma:
            k = self.dma_n[eng] % self.ndma
            self.dma_n[eng] += 1
            name = "d_%s_%d" % (eng, k)
            self.dma_cum[name] = self.dma_cum.get(name, 0) + 1
            prod = (name, self.dma_cum[name])
            if prod[1] > 1:
                deps.add((name, prod[1] - 1))
        else:
            prod = (eng, len(self.ops[eng]) + 1)
        self.ops[eng].append([fn, deps, prod, dma, False])
        for r in reads:
            self._record(r, prod, False)
        for w in writes:
            self._record(w, prod, True)
        return prod

    def emit(self):
        nc = self.nc
        needed = set()
        for e in self.ENG:
            for op in self.ops[e]:
                for d in op[1]:
                    if not (d[0] == "tensor" and e == "tensor"):
                        needed.add(d)
        ms = {}
        for e in self.ENG:
            cnt = 0
            for i, op in enumerate(self.ops[e]):
                if (not op[3]) and (e, i + 1) in needed:
                    cnt += 1
                    op[4] = True
                    ms[(e, i + 1)] = cnt
        names = list(self.ENG) + sorted(self.dma_cum.keys())
        import contextlib
        with contextlib.ExitStack() as st:
            sems = {n: st.enter_context(nc.semaphore("s_" + n)) for n in names}
            block = st.enter_context(nc.Block())

            def run(e, eng):
                waited = {}
                for fn, deps, prod, dma, inc in self.ops[e]:
                    need = {}
                    for (p, s) in deps:
                        if p == "tensor" and e == "tensor":
                            continue
                        v = ms[(p, s)] if p in self.ENG else 16 * s
                        if v > need.get(p, 0):
                            need[p] = v
                    for p, v in need.items():
                        if waited.get(p, 0) < v:
                            eng.wait_ge(sems[p], v)
                            waited[p] = v
                    ins = fn(eng)
                    if dma:
                        ins.then_inc(sems[prod[0]], 16)
                    elif inc:
                        ins.then_inc(sems[e], 1)

            @block.tensor
            def _(eng):
                run("tensor", eng)

            @block.vector
            def _(eng):
                run("vector", eng)

            @block.scalar
            def _(eng):
                run("scalar", eng)

            @block.gpsimd
            def _(eng):
                run("gpsimd", eng)

            @block.sync
            def _(eng):
                run("sync", eng)


def build(depth, dbg=False):
    nc = bass.Bass("TRN2", target_bir_lowering=False)
    L = depth

    def din(name, shape):
        return nc.dram_tensor(name, list(shape), F32, kind="ExternalInput").ap()

    x_d = din("x", [S_LAT, D])
    ctx_d = din("ctx", [S_CTX, D])
    rows_d = din("rows", [3, 128, 128])
    w_mod = din("w_mod", [L, D, 6 * D])
    w_in = din("w_in", [L, D, D_IN])
    gains_d = din("gains", [L, 1344])
    w_uq = din("w_uq", [L, 384, 768])
    w_ukv = din("w_ukv", [L, 256, 1024])
    w_pool = din("w_pool", [L, 4, 128, 128])
    w_branch = din("w_branch", [L, 4, 512, D])
    w_o = din("w_o", [L, D, D])
    w_ff1 = din("w_ff1", [L, D, 4 * D])
    w_ff2 = din("w_ff2", [L, 4 * D, D])
    identf_d = din("identf", [128, 128])
    ropeA_d = din("ropeA", [128, 2 * 16 * 64])
    ropeB_d = din("ropeB", [128, 2 * 16 * 32])
    poolfix_d = din("poolfix", [128, 128])
    y_d = nc.dram_tensor("y", [S_LAT, D], F32, kind="ExternalOutput").ap()
    xs_d = nc.dram_tensor("xs", [8, 128, NT], F32, kind="Internal").ap()

    S = Sched(nc)
    import contextlib
    es = contextlib.ExitStack()

    def sb(name, shape, dt):
        return es.enter_context(nc.sbuf_tensor(name, list(shape), dt))

    hT = sb("hT", [128, 8 * NT], BF16)
    U = sb("U", [128, 47104], BF16)
    ring = [sb("w%d" % i, [128, 4096], BF16) for i in range(3)]
    xblk = [sb("xb%d" % i, [128, 1024], F32) for i in range(2)]
    PT = [sb("pt%d" % i, [128, 512], BF16) for i in range(3)]
    tmpf = [sb("tf%d" % i, [128, 512], F32) for i in range(4)]
    tmpb = [sb("tb%d" % i, [128, 512], BF16) for i in range(3)]
    rstd_bc = sb("rstd_bc", [128, 512], F32)
    oA = sb("oA", [128, 512], F32)
    small = sb("small", [128, 64], F32)
    small2 = sb("small2", [128, 64], F32)
    modraw = [sb("modraw0", [128, 96], F32), sb("modraw1", [128, 96], F32)]
    smalls = [small, small2]
    cols = sb("cols", [128, 384], F32)
    identf = sb("identf_s", [128, 128], F32)
    identb = sb("identb", [128, 128], BF16)
    onesb = sb("onesb", [128, 128], BF16)
    ropeA = sb("ropeA_s", [128, 2 * 16 * 64], F32)
    ropeB = sb("ropeB_s", [128, 2 * 16 * 32], F32)
    poolfix = sb("poolfix_s", [128, 128], F32)
    gains = sb("gains_s", [128, 1344], F32)
    modS = sb("modS", [128, 96], F32)
    gsS = sb("gsS", [128, 32], F32)
    sc = sb("sc", [128, 16], BF16)
    lamt = sb("lamt", [128, 8], F32)
    gsub = sb("gsub", [128, 128], F32)
    ybtok = sb("ybtok", [128, NTT * 128], BF16)
    PS = [es.enter_context(nc.psum_tensor("ps%d" % i, [128, 512], F32)) for i in range(8)]

    hT3 = hT[:].rearrange("p (k t) -> p k t", k=8)
    xT3 = U[:, 0:36864].bitcast(F32).rearrange("p (k t) -> p k t", k=8)
    aT3 = U[:, 36864:46080].rearrange("p (k t) -> p k t", k=4)
    yT3 = U[:, 0:9216].rearrange("p (k t) -> p k t", k=4)
    mix3 = U[:, 9216:27648].rearrange("p (k t) -> p k t", k=8)
    QO = 27648

    def Q(a, b):
        return U[:, QO + a:QO + b]

    ring_i = [0]

    def slot():
        r = ring[ring_i[0] % 3]
        ring_i[0] += 1
        return r

    def wload(dst, src):
        S.add("gpsimd", lambda e, d=dst, s=src: e.dma_start(out=d, in_=s), [src], [dst], dma=True)

    def dma(dst, src, rd=None, wr=None):
        S.add("sync", lambda e, d=dst, s=src: e.dma_start(out=d, in_=s),
              [src] if rd is None else rd, [dst] if wr is None else wr, dma=True)

    def bankreg(out):
        r = S.region(out)
        return (r[0], r[1], r[2], 0, 2048)

    def mm(out, lhsT, rhs, start=True, stop=True, skip=False):
        S.add("tensor", lambda e: e.matmul(out, lhsT, rhs, start=start, stop=stop, skip_group_check=skip),
              [lhsT, rhs], [bankreg(out)])

    def tr(out, in_, ident):
        S.add("tensor", lambda e: e.transpose(out, in_, ident), [in_, ident], [bankreg(out)])

    def act(out, in_, func, bias=None, scale=None):
        rd = [in_]
        kw = {}
        if bias is not None:
            kw["bias"] = bias
            if not isinstance(bias, (int, float)):
                rd.append(bias)
        if scale is not None:
            kw["scale"] = scale
            if not isinstance(scale, (int, float)):
                rd.append(scale)
        S.add("scalar", lambda e: e.activation(out=out, in_=in_, func=func, **kw), rd, [out])

    def tt(out, in0, in1, op, eng="vector"):
        S.add(eng, lambda e: e.tensor_tensor(out=out, in0=in0, in1=in1, op=op), [in0, in1], [out])

    def ts(out, in0, s1, op0, s2=None, op1=None, eng="vector"):
        rd = [in0] + [s for s in (s1, s2) if s is not None and not isinstance(s, (int, float))]
        if op1 is None:
            S.add(eng, lambda e: e.tensor_scalar(out=out, in0=in0, scalar1=s1, scalar2=None, op0=op0), rd, [out])
        else:
            S.add(eng, lambda e: e.tensor_scalar(out=out, in0=in0, scalar1=s1, scalar2=s2, op0=op0, op1=op1), rd, [out])

    def stt(out, in0, scalar, in1, op0, op1):
        rd = [in0, in1] + ([] if isinstance(scalar, (int, float)) else [scalar])
        S.add("vector", lambda e: e.scalar_tensor_tensor(out=out, in0=in0, scalar=scalar, in1=in1, op0=op0, op1=op1),
              rd, [out])

    def cp(out, in_, eng="vector"):
        if eng == "scalar":
            S.add("scalar", lambda e: e.copy(out=out, in_=in_), [in_], [out])
        else:
            S.add(eng, lambda e: e.tensor_copy(out=out, in_=in_), [in_], [out])

    def red(out, in_):
        S.add("vector", lambda e: e.tensor_reduce(out=out, in_=in_, axis=AX.X, op=ALU.add), [in_], [out])

    def recip(out, in_):
        S.add("vector", lambda e: e.reciprocal(out=out, in_=in_), [in_], [out])

    def mset(ap, v, eng="vector"):
        S.add(eng, lambda e: e.memset(ap, v), [], [ap])

    def interleave(gens):
        gens = list(gens)
        while gens:
            for g in list(gens):
                try:
                    next(g)
                except StopIteration:
                    gens.remove(g)

    bank_i = [0]
    bank_list = [list(range(8))]

    def bank():
        b = bank_list[0][bank_i[0] % len(bank_list[0])]
        bank_i[0] += 1
        return PS[b]

    rot = {}

    def rr(lst, key):
        i = rot.get(key, 0)
        rot[key] = i + 1
        return lst[i % len(lst)]

    dma(identf[:], identf_d)
    dma(ropeA[:], ropeA_d)
    dma(ropeB[:], ropeB_d)
    dma(poolfix[:], poolfix_d)
    cp(identb[:], identf[:])
    mset(onesb[:], 1.0)
    for r in range(3):
        xb = xblk[r % 2]
        dma(xb[:, 0:128], rows_d[r])
        pb = bank()
        tr(pb[:, 0:128], xb[:, 0:128], identf[:])
        cp(cols[:, r * 128:(r + 1) * 128], pb[:, 0:128])
    sc3 = sc[:].rearrange("p (k s) -> p k s", s=2)
    act(sc3[:, :, 0], cols[:, 320:328], AF.Silu)
    act(sc3[:, :, 1], cols[:, 328:336], AF.Silu)

    for tt_ in range(NTT):
        xb = rr(xblk, "xin")
        src = ctx_d[tt_ * 128:(tt_ + 1) * 128, :] if tt_ < 2 else x_d[(tt_ - 2) * 128:(tt_ - 1) * 128, :]
        dma(xb[:], src)
        for half in range(2):
            pb = bank()
            for j in range(4):
                k = half * 4 + j
                tr(pb[:, j * 128:(j + 1) * 128], xb[:, k * 128:(k + 1) * 128], identf[:])
            cp(xT3[:, half * 4:half * 4 + 4, tt_ * 128:(tt_ + 1) * 128],
               pb[:].rearrange("p (k t) -> p k t", k=4), eng=("vector" if half == 0 else "scalar"))

    def w_cols(W2d, c0, n, dst, kch=8):
        wload(dst, W2d[:, c0:c0 + n].rearrange("(k p) c -> p k c", p=128))

    def norm_and_h(l, ni, tbs):
        for (t0, n) in tbs:
            st = 1 if t0 == 0 else 0
            pb = bank()
            for k in range(8):
                sq = rr(tmpb, "sq")
                act(sq[:, 0:n], xT3[:, k, t0:t0 + n], AF.Square)
                mm(pb[:, 0:n], onesb[:], sq[:, 0:n], start=(k == 0), stop=(k == 7))
            t = rr(tmpf, "tf")
            act(t[:, 0:n], pb[:, 0:n], AF.Sqrt, bias=EPS, scale=1.0 / D)
            recip(rstd_bc[:, 0:n], t[:, 0:n])
            for k in range(8):
                t = rr(tmpf, "tf")
                tt(t[:, 0:n], xT3[:, k, t0:t0 + n], rstd_bc[:, 0:n], ALU.mult)
                gi = ni * 16 + k * 2 + st
                mi = ((0 if ni == 0 else 24) + k) * 2 + st
                act(hT3[:, k, t0:t0 + n], t[:, 0:n], AF.Identity, bias=modS[:, mi:mi + 1], scale=gsS[:, gi:gi + 1])

    def fpass(wslot3, ncols_chunks, rhs3, kch, tbs, evac):
        for j in ncols_chunks:
            for (t0, n) in tbs:
                pb = bank()
                for k in range(kch):
                    mm(pb[:, 0:n], wslot3[:, k, j * 128:(j + 1) * 128], rhs3[:, k, t0:t0 + n],
                       start=(k == 0), stop=(k == kch - 1))
                evac(j, t0, n, pb)

    for l in range(L):
        last = (l == L - 1)
        lam_init = 0.8 - 0.6 * math.exp(-0.3 * l)
        TBX = TB_ALL[1:] if last else TB_ALL
        W_in = w_in[l]

        dma(gains[:], gains_d[l:l + 1, :].to_broadcast([128, 1344]))
        G_QA, G_KA, G_SUB, G_CQ, G_CKV, G_QB, G_KB, G_LAM = 0, 64, 128, 256, 640, 896, 992, 1088
        t = rr(tmpf, "tf")
        tt(t[:, 0:64], gains[:, G_LAM:G_LAM + 64], gains[:, G_LAM + 64:G_LAM + 128], ALU.mult)
        tt(t[:, 64:128], gains[:, G_LAM + 128:G_LAM + 192], gains[:, G_LAM + 192:G_LAM + 256], ALU.mult)
        red(lamt[:, 0:2], t[:, 0:128].rearrange("p (a b) -> p a b", a=2))
        act(lamt[:, 2:4], lamt[:, 0:2], AF.Exp)
        tt(lamt[:, 5:6], lamt[:, 3:4], lamt[:, 2:3], ALU.subtract)
        ts(lamt[:, 4:5], lamt[:, 5:6], -lam_init, ALU.add)
        ts(gsub[:], gains[:, G_SUB:G_SUB + 128], 1.0 - lam_init, ALU.mult)

        def mod_load(l2, s_):
            wsl = slot()
            w3 = wsl[:].rearrange("p (k c) -> p k c", k=8)
            w_cols(w_mod[l2], s_ * 512, 512, w3)
            return w3

        def mod_mm(l2, s_, w3):
            pbm = bank()
            for m in range(4):
                for k in range(8):
                    mm(pbm[:, m * 2:m * 2 + 2], w3[:, k, m * 128:(m + 1) * 128], sc3[:, k, :],
                       start=(k == 0), stop=(k == 7))
            cp(modraw[l2 % 2][:, s_ * 8:(s_ + 1) * 8], pbm[:, 0:8])
        if l == 0:
            for s_ in range(12):
                mod_mm(0, s_, mod_load(0, s_))
        mps = modraw[l % 2]
        mod3 = modS[:].rearrange("p (m s) -> p m s", s=2)
        tt(mod3, mps[:, 0:96].rearrange("p (m s) -> p m s", s=2),
           cols[:, l * 48:(l + 1) * 48].unsqueeze(2).to_broadcast([128, 48, 2]), ALU.add)
        gs4 = gsS[:].rearrange("p (n k s) -> p n k s", n=2, s=2)
        for ni in range(2):
            so = 8 if ni == 0 else 32
            gcol = (192 if ni == 0 else 224) + l * 8
            ts(gs4[:, ni], mod3[:, so:so + 8, :], 1.0, ALU.add)
            tt(gs4[:, ni], gs4[:, ni], cols[:, gcol:gcol + 8].unsqueeze(2).to_broadcast([128, 8, 2]), ALU.mult)

        norm_and_h(l, 0, TB_ALL)
        for k in range(8):
            dma(xs_d[k], xT3[:, k, :], wr=[("xs", k, k + 1, 0, NT * 4)])

        bank_list[0] = [0, 1]
        sc_banks = [PS[2], PS[3]]
        acc_sets = [(PS[4], PS[5]), (PS[6], PS[7])]

        units = []

        def attention(*u):
            units.append(u)

        def run_units():
            steps = [(ui, ki) for ui, u in enumerate(units) for ki in range(len(u[7]))]

            def score(i):
                ui, ki = steps[i]
                qT, kT, vfn, dv, scale, q0, nq, kts, fin = units[ui]
                sb_ = rr(sc_banks, "sc")
                kt = kts[ki]
                mm(sb_[:, 0:nq], kT[:, kt * 128:(kt + 1) * 128], qT[:, q0:q0 + nq])
                return sb_
            accs = None
            import os
            PRE = os.environ.get("NOPRE") is None
            cur = score(0)
            for i, (ui, ki) in enumerate(steps):
                qT, kT, vfn, dv, scale, q0, nq, kts, fin = units[ui]
                if PRE:
                    nxt = score(i + 1) if i + 1 < len(steps) else None
                nqt = nq // 128
                per_bank = 2 if dv == 128 else 4
                if ki == 0:
                    accs = rr(acc_sets, "acc")
                kt = kts[ki]
                p = rr(PT, "pt")
                act(p[:, 0:nq], cur[:, 0:nq], AF.Exp, scale=scale)
                for qt in range(nqt):
                    bk = accs[qt // per_bank]
                    o = (qt % per_bank) * (dv + 1)
                    first = (ki == 0 and qt % per_bank == 0)
                    mm(bk[:, o:o + dv + 1], p[:, qt * 128:(qt + 1) * 128], vfn(kt),
                       start=first, stop=(ki == len(kts) - 1), skip=True)
                if ki == len(kts) - 1:
                    fin(accs, nqt, per_bank)
                if not PRE:
                    nxt = score(i + 1) if i + 1 < len(steps) else None
                cur = nxt
            del units[:]

        KA3 = Q(0, 4608).rearrange("p (c t) -> p c t", c=2)
        QP4 = Q(4608, 13824).rearrange("p (h m t) -> p h m t", h=2, m=2)
        VA4 = Q(13824, 13824 + NTT * 258).rearrange("p (t h c) -> p t h c", t=NTT, h=2)
        rA = ropeA[:].rearrange("p (s t c) -> p s t c", s=2, t=16)
        qk_unit_done = [0]
        for hp in range(2):
            s1 = slot()
            s2 = slot()
            w1 = s1[:].rearrange("p (k c) -> p k c", k=8)
            w2 = s2[:, 0:2048].rearrange("p (k c) -> p k c", k=8)
            w_cols(W_in, OFF_Q + hp * 256, 256, w1[:, :, 0:256])
            w_cols(W_in, OFF_K + hp * 256, 256, w1[:, :, 256:512])
            w_cols(W_in, OFF_V + hp * 256, 256, w2)
            for hh_ in range(2):
                mset(VA4[:, :, hh_, 128:129], 1.0)
                mset(QP4[64:128, hh_, 0, :], 0.0)
                mset(QP4[0:64, hh_, 1, :], 0.0)
            bank_list[0] = list(range(8))

            def a_mm(tt_):
                tk = slice(tt_ * 128, (tt_ + 1) * 128)
                p1 = bank()
                for k in range(8):
                    mm(p1[:], hT3[:, k, tk], w1[:, k, :], start=(k == 0), stop=(k == 7))
                p2 = bank()
                for k in range(8):
                    mm(p2[:, 0:256], hT3[:, k, tk], w2[:, k, :], start=(k == 0), stop=(k == 7))
                return p1, p2
            nxt_ = a_mm(0)
            for tt_ in range(NTT):
                tk = slice(tt_ * 128, (tt_ + 1) * 128)
                p1, p2 = nxt_
                if os.environ.get('NOPIPE_A') is None:
                    nxt_ = a_mm(tt_ + 1) if tt_ + 1 < NTT else None
                sq = rr(tmpf, "tf")
                act(sq[:], p1[:], AF.Square)
                red(small[:, 0:8], sq[:].rearrange("p (g c) -> p g c", g=8))
                act(small[:, 8:16], small[:, 0:8], AF.Sqrt, bias=EPS, scale=1.0 / 64)
                recip(small[:, 16:24], small[:, 8:16])
                qn = rr(tmpf, "tf")
                qn3 = qn[:].rearrange("p (g c) -> p g c", g=8)
                tt(qn3, p1[:].rearrange("p (g c) -> p g c", g=8),
                   small[:, 16:24].unsqueeze(2).to_broadcast([128, 8, 64]), ALU.mult)
                qn4 = qn[:].rearrange("p (a g c) -> p a g c", a=2, g=4)
                tt(qn4, qn4, gains[:, G_QA:G_QA + 128].rearrange("p (a c) -> p a c", a=2).unsqueeze(2)
                   .to_broadcast([128, 2, 4, 64]), ALU.mult)
                qkb = rr(tmpb, "qkb")
                if tt_ >= 2:
                    cc = rA[:, 0, tt_ - 2, :].unsqueeze(1).to_broadcast([128, 8, 64])
                    ss = rA[:, 1, tt_ - 2, :].unsqueeze(1).to_broadcast([128, 8, 64])
                    t1 = rr(tmpf, "tf")
                    t2 = rr(tmpf, "tf")
                    t13 = t1[:].rearrange("p (g c) -> p g c", g=8)
                    t23 = t2[:].rearrange("p (g c) -> p g c", g=8)
                    tt(t13, qn3, cc, ALU.mult)
                    tt(t23, qn3, ss, ALU.mult)
                    qkb3 = qkb[:].rearrange("p (g c) -> p g c", g=8)
                    tt(qkb3[:, :, 0:32], t13[:, :, 0:32], t23[:, :, 32:64], ALU.subtract)
                    tt(qkb3[:, :, 32:64], t23[:, :, 0:32], t13[:, :, 32:64], ALU.add)
                else:
                    cp(qkb[:], qn[:])
                cp(VA4[:, tt_, :, 0:128], p2[:, 0:256].rearrange("p (h c) -> p h c", h=2), eng="scalar")
                pbk = bank()
                pbb = pbk[:].bitcast(BF16)
                for j in range(4):
                    tr(pbb[:, j * 128:(j + 1) * 128], qkb[:, j * 128:(j + 1) * 128], identb[:])
                pb4 = pbb[:, 0:512].rearrange("p (c t) -> p c t", c=4)
                cp(KA3[:, :, tk], pb4[:, 2:4, :], eng="scalar")
                cp(QP4[0:64, :, 0, tk], pb4[0:64, 0:2, :], eng="scalar")
                cp(QP4[64:128, :, 1, tk], pb4[64:128, 0:2, :], eng="scalar")
                if os.environ.get('NOPIPE_A') is not None:
                    nxt_ = a_mm(tt_ + 1) if tt_ + 1 < NTT else None
            bank_list[0] = [0, 1]
            for hh in range(2):
                h = hp * 2 + hh
                qblocks = [(256 + i * 512, 512, list(range(NTT))) for i in range(4)]
                if not last:
                    qblocks.append((0, 256, [0, 1]))
                for (q0, nq, kts) in qblocks:
                    def fin0(accs, nqt, per_bank, nq=nq):
                        for b in range((nqt + 1) // 2):
                            a3 = accs[b][:, 0:258].rearrange("p (t c) -> p t c", t=2)
                            recip(small[:, 24 + 2 * b:26 + 2 * b], a3[:, :, 128])
                            tt(oA[:, b * 256:(b + 1) * 256].rearrange("p (t c) -> p t c", t=2), a3[:, :, 0:128],
                               small[:, 24 + 2 * b:26 + 2 * b].unsqueeze(2).to_broadcast([128, 2, 128]), ALU.mult)

                    def fin1(accs, nqt, per_bank, nq=nq, q0=q0, h=h):
                        o = rr(tmpf, "tf")
                        for b in range((nqt + 1) // 2):
                            a3 = accs[b][:, 0:258].rearrange("p (t c) -> p t c", t=2)
                            recip(small[:, 28 + 2 * b:30 + 2 * b], a3[:, :, 128])
                            ts(small[:, 32 + 2 * b:34 + 2 * b], small[:, 28 + 2 * b:30 + 2 * b], lamt[:, 4:5], ALU.mult)
                            o3 = o[:, b * 256:(b + 1) * 256].rearrange("p (t c) -> p t c", t=2)
                            tt(o3, a3[:, :, 0:128],
                               small[:, 32 + 2 * b:34 + 2 * b].unsqueeze(2).to_broadcast([128, 2, 128]), ALU.mult)
                        tt(o[:, 0:nq], o[:, 0:nq], oA[:, 0:nq], ALU.add)
                        sq = rr(tmpf, "tf")
                        act(sq[:, 0:nq], o[:, 0:nq], AF.Square)
                        red(small[:, 36:36 + nqt], sq[:, 0:nq].rearrange("p (t c) -> p t c", c=128))
                        act(small[:, 40:40 + nqt], small[:, 36:36 + nqt], AF.Sqrt, bias=EPS, scale=1.0 / 128)
                        recip(small[:, 44:44 + nqt], small[:, 40:40 + nqt])
                        o3 = o[:, 0:nq].rearrange("p (t c) -> p t c", c=128)
                        tt(o3, o3, small[:, 44:44 + nqt].unsqueeze(2).to_broadcast([128, nqt, 128]), ALU.mult)
                        yb = rr(tmpb, "qkb")
                        tt(yb[:, 0:nq].rearrange("p (t c) -> p t c", c=128), o3,
                           gsub[:].unsqueeze(1).to_broadcast([128, nqt, 128]), ALU.mult)
                        pbk = bank()
                        pbb = pbk[:].bitcast(BF16)
                        for qt in range(nqt):
                            tr(pbb[:, qt * 128:(qt + 1) * 128], yb[:, qt * 128:(qt + 1) * 128], identb[:])
                        cp(yT3[:, h, q0:q0 + nq], pbb[:, 0:nq], eng="scalar")

                    for m in range(2):
                        attention(QP4[:, hh, m, :], KA3[:, hh, :],
                                  lambda kt, hh=hh: VA4[:, kt, hh, :], 128, 0.125, q0, nq, kts,
                                  fin0 if m == 0 else fin1)
            run_units()

        merged = [0]

        def merge(n):
            bank_list[0] = list(range(8))
            g1 = slot()
            g2 = slot()
            wbs = slot()
            gw = [g1[:].rearrange("p (k c) -> p k c", k=8), g2[:].rearrange("p (k c) -> p k c", k=8)]
            w_cols(W_in, OFF_G + n * 1024, 512, gw[0])
            w_cols(W_in, OFF_G + n * 1024 + 512, 512, gw[1])
            wb3 = wbs[:].rearrange("p (k c) -> p k c", k=4)
            wload(wb3, w_branch[l, n].rearrange("(k p) c -> p k c", p=128))
            first = (merged[0] == 0)
            merged[0] += 1
            for j in range(8):
                for (t0, n_) in TBX:
                    pg = bank()
                    for k in range(8):
                        mm(pg[:, 0:n_], gw[j // 4][:, k, (j % 4) * 128:(j % 4 + 1) * 128], hT3[:, k, t0:t0 + n_],
                           start=(k == 0), stop=(k == 7))
                    pp = bank()
                    for k in range(4):
                        mm(pp[:, 0:n_], wb3[:, k, j * 128:(j + 1) * 128], yT3[:, k, t0:t0 + n_],
                           start=(k == 0), stop=(k == 3))
                    sg = rr(tmpf, "tf")
                    act(sg[:, 0:n_], pg[:, 0:n_], AF.Sigmoid)
                    if first:
                        tt(mix3[:, j, t0:t0 + n_], sg[:, 0:n_], pp[:, 0:n_], ALU.mult)
                    else:
                        tt(sg[:, 0:n_], sg[:, 0:n_], pp[:, 0:n_], ALU.mult)
                        tt(mix3[:, j, t0:t0 + n_], sg[:, 0:n_], mix3[:, j, t0:t0 + n_], ALU.add)

        merge(0)

        bank_list[0] = [0, 1, 6, 7]
        acc_sets = [(PS[4],), (PS[5],)]
        CT3 = Q(0, 11520).rearrange("p (c t) -> p c t", c=5)
        QB3 = Q(11520, 16128).rearrange("p (c t) -> p c t", c=2)
        VB3 = Q(16128, 16128 + NTT * 65).rearrange("p (t c) -> p t c", c=65)
        KR3 = Q(17298, 17298 + NTT * 32).rearrange("p (t c) -> p t c", c=32)
        rB = ropeB[:].rearrange("p (s t c) -> p s t c", s=2, t=16)
        s1 = slot()
        s2 = slot()
        w1 = s1[:].rearrange("p (k c) -> p k c", k=8)
        w2 = s2[:, 0:1280].rearrange("p (k c) -> p k c", k=8)
        w_cols(W_in, OFF_CQ, 512, w1)
        w_cols(W_in, OFF_CQ + 512, 160, w2)
        bank_list[0] = list(range(8))

        def b1_mm(tt_):
            tk = slice(tt_ * 128, (tt_ + 1) * 128)
            p1 = bank()
            for k in range(8):
                mm(p1[:], hT3[:, k, tk], w1[:, k, :], start=(k == 0), stop=(k == 7))
            p2 = bank()
            for k in range(8):
                mm(p2[:, 0:160], hT3[:, k, tk], w2[:, k, :], start=(k == 0), stop=(k == 7))
            return p1, p2
        nxt_ = b1_mm(0)
        for tt_ in range(NTT):
            tk = slice(tt_ * 128, (tt_ + 1) * 128)
            p1, p2 = nxt_
            if os.environ.get('NOPIPE_B1') is None:
                nxt_ = b1_mm(tt_ + 1) if tt_ + 1 < NTT else None
            sq = rr(tmpf, "tf")
            sq2 = rr(tmpf, "tf")
            act(sq[:], p1[:], AF.Square)
            act(sq2[:, 0:160], p2[:, 0:160], AF.Square)
            red(small[:, 0:1], sq[:, 0:384])
            red(small[:, 4:5], sq[:, 384:512])
            red(small[:, 5:6], sq2[:, 0:128])
            tt(small[:, 1:2], small[:, 4:5], small[:, 5:6], ALU.add)
            red(small[:, 2:3], sq2[:, 128:160])
            tt(small[:, 8:11], small[:, 0:3], poolfix[:, 32:35], ALU.mult)
            act(small[:, 12:15], small[:, 8:11], AF.Sqrt, bias=EPS)
            recip(small[:, 16:19], small[:, 12:15])
            cb = rr(tmpb, "qkb")
            stt(cb[:, 0:384], p1[:, 0:384], small[:, 16:17], gains[:, G_CQ:G_CQ + 384], ALU.mult, ALU.mult)
            stt(cb[:, 384:512], p1[:, 384:512], small[:, 17:18], gains[:, G_CKV:G_CKV + 128], ALU.mult, ALU.mult)
            cb2 = rr(tmpb, "qkb")
            stt(cb2[:, 0:128], p2[:, 0:128], small[:, 17:18], gains[:, G_CKV + 128:G_CKV + 256], ALU.mult, ALU.mult)
            if tt_ >= 2:
                kr = rr(tmpf, "tf")
                stt(kr[:, 0:32], p2[:, 128:160], small[:, 18:19], gains[:, G_KB + 64:G_KB + 96], ALU.mult, ALU.mult)
                tt(kr[:, 32:64], kr[:, 0:32], rB[:, 0, tt_ - 2, :], ALU.mult)
                tt(kr[:, 64:96], kr[:, 0:32], rB[:, 1, tt_ - 2, :], ALU.mult)
                tt(KR3[:, tt_, 0:16], kr[:, 32:48], kr[:, 80:96], ALU.subtract)
                tt(KR3[:, tt_, 16:32], kr[:, 64:80], kr[:, 48:64], ALU.add)
            else:
                stt(KR3[:, tt_, :], p2[:, 128:160], small[:, 18:19], gains[:, G_KB + 64:G_KB + 96], ALU.mult, ALU.mult)
            pbk = bank()
            pbb = pbk[:].bitcast(BF16)
            for j in range(4):
                tr(pbb[:, j * 128:(j + 1) * 128], cb[:, j * 128:(j + 1) * 128], identb[:])
            tr(pbb[:, 512:640], cb2[:, 0:128], identb[:])
            cp(CT3[:, :, tk], pbb[:, 0:640].rearrange("p (c t) -> p c t", c=5), eng="scalar")
            if os.environ.get('NOPIPE_B1') is not None:
                nxt_ = b1_mm(tt_ + 1) if tt_ + 1 < NTT else None
        bank_list[0] = [0, 1, 6, 7]
        su = slot()
        sk = slot()
        wuq3 = su[:, 0:2304].rearrange("p (k c) -> p k c", k=3)
        wukv3 = sk[:, 0:2048].rearrange("p (k c) -> p k c", k=2)
        wload(wuq3, w_uq[l].rearrange("(k p) c -> p k c", p=128))
        wload(wukv3, w_ukv[l].rearrange("(k p) c -> p k c", p=128))
        yb4 = ybtok[:].rearrange("p (t h c) -> p t h c", t=NTT, h=2)
        for h in range(8):
            mset(VB3[:, :, 64:65], 1.0)
            def b2_mm(tt_, h=h):
                tk = slice(tt_ * 128, (tt_ + 1) * 128)
                pq = bank()
                for k in range(2):
                    mm(pq[:, 0:128], CT3[:, 3 + k, tk], wukv3[:, k, h * 128:(h + 1) * 128], start=(k == 0), stop=(k == 1))
                for k in range(3):
                    mm(pq[:, 128:224], CT3[:, k, tk], wuq3[:, k, h * 96:(h + 1) * 96], start=(k == 0), stop=(k == 2),
                       skip=True)
                return pq
            def b2_chain(tt_, pq, par, h=h):
                tk = slice(tt_ * 128, (tt_ + 1) * 128)
                sm = smalls[par]
                sq = tmpf[par]
                kr = tmpf[2 + par]
                qk = tmpb[par]
                act(sq[:, 0:224], pq[:, 0:224], AF.Square)
                yield
                red(sm[:, 0:7], sq[:, 0:224].rearrange("p (g c) -> p g c", c=32))
                yield
                r8 = sm[:, 0:8].rearrange("p (a b) -> p a b", b=4)
                tt(sm[:, 8:10], r8[:, :, 0], r8[:, :, 1], ALU.add)
                ts(sm[:, 10:11], sm[:, 6:7], 2.0, ALU.mult)
                cp(VB3[:, tt_, 0:64], pq[:, 64:128], eng="scalar")
                yield
                act(sm[:, 12:15], sm[:, 8:11], AF.Sqrt, bias=EPS, scale=1.0 / 64)
                yield
                recip(sm[:, 16:19], sm[:, 12:15])
                yield
                stt(qk[:, 0:64], pq[:, 128:192], sm[:, 17:18], gains[:, G_QB:G_QB + 64], ALU.mult, ALU.mult)
                stt(qk[:, 96:160], pq[:, 0:64], sm[:, 16:17], gains[:, G_KB:G_KB + 64], ALU.mult, ALU.mult)
                if tt_ >= 2:
                    stt(kr[:, 0:32], pq[:, 192:224], sm[:, 18:19], gains[:, G_QB + 64:G_QB + 96], ALU.mult, ALU.mult)
                else:
                    stt(qk[:, 64:96], pq[:, 192:224], sm[:, 18:19], gains[:, G_QB + 64:G_QB + 96], ALU.mult, ALU.mult)
                cp(qk[:, 160:192], KR3[:, tt_, :], eng="gpsimd")
                yield
                if tt_ >= 2:
                    tt(kr[:, 32:64], kr[:, 0:32], rB[:, 0, tt_ - 2, :], ALU.mult, eng="gpsimd")
                    tt(kr[:, 64:96], kr[:, 0:32], rB[:, 1, tt_ - 2, :], ALU.mult, eng="gpsimd")
                    yield
                    tt(qk[:, 64:80], kr[:, 32:48], kr[:, 80:96], ALU.subtract, eng="gpsimd")
                    tt(qk[:, 80:96], kr[:, 64:80], kr[:, 48:64], ALU.add, eng="gpsimd")
                    yield
                pbk = bank()
                pbb = pbk[:].bitcast(BF16)
                tr(pbb[0:96, 0:128], qk[:, 0:96], identb[:])
                tr(pbb[0:96, 128:256], qk[:, 96:192], identb[:])
                yield
                cp(QB3[0:96, :, tk], pbb[0:96, 0:256].rearrange("p (c t) -> p c t", c=2), eng="scalar")

            bank_list[0] = list(range(8))
            pairs = [(2 * i, 2 * i + 1) for i in range(NTT // 2)]
            nxt_ = [b2_mm(pairs[0][0]), b2_mm(pairs[0][1])]
            for pi, (ta, tb_) in enumerate(pairs):
                cur_ = nxt_
                if pi + 1 < len(pairs):
                    nxt_ = [b2_mm(pairs[pi + 1][0]), b2_mm(pairs[pi + 1][1])]
                interleave([b2_chain(ta, cur_[0], 0), b2_chain(tb_, cur_[1], 1)])
            bank_list[0] = [0, 1, 6, 7]
            qblocks = [(256 + i * 512, 512, list(range(NTT))) for i in range(4)]
            if not last:
                qblocks.append((0, 256, [0, 1]))
            for (q0, nq, kts) in qblocks:
                def finb(accs, nqt, per_bank, q0=q0, h=h):
                    a3 = accs[0][:, 0:nqt * 65].rearrange("p (t c) -> p t c", c=65)
                    recip(small[:, 24:24 + nqt], a3[:, :, 64])
                    tt(yb4[:, q0 // 128:q0 // 128 + nqt, h % 2, :], a3[:, :, 0:64],
                       small[:, 24:24 + nqt].unsqueeze(2).to_broadcast([128, nqt, 64]), ALU.mult)
                attention(QB3[0:96, 0, :], QB3[0:96, 1, :], lambda kt: VB3[:, kt, :], 64, 96.0 ** -0.5,
                          q0, nq, kts, finb)
            run_units()
            if h % 2 == 1:
                tiles = list(range(2, NTT)) if last else list(range(NTT))
                for g0 in range(0, len(tiles), 4):
                    grp = tiles[g0:g0 + 4]
                    pbk = bank()
                    pbb = pbk[:].bitcast(BF16)
                    for i, tq in enumerate(grp):
                        tr(pbb[:, i * 128:(i + 1) * 128], ybtok[:, tq * 128:(tq + 1) * 128], identb[:])
                    cp(yT3[:, h // 2, grp[0] * 128:(grp[-1] + 1) * 128], pbb[:, 0:len(grp) * 128], eng="scalar")
        merge(1)

        bank_list[0] = list(range(8))
        UP = [Q(i * 4736, (i + 1) * 4736).bitcast(F32) for i in range(3)]
        DTp = Q(14208, 14208 + PADL)
        su_ = slot()
        wu3 = su_[:].rearrange("p (k c) -> p k c", k=8)
        w_cols(W_in, OFF_U, 512, wu3)
        sp_ = slot()
        wp3 = sp_[:, 0:512].rearrange("p (g d) -> p g d", g=4)
        wload(wp3, w_pool[l].rearrange("g c d -> c g d"))
        for g in range(4):
            w = 2 << g
            A_, B_, C_ = UP
            for (a, b) in ((0, 16), (272, 304), (2352, 2368)):
                mset(A_[:, a:b], 0.0)

            def ev_u(j, t0, n, pb, A_=A_):
                cp(A_[:, pcol(t0):pcol(t0) + n], pb[:, 0:n], eng="scalar")
            fpass(wu3, [g], hT3, 8, TB_ALL, ev_u)
            src, dst = A_, B_
            tt(dst[:, 1:PADL], src[:, 0:PADL - 1], src[:, 1:PADL], ALU.add)
            lo, hi = 1, PADL
            if g >= 1:
                src, dst = B_, C_
                tt(dst[:, 2:PADL - 1], src[:, 1:PADL - 2], src[:, 3:PADL], ALU.add)
            if g >= 2:
                src, dst = C_, B_
                tt(dst[:, 4:PADL - 3], src[:, 2:PADL - 5], src[:, 6:PADL - 1], ALU.add)
            if g >= 3:
                src, dst = B_, C_
                tt(dst[:, 8:PADL - 7], src[:, 4:PADL - 11], src[:, 12:PADL - 3], ALU.add)
            Lg = dst
            ts(Lg[:, 16:2352], Lg[:, 16:2352], 1.0 / w, ALU.mult)
            for (s0, sl) in ((16, 256), (304, 2048)):
                tt(Lg[:, s0:s0 + 8], Lg[:, s0:s0 + 8], poolfix[:, g * 8:g * 8 + 8], ALU.mult)
                tt(Lg[:, s0 + sl - 8:s0 + sl], Lg[:, s0 + sl - 8:s0 + sl], poolfix[:, 40 + g * 8:48 + g * 8], ALU.mult)
            tt(DTp[:, 16:2352], Lg[:, 16:2352], A_[:, 16:2352], ALU.subtract)
            for (t0, n) in TB_ALL:
                pb = bank()
                mm(pb[:, 0:n], wp3[:, g, :], DTp[:, pcol(t0):pcol(t0) + n])
                act(yT3[:, g, t0:t0 + n], pb[:, 0:n], AF.Identity, scale=cols[:, 256 + l * 4 + g:257 + l * 4 + g])
        merge(2)

        PCs, U2, YS = UP
        for g in range(4):
            sd = slot()
            wd4 = sd[:, 0:3072].rearrange("p (k s c) -> p k s c", k=8, s=3)
            for si, off in enumerate((OFF_PC, OFF_PX, OFF_PB)):
                w_cols(W_in, off + g * 128, 128, wd4[:, :, si, :])
            wm_ = mod_load(l + 1, 8 + g) if not last else None
            for (a, b) in ((0, 16), (272, 304), (2352, 2368)):
                mset(U2[:, a:b], 0.0)

            def ev_pc(j, t0, n, pb):
                cp(PCs[:, pcol(t0):pcol(t0) + n], pb[:, 0:n], eng="scalar")

            def ev_px(j, t0, n, pb):
                tt(U2[:, pcol(t0):pcol(t0) + n], pb[:, 0:n], PCs[:, pcol(t0):pcol(t0) + n], ALU.mult)

            def ev_pb(j, t0, n, pb, g=g):
                tt(yT3[:, g, t0:t0 + n], pb[:, 0:n], YS[:, pcol(t0):pcol(t0) + n], ALU.mult)
            fpass(wd4[:, :, 0, :], [0], hT3, 8, TB_ALL, ev_pc)
            fpass(wd4[:, :, 1, :], [0], hT3, 8, TB_ALL, ev_px)
            if wm_ is not None:
                mod_mm(l + 1, 8 + g, wm_)
            wc = 272 + l * 12
            ts(YS[:, 16:2352], U2[:, 16:2352], cols[:, wc + 4 + g:wc + 5 + g], ALU.mult)
            stt(YS[:, 16:2352], U2[:, 15:2351], cols[:, wc + g:wc + g + 1], YS[:, 16:2352], ALU.mult, ALU.add)
            stt(YS[:, 16:2352], U2[:, 17:2353], cols[:, wc + 8 + g:wc + 9 + g], YS[:, 16:2352], ALU.mult, ALU.add)
            fpass(wd4[:, :, 2, :], [0], hT3, 8, TB_ALL, ev_pb)
        merge(3)

        so1 = slot()
        so2 = slot()
        wo = [so1[:].rearrange("p (k c) -> p k c", k=8), so2[:].rearrange("p (k c) -> p k c", k=8)]
        w_cols(w_o[l], 0, 512, wo[0])
        w_cols(w_o[l], 512, 512, wo[1])
        bufs8 = [xblk[0][:, 0:512], xblk[0][:, 512:1024], xblk[1][:, 0:512], xblk[1][:, 512:1024]] + [t_[:] for t_ in tmpf]
        bank_list[0] = list(range(7))
        for (t0, n) in TBX:
            st = 1 if t0 == 0 else 0
            pbn = PS[7]
            for i in range(8):
                dma(bufs8[i][:, 0:n], xs_d[i][:, t0:t0 + n], rd=[("xs", i, i + 1, t0 * 4, (t0 + n) * 4)])
            for i in range(8):
                xb = bufs8[i]
                pb = bank()
                for k in range(8):
                    mm(pb[:, 0:n], wo[i // 4][:, k, (i % 4) * 128:(i % 4 + 1) * 128], mix3[:, k, t0:t0 + n],
                       start=(k == 0), stop=(k == 7))
                gi = (16 + i) * 2 + st
                stt(xb[:, 0:n], pb[:, 0:n], modS[:, gi:gi + 1], xb[:, 0:n], ALU.mult, ALU.add)
                dma(xs_d[i][:, t0:t0 + n], xb[:, 0:n], wr=[("xs", i, i + 1, t0 * 4, (t0 + n) * 4)])
                sq = rr(tmpb, "sq")
                act(sq[:, 0:n], xb[:, 0:n], AF.Square)
                mm(pbn[:, 0:n], onesb[:], sq[:, 0:n], start=(i == 0), stop=(i == 7))
            act(oA[:, 0:n], pbn[:, 0:n], AF.Sqrt, bias=EPS, scale=1.0 / D)
            recip(rstd_bc[:, 0:n], oA[:, 0:n])
            for i in range(8):
                xb = bufs8[i]
                tt(xb[:, 0:n], xb[:, 0:n], rstd_bc[:, 0:n], ALU.mult)
                act(hT3[:, i, t0:t0 + n], xb[:, 0:n], AF.Identity, bias=modS[:, (24 + i) * 2 + st:(24 + i) * 2 + st + 1],
                    scale=gsS[:, 16 + i * 2 + st:16 + i * 2 + st + 1])
        bank_list[0] = list(range(8))
        for k in range(8):
            dma(xT3[:, k, :], xs_d[k], rd=[("xs", k, k + 1, 0, NT * 4)])

        for g in range(8):
            s1 = slot()
            s2 = slot()
            w13 = s1[:].rearrange("p (k c) -> p k c", k=8)
            w23 = s2[:].rearrange("p (k c) -> p k c", k=4)
            w_cols(w_ff1[l], g * 512, 512, w13)
            wload(w23, w_ff2[l][g * 512:(g + 1) * 512, :].rearrange("(k p) c -> p k c", p=128))
            wm_ = mod_load(l + 1, g) if not last else None

            def ev_a(j, t0, n, pb):
                r = rr(tmpf, "tf")
                act(r[:, 0:n], pb[:, 0:n], AF.Relu)
                tt(aT3[:, j, t0:t0 + n], r[:, 0:n], r[:, 0:n], ALU.mult)
            fpass(w13, range(4), hT3, 8, TBX, ev_a)
            if wm_ is not None:
                mod_mm(l + 1, g, wm_)

            def ev_x(i, t0, n, pb):
                gi = (40 + i) * 2 + (1 if t0 == 0 else 0)
                stt(xT3[:, i, t0:t0 + n], pb[:, 0:n], modS[:, gi:gi + 1], xT3[:, i, t0:t0 + n], ALU.mult, ALU.add)
            fpass(w23, range(8), aT3, 4, TBX, ev_x)

    for tt_ in range(2, NTT):
        xb = rr(xblk, "xin")
        for half in range(2):
            pb = bank()
            for j in range(4):
                k = half * 4 + j
                tr(pb[:, j * 128:(j + 1) * 128], xT3[:, k, tt_ * 128:(tt_ + 1) * 128], identf[:])
            cp(xb[:, half * 512:(half + 1) * 512], pb[:], eng=("vector" if half == 0 else "scalar"))
        dma(y_d[(tt_ - 2) * 128:(tt_ - 1) * 128, :], xb[:])
    S.add("sync", lambda e: e.nop(), [y_d], [])
    S.emit()
    es.close()
    return nc


def _consts():
    identf = np.eye(128, dtype=np.float32)

    def rope(rot_dim):
        rows = S_LAT // 64
        row = np.repeat(np.arange(rows, dtype=np.float32), 64)
        col = np.tile(np.arange(64, dtype=np.float32), rows)
        n_freq = rot_dim // 4
        inv = (np.float32(10000.0) ** (-np.arange(n_freq, dtype=np.float32) / np.float32(n_freq))).astype(np.float32)
        ang = np.concatenate([row[:, None] * inv, col[:, None] * inv], axis=-1).astype(np.float32)
        cs, sn = np.cos(ang).astype(np.float32), np.sin(ang).astype(np.float32)
        half = rot_dim // 2
        cc = np.concatenate([cs, cs], -1).reshape(16, 128, 2 * half).transpose(1, 0, 2)
        ss = np.concatenate([sn, sn], -1).reshape(16, 128, 2 * half).transpose(1, 0, 2)
        return np.ascontiguousarray(np.stack([cc, ss], 1).reshape(128, -1)).astype(np.float32)

    fix = np.ones((128,), np.float32)
    for g, w in enumerate((2, 4, 8, 16)):
        for t in range(w // 2):
            fix[g * 8 + t] = w / (t + w // 2)
        for j in range(1, w // 2):
            fix[40 + g * 8 + (8 - j)] = w / (j + w // 2)
    fix[32:35] = (1.0 / 384, 1.0 / 256, 1.0 / 32)
    poolfix = np.tile(fix[None, :], (128, 1)).astype(np.float32)
    return identf, rope(64), rope(32), poolfix


_NC_CACHE = {}


def kernel(x, c, ctx, c_ctx, w_mod, b_mod, g_norm1, g_norm2, w_in, gq_a, gk_a, lam_a, g_sub_a,
           g_cq, w_uq, g_ckv, w_ukv, gq_b, gk_b, w_pool, s_pool, w_conv, w_branch, w_o, w_ff1, w_ff2,
           _cores=8):
    f = lambda a: np.ascontiguousarray(np.asarray(a, dtype=np.float32))
    x, c, ctx, c_ctx = f(x), f(c), f(ctx), f(c_ctx)
    L = int(np.asarray(w_mod).shape[0])
    identf, ropeA, ropeB, poolfix = _consts()
    gains = np.concatenate([f(gq_a), f(gk_a), f(g_sub_a), f(g_cq), f(g_ckv), f(gq_b), f(gk_b),
                            f(lam_a).reshape(L, 256)], axis=1)
    shared = dict(w_mod=f(w_mod), w_in=f(w_in), gains=np.ascontiguousarray(gains), w_uq=f(w_uq), w_ukv=f(w_ukv),
                  w_pool=f(w_pool), w_branch=f(w_branch), w_o=f(w_o), w_ff1=f(w_ff1), w_ff2=f(w_ff2),
                  identf=identf, ropeA=ropeA, ropeB=ropeB, poolfix=poolfix)
    rows = np.zeros((384, 128), np.float32)
    rows[0:L * 48] = f(b_mod).reshape(L * 48, 128)
    rows[192:192 + L * 8] = f(g_norm1).reshape(L * 8, 128)
    rows[224:224 + L * 8] = f(g_norm2).reshape(L * 8, 128)
    rows[256:256 + L * 4] = f(s_pool).reshape(L * 4, 128)
    rows[272:272 + L * 12] = f(w_conv).reshape(L * 12, 128)
    rows[328:336] = c_ctx.reshape(8, 128)
    if L not in _NC_CACHE:
        _NC_CACHE[L] = build(L)
    nc = _NC_CACHE[L]
    in_maps = []
    for b in range(_cores):
        r = rows.copy()
        r[320:328] = c[b].reshape(8, 128)
        m = dict(shared)
        m.update(x=x[b], ctx=ctx[b], rows=np.ascontiguousarray(r.reshape(3, 128, 128)))
        in_maps.append(m)
    res = run_bass_kernel_spmd(nc, in_maps, core_ids=list(range(_cores)))
    return np.stack([np.asarray(res.results[b]["y"], dtype=np.float32) for b in range(_cores)], axis=0)
```

```python
import math
import os
import numpy as np
import concourse.bass as bass
import concourse.mybir as mybir
from concourse.bass_utils import run_bass_kernel_spmd

F32 = mybir.dt.float32
BF16 = mybir.dt.bfloat16
AF = mybir.ActivationFunctionType
ALU = mybir.AluOpType
AX = mybir.AxisListType
DSZ = {F32: 4, BF16: 2}

D = 1024
S_LAT = 2048
S_CTX = 256
NT = S_LAT + S_CTX
NTT = NT // 128
D_IN = 8352
EPS = 1e-6
OFF_Q, OFF_K, OFF_V, OFF_CQ, OFF_CKV, OFF_KR, OFF_U, OFF_PB, OFF_PC, OFF_PX, OFF_G = (
    0, 512, 1024, 1536, 1920, 2176, 2208, 2720, 3232, 3744, 4256)
TB_ALL = [(0, 256), (256, 512), (768, 512), (1280, 512), (1792, 512)]
PADL = 2368


def pcol(t):
    return t + 16 if t < 256 else t + 48


class Sched:
    ENG = ("tensor", "vector", "scalar", "gpsimd", "sync")

    def __init__(self, nc, ndma=12):
        self.nc = nc
        self.ops = {e: [] for e in self.ENG}
        self.track = {}
        self.ndma = ndma
        self.dma_n = {"sync": 0, "gpsimd": 0}
        self.dma_cum = {}

    @staticmethod
    def region(ap):
        if isinstance(ap, tuple):
            return ap
        dims = ap.ap
        dsz = DSZ[ap.dtype]
        row = dims[0][0]
        off = ap.offset
        sp = str(ap.space)
        if "SB" in sp or "PSUM" in sp:
            p0 = off // row
            f0 = off % row
            ext = 0
            for s, c in dims[1:]:
                ext += (c - 1) * abs(s)
            return (ap.tensor.name, p0, p0 + dims[0][1], f0 * dsz, (f0 + ext + 1) * dsz)
        ext = 0
        for s, c in dims:
            ext += (c - 1) * abs(s)
        return (ap.tensor.name, 0, 1, off * dsz, (off + ext + 1) * dsz)

    def _conf(self, r, write, deps):
        for e in self.track.get(r[0], ()):
            if (write or e[5]) and e[0] < r[2] and r[1] < e[1] and e[2] < r[4] and r[3] < e[3]:
                deps.add(e[4])

    def _record(self, r, prod, write):
        lst = self.track.setdefault(r[0], [])
        if write:
            lst[:] = [e for e in lst if not (r[1] <= e[0] and e[1] <= r[2] and r[3] <= e[2] and e[3] <= r[4])]
        else:
            lst[:] = [e for e in lst if not ((not e[5]) and e[4][0] == prod[0] and e[0] == r[1] and e[1] == r[2]
                                             and e[2] == r[3] and e[3] == r[4])]
        lst.append((r[1], r[2], r[3], r[4], prod, write))

    def add(self, eng, fn, reads, writes, dma=False):
        reads = [self.region(a) for a in reads]
        writes = [self.region(a) for a in writes]
        deps = set()
        for r in reads:
            self._conf(r, False, deps)
        for w in writes:
            self._conf(w, True, deps)
        if dma:
            k = self.dma_n[eng] % self.ndma
            self.dma_n[eng] += 1
            name = "d_%s_%d" % (eng, k)
            self.dma_cum[name] = self.dma_cum.get(name, 0) + 1
            prod = (name, self.dma_cum[name])
            if prod[1] > 1:
                deps.add((name, prod[1] - 1))
        else:
            prod = (eng, len(self.ops[eng]) + 1)
        self.ops[eng].append([fn, deps, prod, dma, False])
        for r in reads:
            self._record(r, prod, False)
        for w in writes:
            self._record(w, prod, True)
        return prod

    def emit(self):
        nc = self.nc
        needed = set()
        for e in self.ENG:
            for op in self.ops[e]:
                for d in op[1]:
                    if not (d[0] == "tensor" and e == "tensor"):
                        needed.add(d)
        ms = {}
        for e in self.ENG:
            cnt = 0
            for i, op in enumerate(self.ops[e]):
                if (not op[3]) and (e, i + 1) in needed:
                    cnt += 1
                    op[4] = True
                    ms[(e, i + 1)] = cnt
        names = list(self.ENG) + sorted(self.dma_cum.keys())
        import contextlib
        with contextlib.ExitStack() as st:
            sems = {n: st.enter_context(nc.semaphore("s_" + n)) for n in names}
            block = st.enter_context(nc.Block())

            def run(e, eng):
                waited = {}
                for fn, deps, prod, dma, inc in self.ops[e]:
                    need = {}
                    for (p, s) in deps:
                        if p == "tensor" and e == "tensor":
                            continue
                        v = ms[(p, s)] if p in self.ENG else 16 * s
                        if v > need.get(p, 0):
                            need[p] = v
                    for p, v in need.items():
                        if waited.get(p, 0) < v:
                            eng.wait_ge(sems[p], v)
                            waited[p] = v
                    ins = fn(eng)
                    if dma:
                        ins.then_inc(sems[prod[0]], 16)
                    elif inc:
                        ins.then_inc(sems[e], 1)

            @block.tensor
            def _(eng):
                run("tensor", eng)

            @block.vector
            def _(eng):
                run("vector", eng)

            @block.scalar
            def _(eng):
                run("scalar", eng)

            @block.gpsimd
            def _(eng):
                run("gpsimd", eng)

            @block.sync
            def _(eng):
                run("sync", eng)


def build(depth, dbg=False):
    nc = bass.Bass("TRN2", target_bir_lowering=False)
    L = depth

    def din(name, shape):
        return nc.dram_tensor(name, list(shape), F32, kind="ExternalInput").ap()

    x_d = din("x", [S_LAT, D])
    ctx_d = din("ctx", [S_CTX, D])
    rows_d = din("rows", [3, 128, 128])
    w_mod = din("w_mod", [L, D, 6 * D])
    w_in = din("w_in", [L, D, D_IN])
    gains_d = din("gains", [L, 1344])
    w_uq = din("w_uq", [L, 384, 768])
    w_ukv = din("w_ukv", [L, 256, 1024])
    w_pool = din("w_pool", [L, 4, 128, 128])
    w_branch = din("w_branch", [L, 4, 512, D])
    w_o = din("w_o", [L, D, D])
    w_ff1 = din("w_ff1", [L, D, 4 * D])
    w_ff2 = din("w_ff2", [L, 4 * D, D])
    identf_d = din("identf", [128, 128])
    ropeA_d = din("ropeA", [128, 2 * 16 * 64])
    ropeB_d = din("ropeB", [128, 2 * 16 * 32])
    poolfix_d = din("poolfix", [128, 128])
    y_d = nc.dram_tensor("y", [S_LAT, D], F32, kind="ExternalOutput").ap()
    xs_d = nc.dram_tensor("xs", [8, 128, NT], F32, kind="Internal").ap()

    S = Sched(nc)
    import contextlib
    es = contextlib.ExitStack()

    def sb(name, shape, dt):
        return es.enter_context(nc.sbuf_tensor(name, list(shape), dt))

    hT = sb("hT", [128, 8 * NT], BF16)
    U = sb("U", [128, 47104], BF16)
    ring = [sb("w%d" % i, [128, 4096], BF16) for i in range(3)]
    xblk = [sb("xb%d" % i, [128, 1024], F32) for i in range(2)]
    PT = [sb("pt%d" % i, [128, 512], BF16) for i in range(3)]
    tmpf = [sb("tf%d" % i, [128, 512], F32) for i in range(4)]
    tmpb = [sb("tb%d" % i, [128, 512], BF16) for i in range(3)]
    rstd_bc = sb("rstd_bc", [128, 512], F32)
    oA = sb("oA", [128, 512], F32)
    small = sb("small", [128, 64], F32)
    small2 = sb("small2", [128, 64], F32)
    modraw = [sb("modraw0", [128, 96], F32), sb("modraw1", [128, 96], F32)]
    smalls = [small, small2]
    cols = sb("cols", [128, 384], F32)
    identf = sb("identf_s", [128, 128], F32)
    identb = sb("identb", [128, 128], BF16)
    onesb = sb("onesb", [128, 128], BF16)
    ropeA = sb("ropeA_s", [128, 2 * 16 * 64], F32)
    ropeB = sb("ropeB_s", [128, 2 * 16 * 32], F32)
    poolfix = sb("poolfix_s", [128, 128], F32)
    gains = sb("gains_s", [128, 1344], F32)
    modS = sb("modS", [128, 96], F32)
    gsS = sb("gsS", [128, 32], F32)
    sc = sb("sc", [128, 16], BF16)
    lamt = sb("lamt", [128, 8], F32)
    gsub = sb("gsub", [128, 128], F32)
    ybtok = sb("ybtok", [128, NTT * 128], BF16)
    PS = [es.enter_context(nc.psum_tensor("ps%d" % i, [128, 512], F32)) for i in range(8)]

    hT3 = hT[:].rearrange("p (k t) -> p k t", k=8)
    xT3 = U[:, 0:36864].bitcast(F32).rearrange("p (k t) -> p k t", k=8)
    aT3 = U[:, 36864:46080].rearrange("p (k t) -> p k t", k=4)
    yT3 = U[:, 0:9216].rearrange("p (k t) -> p k t", k=4)
    mix3 = U[:, 9216:27648].rearrange("p (k t) -> p k t", k=8)
    QO = 27648

    def Q(a, b):
        return U[:, QO + a:QO + b]

    ring_i = [0]

    def slot():
        r = ring[ring_i[0] % 3]
        ring_i[0] += 1
        return r

    def wload(dst, src):
        S.add("gpsimd", lambda e, d=dst, s=src: e.dma_start(out=d, in_=s), [src], [dst], dma=True)

    def dma(dst, src, rd=None, wr=None):
        S.add("sync", lambda e, d=dst, s=src: e.dma_start(out=d, in_=s),
              [src] if rd is None else rd, [dst] if wr is None else wr, dma=True)

    def bankreg(out):
        r = S.region(out)
        return (r[0], r[1], r[2], 0, 2048)

    def mm(out, lhsT, rhs, start=True, stop=True, skip=False):
        S.add("tensor", lambda e: e.matmul(out, lhsT, rhs, start=start, stop=stop, skip_group_check=skip),
              [lhsT, rhs], [bankreg(out)])

    def tr(out, in_, ident):
        S.add("tensor", lambda e: e.transpose(out, in_, ident), [in_, ident], [bankreg(out)])

    def act(out, in_, func, bias=None, scale=None):
        rd = [in_]
        kw = {}
        if bias is not None:
            kw["bias"] = bias
            if not isinstance(bias, (int, float)):
                rd.append(bias)
        if scale is not None:
            kw["scale"] = scale
            if not isinstance(scale, (int, float)):
                rd.append(scale)
        S.add("scalar", lambda e: e.activation(out=out, in_=in_, func=func, **kw), rd, [out])

    def tt(out, in0, in1, op, eng="vector"):
        S.add(eng, lambda e: e.tensor_tensor(out=out, in0=in0, in1=in1, op=op), [in0, in1], [out])

    def ts(out, in0, s1, op0, s2=None, op1=None, eng="vector"):
        rd = [in0] + [s for s in (s1, s2) if s is not None and not isinstance(s, (int, float))]
        if op1 is None:
            S.add(eng, lambda e: e.tensor_scalar(out=out, in0=in0, scalar1=s1, scalar2=None, op0=op0), rd, [out])
        else:
            S.add(eng, lambda e: e.tensor_scalar(out=out, in0=in0, scalar1=s1, scalar2=s2, op0=op0, op1=op1), rd, [out])

    def stt(out, in0, scalar, in1, op0, op1):
        rd = [in0, in1] + ([] if isinstance(scalar, (int, float)) else [scalar])
        S.add("vector", lambda e: e.scalar_tensor_tensor(out=out, in0=in0, scalar=scalar, in1=in1, op0=op0, op1=op1),
              rd, [out])

    def cp(out, in_, eng="vector"):
        if eng == "scalar":
            S.add("scalar", lambda e: e.copy(out=out, in_=in_), [in_], [out])
        else:
            S.add(eng, lambda e: e.tensor_copy(out=out, in_=in_), [in_], [out])

    def red(out, in_):
        S.add("vector", lambda e: e.tensor_reduce(out=out, in_=in_, axis=AX.X, op=ALU.add), [in_], [out])

    def recip(out, in_):
        S.add("vector", lambda e: e.reciprocal(out=out, in_=in_), [in_], [out])

    def mset(ap, v, eng="vector"):
        S.add(eng, lambda e: e.memset(ap, v), [], [ap])

    def interleave(gens):
        gens = list(gens)
        while gens:
            for g in list(gens):
                try:
                    next(g)
                except StopIteration:
                    gens.remove(g)

    bank_i = [0]
    bank_list = [list(range(8))]

    def bank():
        b = bank_list[0][bank_i[0] % len(bank_list[0])]
        bank_i[0] += 1
        return PS[b]

    rot = {}

    def rr(lst, key):
        i = rot.get(key, 0)
        rot[key] = i + 1
        return lst[i % len(lst)]

    dma(identf[:], identf_d)
    dma(ropeA[:], ropeA_d)
    dma(ropeB[:], ropeB_d)
    dma(poolfix[:], poolfix_d)
    cp(identb[:], identf[:])
    mset(onesb[:], 1.0)
    for r in range(3):
        xb = xblk[r % 2]
        dma(xb[:, 0:128], rows_d[r])
        pb = bank()
        tr(pb[:, 0:128], xb[:, 0:128], identf[:])
        cp(cols[:, r * 128:(r + 1) * 128], pb[:, 0:128])
    sc3 = sc[:].rearrange("p (k s) -> p k s", s=2)
    act(sc3[:, :, 0], cols[:, 320:328], AF.Silu)
    act(sc3[:, :, 1], cols[:, 328:336], AF.Silu)

    for tt_ in range(NTT):
        xb = rr(xblk, "xin")
        src = ctx_d[tt_ * 128:(tt_ + 1) * 128, :] if tt_ < 2 else x_d[(tt_ - 2) * 128:(tt_ - 1) * 128, :]
        dma(xb[:], src)
        for half in range(2):
            pb = bank()
            for j in range(4):
                k = half * 4 + j
                tr(pb[:, j * 128:(j + 1) * 128], xb[:, k * 128:(k + 1) * 128], identf[:])
            cp(xT3[:, half * 4:half * 4 + 4, tt_ * 128:(tt_ + 1) * 128],
               pb[:].rearrange("p (k t) -> p k t", k=4), eng=("vector" if half == 0 else "scalar"))

    def w_cols(W2d, c0, n, dst, kch=8):
        wload(dst, W2d[:, c0:c0 + n].rearrange("(k p) c -> p k c", p=128))

    def norm_and_h(l, ni, tbs):
        for (t0, n) in tbs:
            st = 1 if t0 == 0 else 0
            pb = bank()
            for k in range(8):
                sq = rr(tmpb, "sq")
                act(sq[:, 0:n], xT3[:, k, t0:t0 + n], AF.Square)
                mm(pb[:, 0:n], onesb[:], sq[:, 0:n], start=(k == 0), stop=(k == 7))
            t = rr(tmpf, "tf")
            act(t[:, 0:n], pb[:, 0:n], AF.Sqrt, bias=EPS, scale=1.0 / D)
            recip(rstd_bc[:, 0:n], t[:, 0:n])
            for k in range(8):
                t = rr(tmpf, "tf")
                tt(t[:, 0:n], xT3[:, k, t0:t0 + n], rstd_bc[:, 0:n], ALU.mult)
                gi = ni * 16 + k * 2 + st
                mi = ((0 if ni == 0 else 24) + k) * 2 + st
                act(hT3[:, k, t0:t0 + n], t[:, 0:n], AF.Identity, bias=modS[:, mi:mi + 1], scale=gsS[:, gi:gi + 1])

    def fpass(wslot3, ncols_chunks, rhs3, kch, tbs, evac):
        for j in ncols_chunks:
            for (t0, n) in tbs:
                pb = bank()
                for k in range(kch):
                    mm(pb[:, 0:n], wslot3[:, k, j * 128:(j + 1) * 128], rhs3[:, k, t0:t0 + n],
                       start=(k == 0), stop=(k == kch - 1))
                evac(j, t0, n, pb)

    for l in range(L):
        last = (l == L - 1)
        lam_init = 0.8 - 0.6 * math.exp(-0.3 * l)
        TBX = TB_ALL[1:] if last else TB_ALL
        W_in = w_in[l]

        dma(gains[:], gains_d[l:l + 1, :].to_broadcast([128, 1344]))
        G_QA, G_KA, G_SUB, G_CQ, G_CKV, G_QB, G_KB, G_LAM = 0, 64, 128, 256, 640, 896, 992, 1088
        t = rr(tmpf, "tf")
        tt(t[:, 0:64], gains[:, G_LAM:G_LAM + 64], gains[:, G_LAM + 64:G_LAM + 128], ALU.mult)
        tt(t[:, 64:128], gains[:, G_LAM + 128:G_LAM + 192], gains[:, G_LAM + 192:G_LAM + 256], ALU.mult)
        red(lamt[:, 0:2], t[:, 0:128].rearrange("p (a b) -> p a b", a=2))
        act(lamt[:, 2:4], lamt[:, 0:2], AF.Exp)
        tt(lamt[:, 5:6], lamt[:, 3:4], lamt[:, 2:3], ALU.subtract)
        ts(lamt[:, 4:5], lamt[:, 5:6], -lam_init, ALU.add)
        ts(gsub[:], gains[:, G_SUB:G_SUB + 128], 1.0 - lam_init, ALU.mult)

        def mod_load(l2, s_):
            wsl = slot()
            w3 = wsl[:].rearrange("p (k c) -> p k c", k=8)
            w_cols(w_mod[l2], s_ * 512, 512, w3)
            return w3

        def mod_mm(l2, s_, w3):
            pbm = bank()
            for m in range(4):
                for k in range(8):
                    mm(pbm[:, m * 2:m * 2 + 2], w3[:, k, m * 128:(m + 1) * 128], sc3[:, k, :],
                       start=(k == 0), stop=(k == 7))
            cp(modraw[l2 % 2][:, s_ * 8:(s_ + 1) * 8], pbm[:, 0:8])
        if l == 0:
            for s_ in range(12):
                mod_mm(0, s_, mod_load(0, s_))
        mps = modraw[l % 2]
        mod3 = modS[:].rearrange("p (m s) -> p m s", s=2)
        tt(mod3, mps[:, 0:96].rearrange("p (m s) -> p m s", s=2),
           cols[:, l * 48:(l + 1) * 48].unsqueeze(2).to_broadcast([128, 48, 2]), ALU.add)
        gs4 = gsS[:].rearrange("p (n k s) -> p n k s", n=2, s=2)
        for ni in range(2):
            so = 8 if ni == 0 else 32
            gcol = (192 if ni == 0 else 224) + l * 8
            ts(gs4[:, ni], mod3[:, so:so + 8, :], 1.0, ALU.add)
            tt(gs4[:, ni], gs4[:, ni], cols[:, gcol:gcol + 8].unsqueeze(2).to_broadcast([128, 8, 2]), ALU.mult)

        norm_and_h(l, 0, TB_ALL)
        for k in range(8):
            dma(xs_d[k], xT3[:, k, :], wr=[("xs", k, k + 1, 0, NT * 4)])

        bank_list[0] = [0]
        sc_banks = [PS[1], PS[2], PS[3]]
        acc_sets = [(PS[4], PS[5]), (PS[6], PS[7])]

        units = []

        def attention(*u):
            units.append(u)

        def run_units():
            steps = [(ui, ki) for ui, u in enumerate(units) for ki in range(len(u[7]))]

            def score(i):
                ui, ki = steps[i]
                qT, kT, vfn, dv, scale, q0, nq, kts, fin = units[ui]
                sb_ = rr(sc_banks, "sc")
                kt = kts[ki]
                mm(sb_[:, 0:nq], kT[:, kt * 128:(kt + 1) * 128], qT[:, q0:q0 + nq])
                return sb_
            accs = None
            issued = []

            def ensure(n_):
                while len(issued) < min(n_, len(steps)):
                    issued.append(score(len(issued)))
            ensure(2)
            for i, (ui, ki) in enumerate(steps):
                qT, kT, vfn, dv, scale, q0, nq, kts, fin = units[ui]
                ensure(i + 3)
                cur = issued[i]
                nqt = nq // 128
                per_bank = 2 if dv == 128 else 4
                if ki == 0:
                    accs = rr(acc_sets, "acc")
                kt = kts[ki]
                p = rr(PT, "pt")
                act(p[:, 0:nq], cur[:, 0:nq], AF.Exp, scale=scale)
                for qt in range(nqt):
                    bk = accs[qt // per_bank]
                    o = (qt % per_bank) * (dv + 1)
                    first = (ki == 0 and qt % per_bank == 0)
                    mm(bk[:, o:o + dv + 1], p[:, qt * 128:(qt + 1) * 128], vfn(kt),
                       start=first, stop=(ki == len(kts) - 1), skip=True)
                if ki == len(kts) - 1:
                    fin(accs, nqt, per_bank)
            del units[:]

        KA3 = Q(0, 4608).rearrange("p (c t) -> p c t", c=2)
        QP4 = Q(4608, 13824).rearrange("p (h m t) -> p h m t", h=2, m=2)
        VA4 = Q(13824, 13824 + NTT * 258).rearrange("p (t h c) -> p t h c", t=NTT, h=2)
        rA = ropeA[:].rearrange("p (s t c) -> p s t c", s=2, t=16)
        qk_unit_done = [0]
        for hp in range(2):
            s1 = slot()
            s2 = slot()
            w1 = s1[:].rearrange("p (k c) -> p k c", k=8)
            w2 = s2[:, 0:2048].rearrange("p (k c) -> p k c", k=8)
            w_cols(W_in, OFF_Q + hp * 256, 256, w1[:, :, 0:256])
            w_cols(W_in, OFF_K + hp * 256, 256, w1[:, :, 256:512])
            w_cols(W_in, OFF_V + hp * 256, 256, w2)
            for hh_ in range(2):
                mset(VA4[:, :, hh_, 128:129], 1.0)
                mset(QP4[64:128, hh_, 0, :], 0.0)
                mset(QP4[0:64, hh_, 1, :], 0.0)
            bank_list[0] = list(range(8))

            def a_mm(tt_):
                tk = slice(tt_ * 128, (tt_ + 1) * 128)
                p1 = bank()
                for k in range(8):
                    mm(p1[:], hT3[:, k, tk], w1[:, k, :], start=(k == 0), stop=(k == 7))
                p2 = bank()
                for k in range(8):
                    mm(p2[:, 0:256], hT3[:, k, tk], w2[:, k, :], start=(k == 0), stop=(k == 7))
                return p1, p2
            nxt_ = a_mm(0)
            for tt_ in range(NTT):
                tk = slice(tt_ * 128, (tt_ + 1) * 128)
                p1, p2 = nxt_
                if os.environ.get('NOPIPE_A') is None:
                    nxt_ = a_mm(tt_ + 1) if tt_ + 1 < NTT else None
                sq = rr(tmpf, "tf")
                act(sq[:], p1[:], AF.Square)
                red(small[:, 0:8], sq[:].rearrange("p (g c) -> p g c", g=8))
                act(small[:, 8:16], small[:, 0:8], AF.Sqrt, bias=EPS, scale=1.0 / 64)
                recip(small[:, 16:24], small[:, 8:16])
                qn = rr(tmpf, "tf")
                qn3 = qn[:].rearrange("p (g c) -> p g c", g=8)
                tt(qn3, p1[:].rearrange("p (g c) -> p g c", g=8),
                   small[:, 16:24].unsqueeze(2).to_broadcast([128, 8, 64]), ALU.mult)
                qn4 = qn[:].rearrange("p (a g c) -> p a g c", a=2, g=4)
                tt(qn4, qn4, gains[:, G_QA:G_QA + 128].rearrange("p (a c) -> p a c", a=2).unsqueeze(2)
                   .to_broadcast([128, 2, 4, 64]), ALU.mult)
                qkb = rr(tmpb, "qkb")
                if tt_ >= 2:
                    cc = rA[:, 0, tt_ - 2, :].unsqueeze(1).to_broadcast([128, 8, 64])
                    ss = rA[:, 1, tt_ - 2, :].unsqueeze(1).to_broadcast([128, 8, 64])
                    t1 = rr(tmpf, "tf")
                    t2 = rr(tmpf, "tf")
                    t13 = t1[:].rearrange("p (g c) -> p g c", g=8)
                    t23 = t2[:].rearrange("p (g c) -> p g c", g=8)
                    tt(t13, qn3, cc, ALU.mult)
                    tt(t23, qn3, ss, ALU.mult)
                    qkb3 = qkb[:].rearrange("p (g c) -> p g c", g=8)
                    tt(qkb3[:, :, 0:32], t13[:, :, 0:32], t23[:, :, 32:64], ALU.subtract)
                    tt(qkb3[:, :, 32:64], t23[:, :, 0:32], t13[:, :, 32:64], ALU.add)
                else:
                    cp(qkb[:], qn[:])
                cp(VA4[:, tt_, :, 0:128], p2[:, 0:256].rearrange("p (h c) -> p h c", h=2), eng="scalar")
                pbk = bank()
                pbb = pbk[:].bitcast(BF16)
                for j in range(4):
                    tr(pbb[:, j * 128:(j + 1) * 128], qkb[:, j * 128:(j + 1) * 128], identb[:])
                pb4 = pbb[:, 0:512].rearrange("p (c t) -> p c t", c=4)
                cp(KA3[:, :, tk], pb4[:, 2:4, :], eng="scalar")
                cp(QP4[0:64, :, 0, tk], pb4[0:64, 0:2, :], eng="scalar")
                cp(QP4[64:128, :, 1, tk], pb4[64:128, 0:2, :], eng="scalar")
                if os.environ.get('NOPIPE_A') is not None:
                    nxt_ = a_mm(tt_ + 1) if tt_ + 1 < NTT else None
            bank_list[0] = [0]
            for hh in range(2):
                h = hp * 2 + hh
                qblocks = [(256 + i * 512, 512, list(range(NTT))) for i in range(4)]
                if not last:
                    qblocks.append((0, 256, [0, 1]))
                for (q0, nq, kts) in qblocks:
                    def fin0(accs, nqt, per_bank, nq=nq):
                        for b in range((nqt + 1) // 2):
                            a3 = accs[b][:, 0:258].rearrange("p (t c) -> p t c", t=2)
                            recip(small[:, 24 + 2 * b:26 + 2 * b], a3[:, :, 128])
                            tt(oA[:, b * 256:(b + 1) * 256].rearrange("p (t c) -> p t c", t=2), a3[:, :, 0:128],
                               small[:, 24 + 2 * b:26 + 2 * b].unsqueeze(2).to_broadcast([128, 2, 128]), ALU.mult)

                    def fin1(accs, nqt, per_bank, nq=nq, q0=q0, h=h):
                        o = rr(tmpf, "tf")
                        for b in range((nqt + 1) // 2):
                            a3 = accs[b][:, 0:258].rearrange("p (t c) -> p t c", t=2)
                            recip(small[:, 28 + 2 * b:30 + 2 * b], a3[:, :, 128])
                            ts(small[:, 32 + 2 * b:34 + 2 * b], small[:, 28 + 2 * b:30 + 2 * b], lamt[:, 4:5], ALU.mult)
                            o3 = o[:, b * 256:(b + 1) * 256].rearrange("p (t c) -> p t c", t=2)
                            tt(o3, a3[:, :, 0:128],
                               small[:, 32 + 2 * b:34 + 2 * b].unsqueeze(2).to_broadcast([128, 2, 128]), ALU.mult)
                        tt(o[:, 0:nq], o[:, 0:nq], oA[:, 0:nq], ALU.add)
                        sq = rr(tmpf, "tf")
                        act(sq[:, 0:nq], o[:, 0:nq], AF.Square)
                        red(small[:, 36:36 + nqt], sq[:, 0:nq].rearrange("p (t c) -> p t c", c=128))
                        act(small[:, 40:40 + nqt], small[:, 36:36 + nqt], AF.Sqrt, bias=EPS, scale=1.0 / 128)
                        recip(small[:, 44:44 + nqt], small[:, 40:40 + nqt])
                        o3 = o[:, 0:nq].rearrange("p (t c) -> p t c", c=128)
                        tt(o3, o3, small[:, 44:44 + nqt].unsqueeze(2).to_broadcast([128, nqt, 128]), ALU.mult)
                        yb = rr(tmpb, "qkb")
                        tt(yb[:, 0:nq].rearrange("p (t c) -> p t c", c=128), o3,
                           gsub[:].unsqueeze(1).to_broadcast([128, nqt, 128]), ALU.mult)
                        pbk = bank()
                        pbb = pbk[:].bitcast(BF16)
                        for qt in range(nqt):
                            tr(pbb[:, qt * 128:(qt + 1) * 128], yb[:, qt * 128:(qt + 1) * 128], identb[:])
                        cp(yT3[:, h, q0:q0 + nq], pbb[:, 0:nq], eng="scalar")

                    for m in range(2):
                        attention(QP4[:, hh, m, :], KA3[:, hh, :],
                                  lambda kt, hh=hh: VA4[:, kt, hh, :], 128, 0.125, q0, nq, kts,
                                  fin0 if m == 0 else fin1)
            run_units()

        merged = [0]

        def merge(n):
            bank_list[0] = list(range(8))
            g1 = slot()
            g2 = slot()
            wbs = slot()
            gw = [g1[:].rearrange("p (k c) -> p k c", k=8), g2[:].rearrange("p (k c) -> p k c", k=8)]
            w_cols(W_in, OFF_G + n * 1024, 512, gw[0])
            w_cols(W_in, OFF_G + n * 1024 + 512, 512, gw[1])
            wb3 = wbs[:].rearrange("p (k c) -> p k c", k=4)
            wload(wb3, w_branch[l, n].rearrange("(k p) c -> p k c", p=128))
            first = (merged[0] == 0)
            merged[0] += 1
            for j in range(8):
                for (t0, n_) in TBX:
                    pg = bank()
                    for k in range(8):
                        mm(pg[:, 0:n_], gw[j // 4][:, k, (j % 4) * 128:(j % 4 + 1) * 128], hT3[:, k, t0:t0 + n_],
                           start=(k == 0), stop=(k == 7))
                    pp = bank()
                    for k in range(4):
                        mm(pp[:, 0:n_], wb3[:, k, j * 128:(j + 1) * 128], yT3[:, k, t0:t0 + n_],
                           start=(k == 0), stop=(k == 3))
                    sg = rr(tmpf, "tf")
                    act(sg[:, 0:n_], pg[:, 0:n_], AF.Sigmoid)
                    if first:
                        tt(mix3[:, j, t0:t0 + n_], sg[:, 0:n_], pp[:, 0:n_], ALU.mult)
                    else:
                        tt(sg[:, 0:n_], sg[:, 0:n_], pp[:, 0:n_], ALU.mult)
                        tt(mix3[:, j, t0:t0 + n_], sg[:, 0:n_], mix3[:, j, t0:t0 + n_], ALU.add)

        merge(0)

        bank_list[0] = [0, 6, 7]
        acc_sets = [(PS[4],), (PS[5],)]
        CT3 = Q(0, 11520).rearrange("p (c t) -> p c t", c=5)
        QB3 = Q(11520, 16128).rearrange("p (c t) -> p c t", c=2)
        VB3 = Q(16128, 16128 + NTT * 65).rearrange("p (t c) -> p t c", c=65)
        KR3 = Q(17298, 17298 + NTT * 32).rearrange("p (t c) -> p t c", c=32)
        rB = ropeB[:].rearrange("p (s t c) -> p s t c", s=2, t=16)
        s1 = slot()
        s2 = slot()
        w1 = s1[:].rearrange("p (k c) -> p k c", k=8)
        w2 = s2[:, 0:1280].rearrange("p (k c) -> p k c", k=8)
        w_cols(W_in, OFF_CQ, 512, w1)
        w_cols(W_in, OFF_CQ + 512, 160, w2)
        bank_list[0] = list(range(8))

        def b1_mm(tt_):
            tk = slice(tt_ * 128, (tt_ + 1) * 128)
            p1 = bank()
            for k in range(8):
                mm(p1[:], hT3[:, k, tk], w1[:, k, :], start=(k == 0), stop=(k == 7))
            p2 = bank()
            for k in range(8):
                mm(p2[:, 0:160], hT3[:, k, tk], w2[:, k, :], start=(k == 0), stop=(k == 7))
            return p1, p2
        nxt_ = b1_mm(0)
        for tt_ in range(NTT):
            tk = slice(tt_ * 128, (tt_ + 1) * 128)
            p1, p2 = nxt_
            if os.environ.get('NOPIPE_B1') is None:
                nxt_ = b1_mm(tt_ + 1) if tt_ + 1 < NTT else None
            sq = rr(tmpf, "tf")
            sq2 = rr(tmpf, "tf")
            act(sq[:], p1[:], AF.Square)
            act(sq2[:, 0:160], p2[:, 0:160], AF.Square)
            red(small[:, 0:1], sq[:, 0:384])
            red(small[:, 4:5], sq[:, 384:512])
            red(small[:, 5:6], sq2[:, 0:128])
            tt(small[:, 1:2], small[:, 4:5], small[:, 5:6], ALU.add)
            red(small[:, 2:3], sq2[:, 128:160])
            tt(small[:, 8:11], small[:, 0:3], poolfix[:, 32:35], ALU.mult)
            act(small[:, 12:15], small[:, 8:11], AF.Sqrt, bias=EPS)
            recip(small[:, 16:19], small[:, 12:15])
            cb = rr(tmpb, "qkb")
            stt(cb[:, 0:384], p1[:, 0:384], small[:, 16:17], gains[:, G_CQ:G_CQ + 384], ALU.mult, ALU.mult)
            stt(cb[:, 384:512], p1[:, 384:512], small[:, 17:18], gains[:, G_CKV:G_CKV + 128], ALU.mult, ALU.mult)
            cb2 = rr(tmpb, "qkb")
            stt(cb2[:, 0:128], p2[:, 0:128], small[:, 17:18], gains[:, G_CKV + 128:G_CKV + 256], ALU.mult, ALU.mult)
            if tt_ >= 2:
                kr = rr(tmpf, "tf")
                stt(kr[:, 0:32], p2[:, 128:160], small[:, 18:19], gains[:, G_KB + 64:G_KB + 96], ALU.mult, ALU.mult)
                tt(kr[:, 32:64], kr[:, 0:32], rB[:, 0, tt_ - 2, :], ALU.mult)
                tt(kr[:, 64:96], kr[:, 0:32], rB[:, 1, tt_ - 2, :], ALU.mult)
                tt(KR3[:, tt_, 0:16], kr[:, 32:48], kr[:, 80:96], ALU.subtract)
                tt(KR3[:, tt_, 16:32], kr[:, 64:80], kr[:, 48:64], ALU.add)
            else:
                stt(KR3[:, tt_, :], p2[:, 128:160], small[:, 18:19], gains[:, G_KB + 64:G_KB + 96], ALU.mult, ALU.mult)
            pbk = bank()
            pbb = pbk[:].bitcast(BF16)
            for j in range(4):
                tr(pbb[:, j * 128:(j + 1) * 128], cb[:, j * 128:(j + 1) * 128], identb[:])
            tr(pbb[:, 512:640], cb2[:, 0:128], identb[:])
            cp(CT3[:, :, tk], pbb[:, 0:640].rearrange("p (c t) -> p c t", c=5), eng="scalar")
            if os.environ.get('NOPIPE_B1') is not None:
                nxt_ = b1_mm(tt_ + 1) if tt_ + 1 < NTT else None
        bank_list[0] = [0, 6, 7]
        mset(QB3[64:128, :, :], 0.0)
        su = slot()
        sk = slot()
        wuq3 = su[:, 0:2304].rearrange("p (k c) -> p k c", k=3)
        wukv3 = sk[:, 0:2048].rearrange("p (k c) -> p k c", k=2)
        wload(wuq3, w_uq[l].rearrange("(k p) c -> p k c", p=128))
        wload(wukv3, w_ukv[l].rearrange("(k p) c -> p k c", p=128))
        yb4 = ybtok[:].rearrange("p (t h c) -> p t h c", t=NTT, h=2)
        for h in range(8):
            mset(VB3[:, :, 64:65], 1.0)
            def b2_mm(tt_, h=h):
                tk = slice(tt_ * 128, (tt_ + 1) * 128)
                pq = bank()
                for k in range(2):
                    mm(pq[:, 0:128], CT3[:, 3 + k, tk], wukv3[:, k, h * 128:(h + 1) * 128], start=(k == 0), stop=(k == 1))
                for k in range(3):
                    mm(pq[:, 128:224], CT3[:, k, tk], wuq3[:, k, h * 96:(h + 1) * 96], start=(k == 0), stop=(k == 2),
                       skip=True)
                return pq
            def b2_chain(tt_, pq, par, h=h):
                tk = slice(tt_ * 128, (tt_ + 1) * 128)
                sm = smalls[par]
                sq = tmpf[par]
                kr = tmpf[2 + par]
                qk = tmpb[par]
                act(sq[:, 0:224], pq[:, 0:224], AF.Square)
                yield
                red(sm[:, 0:7], sq[:, 0:224].rearrange("p (g c) -> p g c", c=32))
                yield
                r8 = sm[:, 0:8].rearrange("p (a b) -> p a b", b=4)
                tt(sm[:, 8:10], r8[:, :, 0], r8[:, :, 1], ALU.add)
                ts(sm[:, 10:11], sm[:, 6:7], 2.0, ALU.mult)
                cp(VB3[:, tt_, 0:64], pq[:, 64:128], eng="scalar")
                yield
                act(sm[:, 12:15], sm[:, 8:11], AF.Sqrt, bias=EPS, scale=1.0 / 64)
                yield
                recip(sm[:, 16:19], sm[:, 12:15])
                yield
                stt(qk[:, 0:64], pq[:, 128:192], sm[:, 17:18], gains[:, G_QB:G_QB + 64], ALU.mult, ALU.mult)
                stt(qk[:, 96:160], pq[:, 0:64], sm[:, 16:17], gains[:, G_KB:G_KB + 64], ALU.mult, ALU.mult)
                if tt_ >= 2:
                    stt(kr[:, 0:32], pq[:, 192:224], sm[:, 18:19], gains[:, G_QB + 64:G_QB + 96], ALU.mult, ALU.mult)
                else:
                    stt(qk[:, 64:96], pq[:, 192:224], sm[:, 18:19], gains[:, G_QB + 64:G_QB + 96], ALU.mult, ALU.mult)
                cp(qk[:, 160:192], KR3[:, tt_, :], eng="gpsimd")
                yield
                if tt_ >= 2:
                    tt(kr[:, 32:64], kr[:, 0:32], rB[:, 0, tt_ - 2, :], ALU.mult, eng="gpsimd")
                    tt(kr[:, 64:96], kr[:, 0:32], rB[:, 1, tt_ - 2, :], ALU.mult, eng="gpsimd")
                    yield
                    tt(qk[:, 64:80], kr[:, 32:48], kr[:, 80:96], ALU.subtract, eng="gpsimd")
                    tt(qk[:, 80:96], kr[:, 64:80], kr[:, 48:64], ALU.add, eng="gpsimd")
                    yield
                pbk = bank()
                pbb = pbk[:].bitcast(BF16)
                tr(pbb[0:96, 0:128], qk[:, 0:96], identb[:])
                tr(pbb[0:96, 128:256], qk[:, 96:192], identb[:])
                yield
                cp(QB3[0:96, :, tk], pbb[0:96, 0:256].rearrange("p (c t) -> p c t", c=2), eng="scalar")

            bank_list[0] = list(range(8))
            pairs = [(2 * i, 2 * i + 1) for i in range(NTT // 2)]
            nxt_ = [b2_mm(pairs[0][0]), b2_mm(pairs[0][1])]
            for pi, (ta, tb_) in enumerate(pairs):
                cur_ = nxt_
                if pi + 1 < len(pairs):
                    nxt_ = [b2_mm(pairs[pi + 1][0]), b2_mm(pairs[pi + 1][1])]
                interleave([b2_chain(ta, cur_[0], 0), b2_chain(tb_, cur_[1], 1)])
            bank_list[0] = [0, 6, 7]
            qblocks = [(256 + i * 512, 512, list(range(NTT))) for i in range(4)]
            if not last:
                qblocks.append((0, 256, [0, 1]))
            for (q0, nq, kts) in qblocks:
                def finb(accs, nqt, per_bank, q0=q0, h=h):
                    a3 = accs[0][:, 0:nqt * 65].rearrange("p (t c) -> p t c", c=65)
                    recip(small[:, 24:24 + nqt], a3[:, :, 64])
                    tt(yb4[:, q0 // 128:q0 // 128 + nqt, h % 2, :], a3[:, :, 0:64],
                       small[:, 24:24 + nqt].unsqueeze(2).to_broadcast([128, nqt, 64]), ALU.mult)
                attention(QB3[:, 0, :], QB3[:, 1, :], lambda kt: VB3[:, kt, :], 64, 96.0 ** -0.5,
                          q0, nq, kts, finb)
            run_units()
            if h % 2 == 1:
                tiles = list(range(2, NTT)) if last else list(range(NTT))
                for g0 in range(0, len(tiles), 4):
                    grp = tiles[g0:g0 + 4]
                    pbk = bank()
                    pbb = pbk[:].bitcast(BF16)
                    for i, tq in enumerate(grp):
                        tr(pbb[:, i * 128:(i + 1) * 128], ybtok[:, tq * 128:(tq + 1) * 128], identb[:])
                    cp(yT3[:, h // 2, grp[0] * 128:(grp[-1] + 1) * 128], pbb[:, 0:len(grp) * 128], eng="scalar")
        merge(1)

        bank_list[0] = list(range(8))
        UP = [Q(i * 4736, (i + 1) * 4736).bitcast(F32) for i in range(3)]
        DTp = Q(14208, 14208 + PADL)
        su_ = slot()
        wu3 = su_[:].rearrange("p (k c) -> p k c", k=8)
        w_cols(W_in, OFF_U, 512, wu3)
        sp_ = slot()
        wp3 = sp_[:, 0:512].rearrange("p (g d) -> p g d", g=4)
        wload(wp3, w_pool[l].rearrange("g c d -> c g d"))
        for g in range(4):
            w = 2 << g
            A_, B_, C_ = UP
            for (a, b) in ((0, 16), (272, 304), (2352, 2368)):
                mset(A_[:, a:b], 0.0)

            def ev_u(j, t0, n, pb, A_=A_):
                cp(A_[:, pcol(t0):pcol(t0) + n], pb[:, 0:n], eng="scalar")
            fpass(wu3, [g], hT3, 8, TB_ALL, ev_u)
            src, dst = A_, B_
            tt(dst[:, 1:PADL], src[:, 0:PADL - 1], src[:, 1:PADL], ALU.add)
            lo, hi = 1, PADL
            if g >= 1:
                src, dst = B_, C_
                tt(dst[:, 2:PADL - 1], src[:, 1:PADL - 2], src[:, 3:PADL], ALU.add)
            if g >= 2:
                src, dst = C_, B_
                tt(dst[:, 4:PADL - 3], src[:, 2:PADL - 5], src[:, 6:PADL - 1], ALU.add)
            if g >= 3:
                src, dst = B_, C_
                tt(dst[:, 8:PADL - 7], src[:, 4:PADL - 11], src[:, 12:PADL - 3], ALU.add)
            Lg = dst
            ts(Lg[:, 16:2352], Lg[:, 16:2352], 1.0 / w, ALU.mult)
            for (s0, sl) in ((16, 256), (304, 2048)):
                tt(Lg[:, s0:s0 + 8], Lg[:, s0:s0 + 8], poolfix[:, g * 8:g * 8 + 8], ALU.mult)
                tt(Lg[:, s0 + sl - 8:s0 + sl], Lg[:, s0 + sl - 8:s0 + sl], poolfix[:, 40 + g * 8:48 + g * 8], ALU.mult)
            tt(DTp[:, 16:2352], Lg[:, 16:2352], A_[:, 16:2352], ALU.subtract)
            for (t0, n) in TB_ALL:
                pb = bank()
                mm(pb[:, 0:n], wp3[:, g, :], DTp[:, pcol(t0):pcol(t0) + n])
                act(yT3[:, g, t0:t0 + n], pb[:, 0:n], AF.Identity, scale=cols[:, 256 + l * 4 + g:257 + l * 4 + g])
        merge(2)

        PCs, U2, YS = UP
        for g in range(4):
            sd = slot()
            wd4 = sd[:, 0:3072].rearrange("p (k s c) -> p k s c", k=8, s=3)
            for si, off in enumerate((OFF_PC, OFF_PX, OFF_PB)):
                w_cols(W_in, off + g * 128, 128, wd4[:, :, si, :])
            wm_ = mod_load(l + 1, 8 + g) if not last else None
            for (a, b) in ((0, 16), (272, 304), (2352, 2368)):
                mset(U2[:, a:b], 0.0)

            def ev_pc(j, t0, n, pb):
                cp(PCs[:, pcol(t0):pcol(t0) + n], pb[:, 0:n], eng="scalar")

            def ev_px(j, t0, n, pb):
                tt(U2[:, pcol(t0):pcol(t0) + n], pb[:, 0:n], PCs[:, pcol(t0):pcol(t0) + n], ALU.mult)

            def ev_pb(j, t0, n, pb, g=g):
                tt(yT3[:, g, t0:t0 + n], pb[:, 0:n], YS[:, pcol(t0):pcol(t0) + n], ALU.mult)
            fpass(wd4[:, :, 0, :], [0], hT3, 8, TB_ALL, ev_pc)
            fpass(wd4[:, :, 1, :], [0], hT3, 8, TB_ALL, ev_px)
            if wm_ is not None:
                mod_mm(l + 1, 8 + g, wm_)
            wc = 272 + l * 12
            ts(YS[:, 16:2352], U2[:, 16:2352], cols[:, wc + 4 + g:wc + 5 + g], ALU.mult)
            stt(YS[:, 16:2352], U2[:, 15:2351], cols[:, wc + g:wc + g + 1], YS[:, 16:2352], ALU.mult, ALU.add)
            stt(YS[:, 16:2352], U2[:, 17:2353], cols[:, wc + 8 + g:wc + 9 + g], YS[:, 16:2352], ALU.mult, ALU.add)
            fpass(wd4[:, :, 2, :], [0], hT3, 8, TB_ALL, ev_pb)
        merge(3)

        so1 = slot()
        so2 = slot()
        wo = [so1[:].rearrange("p (k c) -> p k c", k=8), so2[:].rearrange("p (k c) -> p k c", k=8)]
        w_cols(w_o[l], 0, 512, wo[0])
        w_cols(w_o[l], 512, 512, wo[1])
        bufs8 = [xblk[0][:, 0:512], xblk[0][:, 512:1024], xblk[1][:, 0:512], xblk[1][:, 512:1024]] + [t_[:] for t_ in tmpf]
        bank_list[0] = list(range(7))
        for (t0, n) in TBX:
            st = 1 if t0 == 0 else 0
            pbn = PS[7]
            for i in range(8):
                dma(bufs8[i][:, 0:n], xs_d[i][:, t0:t0 + n], rd=[("xs", i, i + 1, t0 * 4, (t0 + n) * 4)])
            for i in range(8):
                xb = bufs8[i]
                pb = bank()
                for k in range(8):
                    mm(pb[:, 0:n], wo[i // 4][:, k, (i % 4) * 128:(i % 4 + 1) * 128], mix3[:, k, t0:t0 + n],
                       start=(k == 0), stop=(k == 7))
                gi = (16 + i) * 2 + st
                stt(xb[:, 0:n], pb[:, 0:n], modS[:, gi:gi + 1], xb[:, 0:n], ALU.mult, ALU.add)
                dma(xs_d[i][:, t0:t0 + n], xb[:, 0:n], wr=[("xs", i, i + 1, t0 * 4, (t0 + n) * 4)])
                sq = rr(tmpb, "sq")
                act(sq[:, 0:n], xb[:, 0:n], AF.Square)
                mm(pbn[:, 0:n], onesb[:], sq[:, 0:n], start=(i == 0), stop=(i == 7))
            act(oA[:, 0:n], pbn[:, 0:n], AF.Sqrt, bias=EPS, scale=1.0 / D)
            recip(rstd_bc[:, 0:n], oA[:, 0:n])
            for i in range(8):
                xb = bufs8[i]
                tt(xb[:, 0:n], xb[:, 0:n], rstd_bc[:, 0:n], ALU.mult)
                act(hT3[:, i, t0:t0 + n], xb[:, 0:n], AF.Identity, bias=modS[:, (24 + i) * 2 + st:(24 + i) * 2 + st + 1],
                    scale=gsS[:, 16 + i * 2 + st:16 + i * 2 + st + 1])
        bank_list[0] = list(range(8))
        for k in range(8):
            dma(xT3[:, k, :], xs_d[k], rd=[("xs", k, k + 1, 0, NT * 4)])

        for g in range(8):
            s1 = slot()
            s2 = slot()
            w13 = s1[:].rearrange("p (k c) -> p k c", k=8)
            w23 = s2[:].rearrange("p (k c) -> p k c", k=4)
            w_cols(w_ff1[l], g * 512, 512, w13)
            wload(w23, w_ff2[l][g * 512:(g + 1) * 512, :].rearrange("(k p) c -> p k c", p=128))
            wm_ = mod_load(l + 1, g) if not last else None

            def ev_a(j, t0, n, pb):
                r = rr(tmpf, "tf")
                act(r[:, 0:n], pb[:, 0:n], AF.Relu)
                tt(aT3[:, j, t0:t0 + n], r[:, 0:n], r[:, 0:n], ALU.mult)
            fpass(w13, range(4), hT3, 8, TBX, ev_a)
            if wm_ is not None:
                mod_mm(l + 1, g, wm_)

            def ev_x(i, t0, n, pb):
                gi = (40 + i) * 2 + (1 if t0 == 0 else 0)
                stt(xT3[:, i, t0:t0 + n], pb[:, 0:n], modS[:, gi:gi + 1], xT3[:, i, t0:t0 + n], ALU.mult, ALU.add)
            fpass(w23, range(8), aT3, 4, TBX, ev_x)

    for tt_ in range(2, NTT):
        xb = rr(xblk, "xin")
        for half in range(2):
            pb = bank()
            for j in range(4):
                k = half * 4 + j
                tr(pb[:, j * 128:(j + 1) * 128], xT3[:, k, tt_ * 128:(tt_ + 1) * 128], identf[:])
            cp(xb[:, half * 512:(half + 1) * 512], pb[:], eng=("vector" if half == 0 else "scalar"))
        dma(y_d[(tt_ - 2) * 128:(tt_ - 1) * 128, :], xb[:])
    S.add("sync", lambda e: e.nop(), [y_d], [])
    S.emit()
    es.close()
    return nc


def _consts():
    identf = np.eye(128, dtype=np.float32)

    def rope(rot_dim):
        rows = S_LAT // 64
        row = np.repeat(np.arange(rows, dtype=np.float32), 64)
        col = np.tile(np.arange(64, dtype=np.float32), rows)
        n_freq = rot_dim // 4
        inv = (np.float32(10000.0) ** (-np.arange(n_freq, dtype=np.float32) / np.float32(n_freq))).astype(np.float32)
        ang = np.concatenate([row[:, None] * inv, col[:, None] * inv], axis=-1).astype(np.float32)
        cs, sn = np.cos(ang).astype(np.float32), np.sin(ang).astype(np.float32)
        half = rot_dim // 2
        cc = np.concatenate([cs, cs], -1).reshape(16, 128, 2 * half).transpose(1, 0, 2)
        ss = np.concatenate([sn, sn], -1).reshape(16, 128, 2 * half).transpose(1, 0, 2)
        return np.ascontiguousarray(np.stack([cc, ss], 1).reshape(128, -1)).astype(np.float32)

    fix = np.ones((128,), np.float32)
    for g, w in enumerate((2, 4, 8, 16)):
        for t in range(w // 2):
            fix[g * 8 + t] = w / (t + w // 2)
        for j in range(1, w // 2):
            fix[40 + g * 8 + (8 - j)] = w / (j + w // 2)
    fix[32:35] = (1.0 / 384, 1.0 / 256, 1.0 / 32)
    poolfix = np.tile(fix[None, :], (128, 1)).astype(np.float32)
    return identf, rope(64), rope(32), poolfix


_NC_CACHE = {}


def kernel(x, c, ctx, c_ctx, w_mod, b_mod, g_norm1, g_norm2, w_in, gq_a, gk_a, lam_a, g_sub_a,
           g_cq, w_uq, g_ckv, w_ukv, gq_b, gk_b, w_pool, s_pool, w_conv, w_branch, w_o, w_ff1, w_ff2,
           _cores=8):
    f = lambda a: np.ascontiguousarray(np.asarray(a, dtype=np.float32))
    x, c, ctx, c_ctx = f(x), f(c), f(ctx), f(c_ctx)
    L = int(np.asarray(w_mod).shape[0])
    identf, ropeA, ropeB, poolfix = _consts()
    gains = np.concatenate([f(gq_a), f(gk_a), f(g_sub_a), f(g_cq), f(g_ckv), f(gq_b), f(gk_b),
                            f(lam_a).reshape(L, 256)], axis=1)
    shared = dict(w_mod=f(w_mod), w_in=f(w_in), gains=np.ascontiguousarray(gains), w_uq=f(w_uq), w_ukv=f(w_ukv),
                  w_pool=f(w_pool), w_branch=f(w_branch), w_o=f(w_o), w_ff1=f(w_ff1), w_ff2=f(w_ff2),
                  identf=identf, ropeA=ropeA, ropeB=ropeB, poolfix=poolfix)
    rows = np.zeros((384, 128), np.float32)
    rows[0:L * 48] = f(b_mod).reshape(L * 48, 128)
    rows[192:192 + L * 8] = f(g_norm1).reshape(L * 8, 128)
    rows[224:224 + L * 8] = f(g_norm2).reshape(L * 8, 128)
    rows[256:256 + L * 4] = f(s_pool).reshape(L * 4, 128)
    rows[272:272 + L * 12] = f(w_conv).reshape(L * 12, 128)
    rows[328:336] = c_ctx.reshape(8, 128)
    if L not in _NC_CACHE:
        _NC_CACHE[L] = build(L)
    nc = _NC_CACHE[L]
    in_maps = []
    for b in range(_cores):
        r = rows.copy()
        r[320:328] = c[b].reshape(8, 128)
        m = dict(shared)
        m.update(x=x[b], ctx=ctx[b], rows=np.ascontiguousarray(r.reshape(3, 128, 128)))
        in_maps.append(m)
    res = run_bass_kernel_spmd(nc, in_maps, core_ids=list(range(_cores)))
    return np.stack([np.asarray(res.results[b]["y"], dtype=np.float32) for b in range(_cores)], axis=0)
```

```python
import math
import os
import numpy as np
import concourse.bass as bass
import concourse.mybir as mybir
from concourse.bass_utils import run_bass_kernel_spmd

F32 = mybir.dt.float32
BF16 = mybir.dt.bfloat16
AF = mybir.ActivationFunctionType
ALU = mybir.AluOpType
AX = mybir.AxisListType
DSZ = {F32: 4, BF16: 2}

D = 1024
S_LAT = 2048
S_CTX = 256
NT = S_LAT + S_CTX
NTT = NT // 128
D_IN = 8352
EPS = 1e-6
OFF_Q, OFF_K, OFF_V, OFF_CQ, OFF_CKV, OFF_KR, OFF_U, OFF_PB, OFF_PC, OFF_PX, OFF_G = (
    0, 512, 1024, 1536, 1920, 2176, 2208, 2720, 3232, 3744, 4256)
TB_ALL = [(0, 256), (256, 512), (768, 512), (1280, 512), (1792, 512)]
PADL = 2368


def pcol(t):
    return t + 16 if t < 256 else t + 48


class Sched:
    ENG = ("tensor", "vector", "scalar", "gpsimd", "sync")

    def __init__(self, nc, ndma=12):
        self.nc = nc
        self.ops = {e: [] for e in self.ENG}
        self.track = {}
        self.ndma = ndma
        self.dma_n = {"sync": 0, "gpsimd": 0}
        self.dma_cum = {}

    @staticmethod
    def region(ap):
        if isinstance(ap, tuple):
            return ap
        dims = ap.ap
        dsz = DSZ[ap.dtype]
        row = dims[0][0]
        off = ap.offset
        sp = str(ap.space)
        if "SB" in sp or "PSUM" in sp:
            p0 = off // row
            f0 = off % row
            ext = 0
            for s, c in dims[1:]:
                ext += (c - 1) * abs(s)
            return (ap.tensor.name, p0, p0 + dims[0][1], f0 * dsz, (f0 + ext + 1) * dsz)
        ext = 0
        for s, c in dims:
            ext += (c - 1) * abs(s)
        return (ap.tensor.name, 0, 1, off * dsz, (off + ext + 1) * dsz)

    def _conf(self, r, write, deps):
        for e in self.track.get(r[0], ()):
            if (write or e[5]) and e[0] < r[2] and r[1] < e[1] and e[2] < r[4] and r[3] < e[3]:
                deps.add(e[4])

    def _record(self, r, prod, write):
        lst = self.track.setdefault(r[0], [])
        if write:
            lst[:] = [e for e in lst if not (r[1] <= e[0] and e[1] <= r[2] and r[3] <= e[2] and e[3] <= r[4])]
        else:
            lst[:] = [e for e in lst if not ((not e[5]) and e[4][0] == prod[0] and e[0] == r[1] and e[1] == r[2]
                                             and e[2] == r[3] and e[3] == r[4])]
        lst.append((r[1], r[2], r[3], r[4], prod, write))

    def add(self, eng, fn, reads, writes, dma=False):
        reads = [self.region(a) for a in reads]
        writes = [self.region(a) for a in writes]
        deps = set()
        for r in reads:
            self._conf(r, False, deps)
        for w in writes:
            self._conf(w, True, deps)
        if dma:
            k = self.dma_n[eng] % self.ndma
            self.dma_n[eng] += 1
            name = "d_%s_%d" % (eng, k)
            self.dma_cum[name] = self.dma_cum.get(name, 0) + 1
            prod = (name, self.dma_cum[name])
            if prod[1] > 1:
                deps.add((name, prod[1] - 1))
        else:
            prod = (eng, len(self.ops[eng]) + 1)
        self.ops[eng].append([fn, deps, prod, dma, False])
        for r in reads:
            self._record(r, prod, False)
        for w in writes:
            self._record(w, prod, True)
        return prod

    def emit(self):
        nc = self.nc
        needed = set()
        for e in self.ENG:
            for op in self.ops[e]:
                for d in op[1]:
                    if not (d[0] == "tensor" and e == "tensor"):
                        needed.add(d)
        ms = {}
        for e in self.ENG:
            cnt = 0
            for i, op in enumerate(self.ops[e]):
                if (not op[3]) and (e, i + 1) in needed:
                    cnt += 1
                    op[4] = True
                    ms[(e, i + 1)] = cnt
        names = list(self.ENG) + sorted(self.dma_cum.keys())
        import contextlib
        with contextlib.ExitStack() as st:
            sems = {n: st.enter_context(nc.semaphore("s_" + n)) for n in names}
            block = st.enter_context(nc.Block())

            def run(e, eng):
                waited = {}
                for fn, deps, prod, dma, inc in self.ops[e]:
                    need = {}
                    for (p, s) in deps:
                        if p == "tensor" and e == "tensor":
                            continue
                        v = ms[(p, s)] if p in self.ENG else 16 * s
                        if v > need.get(p, 0):
                            need[p] = v
                    for p, v in need.items():
                        if waited.get(p, 0) < v:
                            eng.wait_ge(sems[p], v)
                            waited[p] = v
                    ins = fn(eng)
                    if dma:
                        ins.then_inc(sems[prod[0]], 16)
                    elif inc:
                        ins.then_inc(sems[e], 1)

            @block.tensor
            def _(eng):
                run("tensor", eng)

            @block.vector
            def _(eng):
                run("vector", eng)

            @block.scalar
            def _(eng):
                run("scalar", eng)

            @block.gpsimd
            def _(eng):
                run("gpsimd", eng)

            @block.sync
            def _(eng):
                run("sync", eng)


def build(depth, dbg=False):
    nc = bass.Bass("TRN2", target_bir_lowering=False)
    L = depth

    def din(name, shape):
        return nc.dram_tensor(name, list(shape), F32, kind="ExternalInput").ap()

    x_d = din("x", [S_LAT, D])
    ctx_d = din("ctx", [S_CTX, D])
    rows_d = din("rows", [3, 128, 128])
    w_mod = din("w_mod", [L, D, 6 * D])
    w_in = din("w_in", [L, D, D_IN])
    gains_d = din("gains", [L, 1344])
    w_uq = din("w_uq", [L, 384, 768])
    w_ukv = din("w_ukv", [L, 256, 1024])
    w_pool = din("w_pool", [L, 4, 128, 128])
    w_branch = din("w_branch", [L, 4, 512, D])
    w_o = din("w_o", [L, D, D])
    w_ff1 = din("w_ff1", [L, D, 4 * D])
    w_ff2 = din("w_ff2", [L, 4 * D, D])
    identf_d = din("identf", [128, 128])
    ropeA_d = din("ropeA", [128, 2 * 16 * 64])
    ropeB_d = din("ropeB", [128, 2 * 16 * 32])
    poolfix_d = din("poolfix", [128, 128])
    y_d = nc.dram_tensor("y", [S_LAT, D], F32, kind="ExternalOutput").ap()
    xs_d = nc.dram_tensor("xs", [8, 128, NT], F32, kind="Internal").ap()

    S = Sched(nc)
    import contextlib
    es = contextlib.ExitStack()

    def sb(name, shape, dt):
        return es.enter_context(nc.sbuf_tensor(name, list(shape), dt))

    hT = sb("hT", [128, 8 * NT], BF16)
    U = sb("U", [128, 47104], BF16)
    ring = [sb("w%d" % i, [128, 4096], BF16) for i in range(3)]
    xblk = [sb("xb%d" % i, [128, 1024], F32) for i in range(2)]
    PT = [sb("pt%d" % i, [128, 512], BF16) for i in range(3)]
    tmpf = [sb("tf%d" % i, [128, 512], F32) for i in range(4)]
    tmpb = [sb("tb%d" % i, [128, 512], BF16) for i in range(3)]
    rstd_bc = sb("rstd_bc", [128, 512], F32)
    oA = sb("oA", [128, 512], F32)
    small = sb("small", [128, 64], F32)
    small2 = sb("small2", [128, 64], F32)
    modraw = [sb("modraw0", [128, 96], F32), sb("modraw1", [128, 96], F32)]
    smalls = [small, small2]
    cols = sb("cols", [128, 384], F32)
    identf = sb("identf_s", [128, 128], F32)
    identb = sb("identb", [128, 128], BF16)
    onesb = sb("onesb", [128, 128], BF16)
    ropeA = sb("ropeA_s", [128, 2 * 16 * 64], F32)
    ropeB = sb("ropeB_s", [128, 2 * 16 * 32], F32)
    poolfix = sb("poolfix_s", [128, 128], F32)
    gains = sb("gains_s", [128, 1344], F32)
    modS = sb("modS", [128, 96], F32)
    gsS = sb("gsS", [128, 32], F32)
    sc = sb("sc", [128, 16], BF16)
    lamt = sb("lamt", [128, 8], F32)
    gsub = sb("gsub", [128, 128], F32)
    ybtok = sb("ybtok", [128, NTT * 128], BF16)
    PS = [es.enter_context(nc.psum_tensor("ps%d" % i, [128, 512], F32)) for i in range(8)]

    hT3 = hT[:].rearrange("p (k t) -> p k t", k=8)
    xT3 = U[:, 0:36864].bitcast(F32).rearrange("p (k t) -> p k t", k=8)
    aT3 = U[:, 36864:46080].rearrange("p (k t) -> p k t", k=4)
    yT3 = U[:, 0:9216].rearrange("p (k t) -> p k t", k=4)
    mix3 = U[:, 9216:27648].rearrange("p (k t) -> p k t", k=8)
    QO = 27648

    def Q(a, b):
        return U[:, QO + a:QO + b]

    ring_i = [0]

    def slot():
        r = ring[ring_i[0] % 3]
        ring_i[0] += 1
        return r

    def wload(dst, src):
        S.add("gpsimd", lambda e, d=dst, s=src: e.dma_start(out=d, in_=s), [src], [dst], dma=True)

    def dma(dst, src, rd=None, wr=None):
        S.add("sync", lambda e, d=dst, s=src: e.dma_start(out=d, in_=s),
              [src] if rd is None else rd, [dst] if wr is None else wr, dma=True)

    def bankreg(out):
        r = S.region(out)
        return (r[0], r[1], r[2], 0, 2048)

    def mm(out, lhsT, rhs, start=True, stop=True, skip=False):
        S.add("tensor", lambda e: e.matmul(out, lhsT, rhs, start=start, stop=stop, skip_group_check=skip),
              [lhsT, rhs], [bankreg(out)])

    def tr(out, in_, ident):
        S.add("tensor", lambda e: e.transpose(out, in_, ident), [in_, ident], [bankreg(out)])

    def act(out, in_, func, bias=None, scale=None):
        rd = [in_]
        kw = {}
        if bias is not None:
            kw["bias"] = bias
            if not isinstance(bias, (int, float)):
                rd.append(bias)
        if scale is not None:
            kw["scale"] = scale
            if not isinstance(scale, (int, float)):
                rd.append(scale)
        S.add("scalar", lambda e: e.activation(out=out, in_=in_, func=func, **kw), rd, [out])

    def tt(out, in0, in1, op, eng="vector"):
        S.add(eng, lambda e: e.tensor_tensor(out=out, in0=in0, in1=in1, op=op), [in0, in1], [out])

    def ts(out, in0, s1, op0, s2=None, op1=None, eng="vector"):
        rd = [in0] + [s for s in (s1, s2) if s is not None and not isinstance(s, (int, float))]
        if op1 is None:
            S.add(eng, lambda e: e.tensor_scalar(out=out, in0=in0, scalar1=s1, scalar2=None, op0=op0), rd, [out])
        else:
            S.add(eng, lambda e: e.tensor_scalar(out=out, in0=in0, scalar1=s1, scalar2=s2, op0=op0, op1=op1), rd, [out])

    def stt(out, in0, scalar, in1, op0, op1):
        rd = [in0, in1] + ([] if isinstance(scalar, (int, float)) else [scalar])
        S.add("vector", lambda e: e.scalar_tensor_tensor(out=out, in0=in0, scalar=scalar, in1=in1, op0=op0, op1=op1),
              rd, [out])

    def cp(out, in_, eng="vector"):
        if eng == "scalar":
            S.add("scalar", lambda e: e.copy(out=out, in_=in_), [in_], [out])
        else:
            S.add(eng, lambda e: e.tensor_copy(out=out, in_=in_), [in_], [out])

    def red(out, in_):
        S.add("vector", lambda e: e.tensor_reduce(out=out, in_=in_, axis=AX.X, op=ALU.add), [in_], [out])

    def recip(out, in_):
        S.add("vector", lambda e: e.reciprocal(out=out, in_=in_), [in_], [out])

    def mset(ap, v, eng="vector"):
        S.add(eng, lambda e: e.memset(ap, v), [], [ap])

    def interleave(gens):
        gens = list(gens)
        while gens:
            for g in list(gens):
                try:
                    next(g)
                except StopIteration:
                    gens.remove(g)

    bank_i = [0]
    bank_list = [list(range(8))]

    def bank():
        b = bank_list[0][bank_i[0] % len(bank_list[0])]
        bank_i[0] += 1
        return PS[b]

    rot = {}

    def rr(lst, key):
        i = rot.get(key, 0)
        rot[key] = i + 1
        return lst[i % len(lst)]

    dma(identf[:], identf_d)
    dma(ropeA[:], ropeA_d)
    dma(ropeB[:], ropeB_d)
    dma(poolfix[:], poolfix_d)
    cp(identb[:], identf[:])
    mset(onesb[:], 1.0)
    for r in range(3):
        xb = xblk[r % 2]
        dma(xb[:, 0:128], rows_d[r])
        pb = bank()
        tr(pb[:, 0:128], xb[:, 0:128], identf[:])
        cp(cols[:, r * 128:(r + 1) * 128], pb[:, 0:128])
    sc3 = sc[:].rearrange("p (k s) -> p k s", s=2)
    act(sc3[:, :, 0], cols[:, 320:328], AF.Silu)
    act(sc3[:, :, 1], cols[:, 328:336], AF.Silu)

    for tt_ in range(NTT):
        xb = rr(xblk, "xin")
        src = ctx_d[tt_ * 128:(tt_ + 1) * 128, :] if tt_ < 2 else x_d[(tt_ - 2) * 128:(tt_ - 1) * 128, :]
        dma(xb[:], src)
        for half in range(2):
            pb = bank()
            for j in range(4):
                k = half * 4 + j
                tr(pb[:, j * 128:(j + 1) * 128], xb[:, k * 128:(k + 1) * 128], identf[:])
            cp(xT3[:, half * 4:half * 4 + 4, tt_ * 128:(tt_ + 1) * 128],
               pb[:].rearrange("p (k t) -> p k t", k=4), eng=("vector" if half == 0 else "scalar"))

    def w_cols(W2d, c0, n, dst, kch=8):
        wload(dst, W2d[:, c0:c0 + n].rearrange("(k p) c -> p k c", p=128))

    def norm_and_h(l, ni, tbs):
        for (t0, n) in tbs:
            st = 1 if t0 == 0 else 0
            pb = bank()
            for k in range(8):
                sq = rr(tmpb, "sq")
                act(sq[:, 0:n], xT3[:, k, t0:t0 + n], AF.Square)
                mm(pb[:, 0:n], onesb[:], sq[:, 0:n], start=(k == 0), stop=(k == 7))
            t = rr(tmpf, "tf")
            act(t[:, 0:n], pb[:, 0:n], AF.Sqrt, bias=EPS, scale=1.0 / D)
            recip(rstd_bc[:, 0:n], t[:, 0:n])
            for k in range(8):
                t = rr(tmpf, "tf")
                tt(t[:, 0:n], xT3[:, k, t0:t0 + n], rstd_bc[:, 0:n], ALU.mult)
                gi = ni * 16 + k * 2 + st
                mi = ((0 if ni == 0 else 24) + k) * 2 + st
                act(hT3[:, k, t0:t0 + n], t[:, 0:n], AF.Identity, bias=modS[:, mi:mi + 1], scale=gsS[:, gi:gi + 1])

    def fpass(wslot3, ncols_chunks, rhs3, kch, tbs, evac):
        for j in ncols_chunks:
            for (t0, n) in tbs:
                pb = bank()
                for k in range(kch):
                    mm(pb[:, 0:n], wslot3[:, k, j * 128:(j + 1) * 128], rhs3[:, k, t0:t0 + n],
                       start=(k == 0), stop=(k == kch - 1))
                evac(j, t0, n, pb)

    for l in range(L):
        last = (l == L - 1)
        lam_init = 0.8 - 0.6 * math.exp(-0.3 * l)
        TBX = TB_ALL[1:] if last else TB_ALL
        W_in = w_in[l]

        dma(gains[:], gains_d[l:l + 1, :].to_broadcast([128, 1344]))
        G_QA, G_KA, G_SUB, G_CQ, G_CKV, G_QB, G_KB, G_LAM = 0, 64, 128, 256, 640, 896, 992, 1088
        t = rr(tmpf, "tf")
        tt(t[:, 0:64], gains[:, G_LAM:G_LAM + 64], gains[:, G_LAM + 64:G_LAM + 128], ALU.mult)
        tt(t[:, 64:128], gains[:, G_LAM + 128:G_LAM + 192], gains[:, G_LAM + 192:G_LAM + 256], ALU.mult)
        red(lamt[:, 0:2], t[:, 0:128].rearrange("p (a b) -> p a b", a=2))
        act(lamt[:, 2:4], lamt[:, 0:2], AF.Exp)
        tt(lamt[:, 5:6], lamt[:, 3:4], lamt[:, 2:3], ALU.subtract)
        ts(lamt[:, 4:5], lamt[:, 5:6], -lam_init, ALU.add)
        ts(gsub[:], gains[:, G_SUB:G_SUB + 128], 1.0 - lam_init, ALU.mult)

        def mod_load(l2, s_):
            wsl = slot()
            w3 = wsl[:].rearrange("p (k c) -> p k c", k=8)
            w_cols(w_mod[l2], s_ * 512, 512, w3)
            return w3

        def mod_mm(l2, s_, w3):
            pbm = bank()
            for m in range(4):
                for k in range(8):
                    mm(pbm[:, m * 2:m * 2 + 2], w3[:, k, m * 128:(m + 1) * 128], sc3[:, k, :],
                       start=(k == 0), stop=(k == 7))
            cp(modraw[l2 % 2][:, s_ * 8:(s_ + 1) * 8], pbm[:, 0:8])
        if l == 0:
            for s_ in range(12):
                mod_mm(0, s_, mod_load(0, s_))
        mps = modraw[l % 2]
        mod3 = modS[:].rearrange("p (m s) -> p m s", s=2)
        tt(mod3, mps[:, 0:96].rearrange("p (m s) -> p m s", s=2),
           cols[:, l * 48:(l + 1) * 48].unsqueeze(2).to_broadcast([128, 48, 2]), ALU.add)
        gs4 = gsS[:].rearrange("p (n k s) -> p n k s", n=2, s=2)
        for ni in range(2):
            so = 8 if ni == 0 else 32
            gcol = (192 if ni == 0 else 224) + l * 8
            ts(gs4[:, ni], mod3[:, so:so + 8, :], 1.0, ALU.add)
            tt(gs4[:, ni], gs4[:, ni], cols[:, gcol:gcol + 8].unsqueeze(2).to_broadcast([128, 8, 2]), ALU.mult)

        norm_and_h(l, 0, TB_ALL)
        for k in range(8):
            dma(xs_d[k], xT3[:, k, :], wr=[("xs", k, k + 1, 0, NT * 4)])

        bank_list[0] = [0]
        sc_banks = [PS[1], PS[2], PS[3]]
        acc_sets = [(PS[4], PS[5]), (PS[6], PS[7])]

        units = []

        def attention(*u):
            units.append(u)

        def run_units():
            steps = [(ui, ki) for ui, u in enumerate(units) for ki in range(len(u[7]))]

            def score(i):
                ui, ki = steps[i]
                qT, kT, vfn, dv, scale, q0, nq, kts, fin = units[ui]
                sb_ = rr(sc_banks, "sc")
                kt = kts[ki]
                mm(sb_[:, 0:nq], kT[:, kt * 128:(kt + 1) * 128], qT[:, q0:q0 + nq])
                return sb_
            accs = None
            issued = []

            def ensure(n_):
                while len(issued) < min(n_, len(steps)):
                    issued.append(score(len(issued)))
            ensure(2)
            for i, (ui, ki) in enumerate(steps):
                qT, kT, vfn, dv, scale, q0, nq, kts, fin = units[ui]
                ensure(i + 3)
                cur = issued[i]
                nqt = nq // 128
                per_bank = 2 if dv == 128 else 4
                if ki == 0:
                    accs = rr(acc_sets, "acc")
                kt = kts[ki]
                p = rr(PT, "pt")
                act(p[:, 0:nq], cur[:, 0:nq], AF.Exp, scale=scale)
                for qt in range(nqt):
                    bk = accs[qt // per_bank]
                    o = (qt % per_bank) * (dv + 1)
                    first = (ki == 0 and qt % per_bank == 0)
                    mm(bk[:, o:o + dv + 1], p[:, qt * 128:(qt + 1) * 128], vfn(kt),
                       start=first, stop=(ki == len(kts) - 1), skip=True)
                if ki == len(kts) - 1:
                    fin(accs, nqt, per_bank)
            del units[:]

        KA3 = Q(0, 4608).rearrange("p (c t) -> p c t", c=2)
        QP4 = Q(4608, 13824).rearrange("p (h m t) -> p h m t", h=2, m=2)
        VA4 = Q(13824, 13824 + NTT * 258).rearrange("p (t h c) -> p t h c", t=NTT, h=2)
        rA = ropeA[:].rearrange("p (s t c) -> p s t c", s=2, t=16)
        qk_unit_done = [0]
        for hp in range(2):
            s1 = slot()
            s2 = slot()
            w1 = s1[:].rearrange("p (k c) -> p k c", k=8)
            w2 = s2[:, 0:2048].rearrange("p (k c) -> p k c", k=8)
            w_cols(W_in, OFF_Q + hp * 256, 256, w1[:, :, 0:256])
            w_cols(W_in, OFF_K + hp * 256, 256, w1[:, :, 256:512])
            w_cols(W_in, OFF_V + hp * 256, 256, w2)
            for hh_ in range(2):
                mset(VA4[:, :, hh_, 128:129], 1.0)
                mset(QP4[64:128, hh_, 0, :], 0.0)
                mset(QP4[0:64, hh_, 1, :], 0.0)
            def a_mm(tt_):
                tk = slice(tt_ * 128, (tt_ + 1) * 128)
                p1 = PS[(tt_ % 3) * 2]
                for k in range(8):
                    mm(p1[:], hT3[:, k, tk], w1[:, k, :], start=(k == 0), stop=(k == 7))
                p2 = PS[(tt_ % 3) * 2 + 1]
                for k in range(8):
                    mm(p2[:, 0:256], hT3[:, k, tk], w2[:, k, :], start=(k == 0), stop=(k == 7))
                return p1, p2

            def a_chain(tt_, p1, p2, par):
                tk = slice(tt_ * 128, (tt_ + 1) * 128)
                sm = smalls[par]
                sq = tmpf[2 * par]
                qn = tmpf[2 * par + 1]
                t1 = xblk[par][:, 0:512]
                t2 = xblk[par][:, 512:1024]
                qkb = tmpb[par]
                act(sq[:], p1[:], AF.Square)
                cp(VA4[:, tt_, :, 0:128], p2[:, 0:256].rearrange("p (h c) -> p h c", h=2), eng="scalar")
                yield
                red(sm[:, 0:8], sq[:].rearrange("p (g c) -> p g c", g=8))
                yield
                act(sm[:, 8:16], sm[:, 0:8], AF.Sqrt, bias=EPS, scale=1.0 / 64)
                yield
                recip(sm[:, 16:24], sm[:, 8:16])
                yield
                qn3 = qn[:].rearrange("p (g c) -> p g c", g=8)
                tt(qn3, p1[:].rearrange("p (g c) -> p g c", g=8),
                   sm[:, 16:24].unsqueeze(2).to_broadcast([128, 8, 64]), ALU.mult)
                yield
                qn4 = qn[:].rearrange("p (a g c) -> p a g c", a=2, g=4)
                tt(qn4, qn4, gains[:, G_QA:G_QA + 128].rearrange("p (a c) -> p a c", a=2).unsqueeze(2)
                   .to_broadcast([128, 2, 4, 64]), ALU.mult)
                yield
                if tt_ >= 2:
                    cc = rA[:, 0, tt_ - 2, :].unsqueeze(1).to_broadcast([128, 8, 64])
                    ss = rA[:, 1, tt_ - 2, :].unsqueeze(1).to_broadcast([128, 8, 64])
                    t13 = t1.rearrange("p (g c) -> p g c", g=8)
                    t23 = t2.rearrange("p (g c) -> p g c", g=8)
                    tt(t13, qn3, cc, ALU.mult)
                    tt(t23, qn3, ss, ALU.mult, eng="gpsimd")
                    yield
                    qkb3 = qkb[:].rearrange("p (g c) -> p g c", g=8)
                    tt(qkb3[:, :, 0:32], t13[:, :, 0:32], t23[:, :, 32:64], ALU.subtract)
                    tt(qkb3[:, :, 32:64], t23[:, :, 0:32], t13[:, :, 32:64], ALU.add, eng="gpsimd")
                    yield
                else:
                    cp(qkb[:], qn[:])
                    yield
                pbk = PS[6 + par]
                pbb = pbk[:].bitcast(BF16)
                for j in range(4):
                    tr(pbb[:, j * 128:(j + 1) * 128], qkb[:, j * 128:(j + 1) * 128], identb[:])
                yield
                pb4 = pbb[:, 0:512].rearrange("p (c t) -> p c t", c=4)
                cp(KA3[:, :, tk], pb4[:, 2:4, :], eng="scalar")
                cp(QP4[0:64, :, 0, tk], pb4[0:64, 0:2, :], eng="scalar")
                cp(QP4[64:128, :, 1, tk], pb4[64:128, 0:2, :], eng="scalar")

            mmd = {0: a_mm(0), 1: a_mm(1)}
            for pi in range(NTT // 2):
                ta, tb_ = 2 * pi, 2 * pi + 1
                if ta + 2 < NTT:
                    mmd[ta + 2] = a_mm(ta + 2)
                interleave([a_chain(ta, mmd[ta][0], mmd[ta][1], 0), a_chain(tb_, mmd[tb_][0], mmd[tb_][1], 1)])
                if tb_ + 2 < NTT:
                    mmd[tb_ + 2] = a_mm(tb_ + 2)
            bank_list[0] = [0]
            for hh in range(2):
                h = hp * 2 + hh
                qblocks = [(256 + i * 512, 512, list(range(NTT))) for i in range(4)]
                if not last:
                    qblocks.append((0, 256, [0, 1]))
                for (q0, nq, kts) in qblocks:
                    def fin0(accs, nqt, per_bank, nq=nq):
                        for b in range((nqt + 1) // 2):
                            a3 = accs[b][:, 0:258].rearrange("p (t c) -> p t c", t=2)
                            recip(small[:, 24 + 2 * b:26 + 2 * b], a3[:, :, 128])
                            tt(oA[:, b * 256:(b + 1) * 256].rearrange("p (t c) -> p t c", t=2), a3[:, :, 0:128],
                               small[:, 24 + 2 * b:26 + 2 * b].unsqueeze(2).to_broadcast([128, 2, 128]), ALU.mult)

                    def fin1(accs, nqt, per_bank, nq=nq, q0=q0, h=h):
                        o = rr(tmpf, "tf")
                        for b in range((nqt + 1) // 2):
                            a3 = accs[b][:, 0:258].rearrange("p (t c) -> p t c", t=2)
                            recip(small[:, 28 + 2 * b:30 + 2 * b], a3[:, :, 128])
                            ts(small[:, 32 + 2 * b:34 + 2 * b], small[:, 28 + 2 * b:30 + 2 * b], lamt[:, 4:5], ALU.mult)
                            o3 = o[:, b * 256:(b + 1) * 256].rearrange("p (t c) -> p t c", t=2)
                            tt(o3, a3[:, :, 0:128],
                               small[:, 32 + 2 * b:34 + 2 * b].unsqueeze(2).to_broadcast([128, 2, 128]), ALU.mult)
                        tt(o[:, 0:nq], o[:, 0:nq], oA[:, 0:nq], ALU.add)
                        sq = rr(tmpf, "tf")
                        act(sq[:, 0:nq], o[:, 0:nq], AF.Square)
                        red(small[:, 36:36 + nqt], sq[:, 0:nq].rearrange("p (t c) -> p t c", c=128))
                        act(small[:, 40:40 + nqt], small[:, 36:36 + nqt], AF.Sqrt, bias=EPS, scale=1.0 / 128)
                        recip(small[:, 44:44 + nqt], small[:, 40:40 + nqt])
                        o3 = o[:, 0:nq].rearrange("p (t c) -> p t c", c=128)
                        tt(o3, o3, small[:, 44:44 + nqt].unsqueeze(2).to_broadcast([128, nqt, 128]), ALU.mult)
                        yb = rr(tmpb, "qkb")
                        tt(yb[:, 0:nq].rearrange("p (t c) -> p t c", c=128), o3,
                           gsub[:].unsqueeze(1).to_broadcast([128, nqt, 128]), ALU.mult)
                        pbk = bank()
                        pbb = pbk[:].bitcast(BF16)
                        for qt in range(nqt):
                            tr(pbb[:, qt * 128:(qt + 1) * 128], yb[:, qt * 128:(qt + 1) * 128], identb[:])
                        cp(yT3[:, h, q0:q0 + nq], pbb[:, 0:nq], eng="scalar")

                    for m in range(2):
                        attention(QP4[:, hh, m, :], KA3[:, hh, :],
                                  lambda kt, hh=hh: VA4[:, kt, hh, :], 128, 0.125, q0, nq, kts,
                                  fin0 if m == 0 else fin1)
            run_units()

        merged = [0]

        def merge(n):
            bank_list[0] = list(range(8))
            g1 = slot()
            g2 = slot()
            wbs = slot()
            gw = [g1[:].rearrange("p (k c) -> p k c", k=8), g2[:].rearrange("p (k c) -> p k c", k=8)]
            w_cols(W_in, OFF_G + n * 1024, 512, gw[0])
            w_cols(W_in, OFF_G + n * 1024 + 512, 512, gw[1])
            wb3 = wbs[:].rearrange("p (k c) -> p k c", k=4)
            wload(wb3, w_branch[l, n].rearrange("(k p) c -> p k c", p=128))
            first = (merged[0] == 0)
            merged[0] += 1
            for j in range(8):
                for (t0, n_) in TBX:
                    pg = bank()
                    for k in range(8):
                        mm(pg[:, 0:n_], gw[j // 4][:, k, (j % 4) * 128:(j % 4 + 1) * 128], hT3[:, k, t0:t0 + n_],
                           start=(k == 0), stop=(k == 7))
                    pp = bank()
                    for k in range(4):
                        mm(pp[:, 0:n_], wb3[:, k, j * 128:(j + 1) * 128], yT3[:, k, t0:t0 + n_],
                           start=(k == 0), stop=(k == 3))
                    sg = rr(tmpf, "tf")
                    act(sg[:, 0:n_], pg[:, 0:n_], AF.Sigmoid)
                    if first:
                        tt(mix3[:, j, t0:t0 + n_], sg[:, 0:n_], pp[:, 0:n_], ALU.mult)
                    else:
                        tt(sg[:, 0:n_], sg[:, 0:n_], pp[:, 0:n_], ALU.mult)
                        tt(mix3[:, j, t0:t0 + n_], sg[:, 0:n_], mix3[:, j, t0:t0 + n_], ALU.add)

        merge(0)

        bank_list[0] = [0, 6, 7]
        acc_sets = [(PS[4],), (PS[5],)]
        CT3 = Q(0, 11520).rearrange("p (c t) -> p c t", c=5)
        QB3 = Q(11520, 16128).rearrange("p (c t) -> p c t", c=2)
        VB3 = Q(16128, 16128 + NTT * 65).rearrange("p (t c) -> p t c", c=65)
        KR3 = Q(17298, 17298 + NTT * 32).rearrange("p (t c) -> p t c", c=32)
        rB = ropeB[:].rearrange("p (s t c) -> p s t c", s=2, t=16)
        s1 = slot()
        s2 = slot()
        w1 = s1[:].rearrange("p (k c) -> p k c", k=8)
        w2 = s2[:, 0:1280].rearrange("p (k c) -> p k c", k=8)
        w_cols(W_in, OFF_CQ, 512, w1)
        w_cols(W_in, OFF_CQ + 512, 160, w2)
        bank_list[0] = list(range(8))

        def b1_mm(tt_):
            tk = slice(tt_ * 128, (tt_ + 1) * 128)
            p1 = bank()
            for k in range(8):
                mm(p1[:], hT3[:, k, tk], w1[:, k, :], start=(k == 0), stop=(k == 7))
            p2 = bank()
            for k in range(8):
                mm(p2[:, 0:160], hT3[:, k, tk], w2[:, k, :], start=(k == 0), stop=(k == 7))
            return p1, p2
        nxt_ = b1_mm(0)
        for tt_ in range(NTT):
            tk = slice(tt_ * 128, (tt_ + 1) * 128)
            p1, p2 = nxt_
            if os.environ.get('NOPIPE_B1') is None:
                nxt_ = b1_mm(tt_ + 1) if tt_ + 1 < NTT else None
            sq = rr(tmpf, "tf")
            sq2 = rr(tmpf, "tf")
            act(sq[:], p1[:], AF.Square)
            act(sq2[:, 0:160], p2[:, 0:160], AF.Square)
            red(small[:, 0:1], sq[:, 0:384])
            red(small[:, 4:5], sq[:, 384:512])
            red(small[:, 5:6], sq2[:, 0:128])
            tt(small[:, 1:2], small[:, 4:5], small[:, 5:6], ALU.add)
            red(small[:, 2:3], sq2[:, 128:160])
            tt(small[:, 8:11], small[:, 0:3], poolfix[:, 32:35], ALU.mult)
            act(small[:, 12:15], small[:, 8:11], AF.Sqrt, bias=EPS)
            recip(small[:, 16:19], small[:, 12:15])
            cb = rr(tmpb, "qkb")
            stt(cb[:, 0:384], p1[:, 0:384], small[:, 16:17], gains[:, G_CQ:G_CQ + 384], ALU.mult, ALU.mult)
            stt(cb[:, 384:512], p1[:, 384:512], small[:, 17:18], gains[:, G_CKV:G_CKV + 128], ALU.mult, ALU.mult)
            cb2 = rr(tmpb, "qkb")
            stt(cb2[:, 0:128], p2[:, 0:128], small[:, 17:18], gains[:, G_CKV + 128:G_CKV + 256], ALU.mult, ALU.mult)
            if tt_ >= 2:
                kr = rr(tmpf, "tf")
                stt(kr[:, 0:32], p2[:, 128:160], small[:, 18:19], gains[:, G_KB + 64:G_KB + 96], ALU.mult, ALU.mult)
                tt(kr[:, 32:64], kr[:, 0:32], rB[:, 0, tt_ - 2, :], ALU.mult)
                tt(kr[:, 64:96], kr[:, 0:32], rB[:, 1, tt_ - 2, :], ALU.mult)
                tt(KR3[:, tt_, 0:16], kr[:, 32:48], kr[:, 80:96], ALU.subtract)
                tt(KR3[:, tt_, 16:32], kr[:, 64:80], kr[:, 48:64], ALU.add)
            else:
                stt(KR3[:, tt_, :], p2[:, 128:160], small[:, 18:19], gains[:, G_KB + 64:G_KB + 96], ALU.mult, ALU.mult)
            pbk = bank()
            pbb = pbk[:].bitcast(BF16)
            for j in range(4):
                tr(pbb[:, j * 128:(j + 1) * 128], cb[:, j * 128:(j + 1) * 128], identb[:])
            tr(pbb[:, 512:640], cb2[:, 0:128], identb[:])
            cp(CT3[:, :, tk], pbb[:, 0:640].rearrange("p (c t) -> p c t", c=5), eng="scalar")
            if os.environ.get('NOPIPE_B1') is not None:
                nxt_ = b1_mm(tt_ + 1) if tt_ + 1 < NTT else None
        bank_list[0] = [0, 6, 7]
        mset(QB3[64:128, :, :], 0.0)
        su = slot()
        sk = slot()
        wuq3 = su[:, 0:2304].rearrange("p (k c) -> p k c", k=3)
        wukv3 = sk[:, 0:2048].rearrange("p (k c) -> p k c", k=2)
        wload(wuq3, w_uq[l].rearrange("(k p) c -> p k c", p=128))
        wload(wukv3, w_ukv[l].rearrange("(k p) c -> p k c", p=128))
        yb4 = ybtok[:].rearrange("p (t h c) -> p t h c", t=NTT, h=2)
        for h in range(8):
            mset(VB3[:, :, 64:65], 1.0)
            def b2_mm(tt_, h=h):
                tk = slice(tt_ * 128, (tt_ + 1) * 128)
                pq = bank()
                for k in range(2):
                    mm(pq[:, 0:128], CT3[:, 3 + k, tk], wukv3[:, k, h * 128:(h + 1) * 128], start=(k == 0), stop=(k == 1))
                for k in range(3):
                    mm(pq[:, 128:224], CT3[:, k, tk], wuq3[:, k, h * 96:(h + 1) * 96], start=(k == 0), stop=(k == 2),
                       skip=True)
                return pq
            def b2_chain(tt_, pq, par, h=h):
                tk = slice(tt_ * 128, (tt_ + 1) * 128)
                sm = smalls[par]
                sq = tmpf[par]
                kr = tmpf[2 + par]
                qk = tmpb[par]
                act(sq[:, 0:224], pq[:, 0:224], AF.Square)
                yield
                red(sm[:, 0:7], sq[:, 0:224].rearrange("p (g c) -> p g c", c=32))
                yield
                r8 = sm[:, 0:8].rearrange("p (a b) -> p a b", b=4)
                tt(sm[:, 8:10], r8[:, :, 0], r8[:, :, 1], ALU.add)
                ts(sm[:, 10:11], sm[:, 6:7], 2.0, ALU.mult)
                cp(VB3[:, tt_, 0:64], pq[:, 64:128], eng="scalar")
                yield
                act(sm[:, 12:15], sm[:, 8:11], AF.Sqrt, bias=EPS, scale=1.0 / 64)
                yield
                recip(sm[:, 16:19], sm[:, 12:15])
                yield
                stt(qk[:, 0:64], pq[:, 128:192], sm[:, 17:18], gains[:, G_QB:G_QB + 64], ALU.mult, ALU.mult)
                stt(qk[:, 96:160], pq[:, 0:64], sm[:, 16:17], gains[:, G_KB:G_KB + 64], ALU.mult, ALU.mult)
                if tt_ >= 2:
                    stt(kr[:, 0:32], pq[:, 192:224], sm[:, 18:19], gains[:, G_QB + 64:G_QB + 96], ALU.mult, ALU.mult)
                else:
                    stt(qk[:, 64:96], pq[:, 192:224], sm[:, 18:19], gains[:, G_QB + 64:G_QB + 96], ALU.mult, ALU.mult)
                cp(qk[:, 160:192], KR3[:, tt_, :], eng="gpsimd")
                yield
                if tt_ >= 2:
                    tt(kr[:, 32:64], kr[:, 0:32], rB[:, 0, tt_ - 2, :], ALU.mult, eng="gpsimd")
                    tt(kr[:, 64:96], kr[:, 0:32], rB[:, 1, tt_ - 2, :], ALU.mult, eng="gpsimd")
                    yield
                    tt(qk[:, 64:80], kr[:, 32:48], kr[:, 80:96], ALU.subtract, eng="gpsimd")
                    tt(qk[:, 80:96], kr[:, 64:80], kr[:, 48:64], ALU.add, eng="gpsimd")
                    yield
                pbk = bank()
                pbb = pbk[:].bitcast(BF16)
                tr(pbb[0:96, 0:128], qk[:, 0:96], identb[:])
                tr(pbb[0:96, 128:256], qk[:, 96:192], identb[:])
                yield
                cp(QB3[0:96, :, tk], pbb[0:96, 0:256].rearrange("p (c t) -> p c t", c=2), eng="scalar")

            bank_list[0] = list(range(8))
            pairs = [(2 * i, 2 * i + 1) for i in range(NTT // 2)]
            nxt_ = [b2_mm(pairs[0][0]), b2_mm(pairs[0][1])]
            for pi, (ta, tb_) in enumerate(pairs):
                cur_ = nxt_
                if pi + 1 < len(pairs):
                    nxt_ = [b2_mm(pairs[pi + 1][0]), b2_mm(pairs[pi + 1][1])]
                interleave([b2_chain(ta, cur_[0], 0), b2_chain(tb_, cur_[1], 1)])
            bank_list[0] = [0, 6, 7]
            qblocks = [(256 + i * 512, 512, list(range(NTT))) for i in range(4)]
            if not last:
                qblocks.append((0, 256, [0, 1]))
            for (q0, nq, kts) in qblocks:
                def finb(accs, nqt, per_bank, q0=q0, h=h):
                    a3 = accs[0][:, 0:nqt * 65].rearrange("p (t c) -> p t c", c=65)
                    recip(small[:, 24:24 + nqt], a3[:, :, 64])
                    tt(yb4[:, q0 // 128:q0 // 128 + nqt, h % 2, :], a3[:, :, 0:64],
                       small[:, 24:24 + nqt].unsqueeze(2).to_broadcast([128, nqt, 64]), ALU.mult)
                attention(QB3[:, 0, :], QB3[:, 1, :], lambda kt: VB3[:, kt, :], 64, 96.0 ** -0.5,
                          q0, nq, kts, finb)
            run_units()
            if h % 2 == 1:
                tiles = list(range(2, NTT)) if last else list(range(NTT))
                for g0 in range(0, len(tiles), 4):
                    grp = tiles[g0:g0 + 4]
                    pbk = bank()
                    pbb = pbk[:].bitcast(BF16)
                    for i, tq in enumerate(grp):
                        tr(pbb[:, i * 128:(i + 1) * 128], ybtok[:, tq * 128:(tq + 1) * 128], identb[:])
                    cp(yT3[:, h // 2, grp[0] * 128:(grp[-1] + 1) * 128], pbb[:, 0:len(grp) * 128], eng="scalar")
        merge(1)

        bank_list[0] = list(range(8))
        UP = [Q(i * 4736, (i + 1) * 4736).bitcast(F32) for i in range(3)]
        DTp = Q(14208, 14208 + PADL)
        su_ = slot()
        wu3 = su_[:].rearrange("p (k c) -> p k c", k=8)
        w_cols(W_in, OFF_U, 512, wu3)
        sp_ = slot()
        wp3 = sp_[:, 0:512].rearrange("p (g d) -> p g d", g=4)
        wload(wp3, w_pool[l].rearrange("g c d -> c g d"))
        for g in range(4):
            w = 2 << g
            A_, B_, C_ = UP
            for (a, b) in ((0, 16), (272, 304), (2352, 2368)):
                mset(A_[:, a:b], 0.0)

            def ev_u(j, t0, n, pb, A_=A_):
                cp(A_[:, pcol(t0):pcol(t0) + n], pb[:, 0:n], eng="scalar")
            fpass(wu3, [g], hT3, 8, TB_ALL, ev_u)
            src, dst = A_, B_
            tt(dst[:, 1:PADL], src[:, 0:PADL - 1], src[:, 1:PADL], ALU.add)
            lo, hi = 1, PADL
            if g >= 1:
                src, dst = B_, C_
                tt(dst[:, 2:PADL - 1], src[:, 1:PADL - 2], src[:, 3:PADL], ALU.add)
            if g >= 2:
                src, dst = C_, B_
                tt(dst[:, 4:PADL - 3], src[:, 2:PADL - 5], src[:, 6:PADL - 1], ALU.add)
            if g >= 3:
                src, dst = B_, C_
                tt(dst[:, 8:PADL - 7], src[:, 4:PADL - 11], src[:, 12:PADL - 3], ALU.add)
            Lg = dst
            ts(Lg[:, 16:2352], Lg[:, 16:2352], 1.0 / w, ALU.mult)
            for (s0, sl) in ((16, 256), (304, 2048)):
                tt(Lg[:, s0:s0 + 8], Lg[:, s0:s0 + 8], poolfix[:, g * 8:g * 8 + 8], ALU.mult)
                tt(Lg[:, s0 + sl - 8:s0 + sl], Lg[:, s0 + sl - 8:s0 + sl], poolfix[:, 40 + g * 8:48 + g * 8], ALU.mult)
            tt(DTp[:, 16:2352], Lg[:, 16:2352], A_[:, 16:2352], ALU.subtract)
            for (t0, n) in TB_ALL:
                pb = bank()
                mm(pb[:, 0:n], wp3[:, g, :], DTp[:, pcol(t0):pcol(t0) + n])
                act(yT3[:, g, t0:t0 + n], pb[:, 0:n], AF.Identity, scale=cols[:, 256 + l * 4 + g:257 + l * 4 + g])
        merge(2)

        PCs, U2, YS = UP
        for g in range(4):
            sd = slot()
            wd4 = sd[:, 0:3072].rearrange("p (k s c) -> p k s c", k=8, s=3)
            for si, off in enumerate((OFF_PC, OFF_PX, OFF_PB)):
                w_cols(W_in, off + g * 128, 128, wd4[:, :, si, :])
            wm_ = mod_load(l + 1, 8 + g) if not last else None
            for (a, b) in ((0, 16), (272, 304), (2352, 2368)):
                mset(U2[:, a:b], 0.0)

            def ev_pc(j, t0, n, pb):
                cp(PCs[:, pcol(t0):pcol(t0) + n], pb[:, 0:n], eng="scalar")

            def ev_px(j, t0, n, pb):
                tt(U2[:, pcol(t0):pcol(t0) + n], pb[:, 0:n], PCs[:, pcol(t0):pcol(t0) + n], ALU.mult)

            def ev_pb(j, t0, n, pb, g=g):
                tt(yT3[:, g, t0:t0 + n], pb[:, 0:n], YS[:, pcol(t0):pcol(t0) + n], ALU.mult)
            fpass(wd4[:, :, 0, :], [0], hT3, 8, TB_ALL, ev_pc)
            fpass(wd4[:, :, 1, :], [0], hT3, 8, TB_ALL, ev_px)
            if wm_ is not None:
                mod_mm(l + 1, 8 + g, wm_)
            wc = 272 + l * 12
            ts(YS[:, 16:2352], U2[:, 16:2352], cols[:, wc + 4 + g:wc + 5 + g], ALU.mult)
            stt(YS[:, 16:2352], U2[:, 15:2351], cols[:, wc + g:wc + g + 1], YS[:, 16:2352], ALU.mult, ALU.add)
            stt(YS[:, 16:2352], U2[:, 17:2353], cols[:, wc + 8 + g:wc + 9 + g], YS[:, 16:2352], ALU.mult, ALU.add)
            fpass(wd4[:, :, 2, :], [0], hT3, 8, TB_ALL, ev_pb)
        merge(3)

        so1 = slot()
        so2 = slot()
        wo = [so1[:].rearrange("p (k c) -> p k c", k=8), so2[:].rearrange("p (k c) -> p k c", k=8)]
        w_cols(w_o[l], 0, 512, wo[0])
        w_cols(w_o[l], 512, 512, wo[1])
        bufs8 = [xblk[0][:, 0:512], xblk[0][:, 512:1024], xblk[1][:, 0:512], xblk[1][:, 512:1024]] + [t_[:] for t_ in tmpf]
        bank_list[0] = list(range(7))
        for (t0, n) in TBX:
            st = 1 if t0 == 0 else 0
            pbn = PS[7]
            for i in range(8):
                dma(bufs8[i][:, 0:n], xs_d[i][:, t0:t0 + n], rd=[("xs", i, i + 1, t0 * 4, (t0 + n) * 4)])
            for i in range(8):
                xb = bufs8[i]
                pb = bank()
                for k in range(8):
                    mm(pb[:, 0:n], wo[i // 4][:, k, (i % 4) * 128:(i % 4 + 1) * 128], mix3[:, k, t0:t0 + n],
                       start=(k == 0), stop=(k == 7))
                gi = (16 + i) * 2 + st
                stt(xb[:, 0:n], pb[:, 0:n], modS[:, gi:gi + 1], xb[:, 0:n], ALU.mult, ALU.add)
                dma(xs_d[i][:, t0:t0 + n], xb[:, 0:n], wr=[("xs", i, i + 1, t0 * 4, (t0 + n) * 4)])
                sq = rr(tmpb, "sq")
                act(sq[:, 0:n], xb[:, 0:n], AF.Square)
                mm(pbn[:, 0:n], onesb[:], sq[:, 0:n], start=(i == 0), stop=(i == 7))
            act(oA[:, 0:n], pbn[:, 0:n], AF.Sqrt, bias=EPS, scale=1.0 / D)
            recip(rstd_bc[:, 0:n], oA[:, 0:n])
            for i in range(8):
                xb = bufs8[i]
                tt(xb[:, 0:n], xb[:, 0:n], rstd_bc[:, 0:n], ALU.mult)
                act(hT3[:, i, t0:t0 + n], xb[:, 0:n], AF.Identity, bias=modS[:, (24 + i) * 2 + st:(24 + i) * 2 + st + 1],
                    scale=gsS[:, 16 + i * 2 + st:16 + i * 2 + st + 1])
        bank_list[0] = list(range(8))
        for k in range(8):
            dma(xT3[:, k, :], xs_d[k], rd=[("xs", k, k + 1, 0, NT * 4)])

        for g in range(8):
            s1 = slot()
            s2 = slot()
            w13 = s1[:].rearrange("p (k c) -> p k c", k=8)
            w23 = s2[:].rearrange("p (k c) -> p k c", k=4)
            w_cols(w_ff1[l], g * 512, 512, w13)
            wload(w23, w_ff2[l][g * 512:(g + 1) * 512, :].rearrange("(k p) c -> p k c", p=128))
            wm_ = mod_load(l + 1, g) if not last else None

            def ev_a(j, t0, n, pb):
                r = rr(tmpf, "tf")
                act(r[:, 0:n], pb[:, 0:n], AF.Relu)
                tt(aT3[:, j, t0:t0 + n], r[:, 0:n], r[:, 0:n], ALU.mult)
            fpass(w13, range(4), hT3, 8, TBX, ev_a)
            if wm_ is not None:
                mod_mm(l + 1, g, wm_)

            def ev_x(i, t0, n, pb):
                gi = (40 + i) * 2 + (1 if t0 == 0 else 0)
                stt(xT3[:, i, t0:t0 + n], pb[:, 0:n], modS[:, gi:gi + 1], xT3[:, i, t0:t0 + n], ALU.mult, ALU.add)
            fpass(w23, range(8), aT3, 4, TBX, ev_x)

    for tt_ in range(2, NTT):
        xb = rr(xblk, "xin")
        for half in range(2):
            pb = bank()
            for j in range(4):
                k = half * 4 + j
                tr(pb[:, j * 128:(j + 1) * 128], xT3[:, k, tt_ * 128:(tt_ + 1) * 128], identf[:])
            cp(xb[:, half * 512:(half + 1) * 512], pb[:], eng=("vector" if half == 0 else "scalar"))
        dma(y_d[(tt_ - 2) * 128:(tt_ - 1) * 128, :], xb[:])
    S.add("sync", lambda e: e.nop(), [y_d], [])
    S.emit()
    es.close()
    return nc


def _consts():
    identf = np.eye(128, dtype=np.float32)

    def rope(rot_dim):
        rows = S_LAT // 64
        row = np.repeat(np.arange(rows, dtype=np.float32), 64)
        col = np.tile(np.arange(64, dtype=np.float32), rows)
        n_freq = rot_dim // 4
        inv = (np.float32(10000.0) ** (-np.arange(n_freq, dtype=np.float32) / np.float32(n_freq))).astype(np.float32)
        ang = np.concatenate([row[:, None] * inv, col[:, None] * inv], axis=-1).astype(np.float32)
        cs, sn = np.cos(ang).astype(np.float32), np.sin(ang).astype(np.float32)
        half = rot_dim // 2
        cc = np.concatenate([cs, cs], -1).reshape(16, 128, 2 * half).transpose(1, 0, 2)
        ss = np.concatenate([sn, sn], -1).reshape(16, 128, 2 * half).transpose(1, 0, 2)
        return np.ascontiguousarray(np.stack([cc, ss], 1).reshape(128, -1)).astype(np.float32)

    fix = np.ones((128,), np.float32)
    for g, w in enumerate((2, 4, 8, 16)):
        for t in range(w // 2):
            fix[g * 8 + t] = w / (t + w // 2)
        for j in range(1, w // 2):
            fix[40 + g * 8 + (8 - j)] = w / (j + w // 2)
    fix[32:35] = (1.0 / 384, 1.0 / 256, 1.0 / 32)
    poolfix = np.tile(fix[None, :], (128, 1)).astype(np.float32)
    return identf, rope(64), rope(32), poolfix


_NC_CACHE = {}


def kernel(x, c, ctx, c_ctx, w_mod, b_mod, g_norm1, g_norm2, w_in, gq_a, gk_a, lam_a, g_sub_a,
           g_cq, w_uq, g_ckv, w_ukv, gq_b, gk_b, w_pool, s_pool, w_conv, w_branch, w_o, w_ff1, w_ff2,
           _cores=8):
    f = lambda a: np.ascontiguousarray(np.asarray(a, dtype=np.float32))
    x, c, ctx, c_ctx = f(x), f(c), f(ctx), f(c_ctx)
    L = int(np.asarray(w_mod).shape[0])
    identf, ropeA, ropeB, poolfix = _consts()
    gains = np.concatenate([f(gq_a), f(gk_a), f(g_sub_a), f(g_cq), f(g_ckv), f(gq_b), f(gk_b),
                            f(lam_a).reshape(L, 256)], axis=1)
    shared = dict(w_mod=f(w_mod), w_in=f(w_in), gains=np.ascontiguousarray(gains), w_uq=f(w_uq), w_ukv=f(w_ukv),
                  w_pool=f(w_pool), w_branch=f(w_branch), w_o=f(w_o), w_ff1=f(w_ff1), w_ff2=f(w_ff2),
                  identf=identf, ropeA=ropeA, ropeB=ropeB, poolfix=poolfix)
    rows = np.zeros((384, 128), np.float32)
    rows[0:L * 48] = f(b_mod).reshape(L * 48, 128)
    rows[192:192 + L * 8] = f(g_norm1).reshape(L * 8, 128)
    rows[224:224 + L * 8] = f(g_norm2).reshape(L * 8, 128)
    rows[256:256 + L * 4] = f(s_pool).reshape(L * 4, 128)
    rows[272:272 + L * 12] = f(w_conv).reshape(L * 12, 128)
    rows[328:336] = c_ctx.reshape(8, 128)
    if L not in _NC_CACHE:
        _NC_CACHE[L] = build(L)
    nc = _NC_CACHE[L]
    in_maps = []
    for b in range(_cores):
        r = rows.copy()
        r[320:328] = c[b].reshape(8, 128)
        m = dict(shared)
        m.update(x=x[b], ctx=ctx[b], rows=np.ascontiguousarray(r.reshape(3, 128, 128)))
        in_maps.append(m)
    res = run_bass_kernel_spmd(nc, in_maps, core_ids=list(range(_cores)))
    return np.stack([np.asarray(res.results[b]["y"], dtype=np.float32) for b in range(_cores)], axis=0)
```

```python
import math
import os
import numpy as np
import concourse.bass as bass
import concourse.mybir as mybir
from concourse.bass_utils import run_bass_kernel_spmd

F32 = mybir.dt.float32
BF16 = mybir.dt.bfloat16
AF = mybir.ActivationFunctionType
ALU = mybir.AluOpType
AX = mybir.AxisListType
DSZ = {F32: 4, BF16: 2}

D = 1024
S_LAT = 2048
S_CTX = 256
NT = S_LAT + S_CTX
NTT = NT // 128
D_IN = 8352
EPS = 1e-6
OFF_Q, OFF_K, OFF_V, OFF_CQ, OFF_CKV, OFF_KR, OFF_U, OFF_PB, OFF_PC, OFF_PX, OFF_G = (
    0, 512, 1024, 1536, 1920, 2176, 2208, 2720, 3232, 3744, 4256)
TB_ALL = [(0, 256), (256, 512), (768, 512), (1280, 512), (1792, 512)]
PADL = 2368


def pcol(t):
    return t + 16 if t < 256 else t + 48


class Sched:
    ENG = ("tensor", "vector", "scalar", "gpsimd", "sync")

    def __init__(self, nc, ndma=12):
        self.nc = nc
        self.ops = {e: [] for e in self.ENG}
        self.track = {}
        self.ndma = ndma
        self.dma_n = {"sync": 0, "gpsimd": 0}
        self.dma_cum = {}

    @staticmethod
    def region(ap):
        if isinstance(ap, tuple):
            return ap
        dims = ap.ap
        dsz = DSZ[ap.dtype]
        row = dims[0][0]
        off = ap.offset
        sp = str(ap.space)
        if "SB" in sp or "PSUM" in sp:
            p0 = off // row
            f0 = off % row
            ext = 0
            for s, c in dims[1:]:
                ext += (c - 1) * abs(s)
            return (ap.tensor.name, p0, p0 + dims[0][1], f0 * dsz, (f0 + ext + 1) * dsz)
        ext = 0
        for s, c in dims:
            ext += (c - 1) * abs(s)
        return (ap.tensor.name, 0, 1, off * dsz, (off + ext + 1) * dsz)

    def _conf(self, r, write, deps):
        for e in self.track.get(r[0], ()):
            if (write or e[5]) and e[0] < r[2] and r[1] < e[1] and e[2] < r[4] and r[3] < e[3]:
                deps.add(e[4])

    def _record(self, r, prod, write):
        lst = self.track.setdefault(r[0], [])
        if write:
            lst[:] = [e for e in lst if not (r[1] <= e[0] and e[1] <= r[2] and r[3] <= e[2] and e[3] <= r[4])]
        else:
            lst[:] = [e for e in lst if not ((not e[5]) and e[4][0] == prod[0] and e[0] == r[1] and e[1] == r[2]
                                             and e[2] == r[3] and e[3] == r[4])]
        lst.append((r[1], r[2], r[3], r[4], prod, write))

    def add(self, eng, fn, reads, writes, dma=False):
        reads = [self.region(a) for a in reads]
        writes = [self.region(a) for a in writes]
        deps = set()
        for r in reads:
            self._conf(r, False, deps)
        for w in writes:
            self._conf(w, True, deps)
        if dma:
            k = self.dma_n[eng] % self.ndma
            self.dma_n[eng] += 1
            name = "d_%s_%d" % (eng, k)
            self.dma_cum[name] = self.dma_cum.get(name, 0) + 1
            prod = (name, self.dma_cum[name])
            if prod[1] > 1:
                deps.add((name, prod[1] - 1))
        else:
            prod = (eng, len(self.ops[eng]) + 1)
        self.ops[eng].append([fn, deps, prod, dma, False])
        for r in reads:
            self._record(r, prod, False)
        for w in writes:
            self._record(w, prod, True)
        return prod

    def emit(self):
        nc = self.nc
        needed = set()
        for e in self.ENG:
            for op in self.ops[e]:
                for d in op[1]:
                    if not (d[0] == "tensor" and e == "tensor"):
                        needed.add(d)
        ms = {}
        for e in self.ENG:
            cnt = 0
            for i, op in enumerate(self.ops[e]):
                if (not op[3]) and (e, i + 1) in needed:
                    cnt += 1
                    op[4] = True
                    ms[(e, i + 1)] = cnt
        names = list(self.ENG) + sorted(self.dma_cum.keys())
        import contextlib
        with contextlib.ExitStack() as st:
            sems = {n: st.enter_context(nc.semaphore("s_" + n)) for n in names}
            block = st.enter_context(nc.Block())

            def run(e, eng):
                waited = {}
                for fn, deps, prod, dma, inc in self.ops[e]:
                    need = {}
                    for (p, s) in deps:
                        if p == "tensor" and e == "tensor":
                            continue
                        v = ms[(p, s)] if p in self.ENG else 16 * s
                        if v > need.get(p, 0):
                            need[p] = v
                    for p, v in need.items():
                        if waited.get(p, 0) < v:
                            eng.wait_ge(sems[p], v)
                            waited[p] = v
                    ins = fn(eng)
                    if dma:
                        ins.then_inc(sems[prod[0]], 16)
                    elif inc:
                        ins.then_inc(sems[e], 1)

            @block.tensor
            def _(eng):
                run("tensor", eng)

            @block.vector
            def _(eng):
                run("vector", eng)

            @block.scalar
            def _(eng):
                run("scalar", eng)

            @block.gpsimd
            def _(eng):
                run("gpsimd", eng)

            @block.sync
            def _(eng):
                run("sync", eng)


def build(depth, dbg=False):
    nc = bass.Bass("TRN2", target_bir_lowering=False)
    L = depth

    def din(name, shape):
        return nc.dram_tensor(name, list(shape), F32, kind="ExternalInput").ap()

    x_d = din("x", [S_LAT, D])
    ctx_d = din("ctx", [S_CTX, D])
    rows_d = din("rows", [3, 128, 128])
    w_mod = din("w_mod", [L, D, 6 * D])
    w_in = din("w_in", [L, D, D_IN])
    gains_d = din("gains", [L, 1344])
    w_uq = din("w_uq", [L, 384, 768])
    w_ukv = din("w_ukv", [L, 256, 1024])
    w_pool = din("w_pool", [L, 4, 128, 128])
    w_branch = din("w_branch", [L, 4, 512, D])
    w_o = din("w_o", [L, D, D])
    w_ff1 = din("w_ff1", [L, D, 4 * D])
    w_ff2 = din("w_ff2", [L, 4 * D, D])
    identf_d = din("identf", [128, 128])
    ropeA_d = din("ropeA", [128, 2 * 16 * 64])
    ropeB_d = din("ropeB", [128, 2 * 16 * 32])
    poolfix_d = din("poolfix", [128, 128])
    y_d = nc.dram_tensor("y", [S_LAT, D], F32, kind="ExternalOutput").ap()
    xs_d = nc.dram_tensor("xs", [8, 128, NT], F32, kind="Internal").ap()

    S = Sched(nc)
    import contextlib
    es = contextlib.ExitStack()

    def sb(name, shape, dt):
        return es.enter_context(nc.sbuf_tensor(name, list(shape), dt))

    hT = sb("hT", [128, 8 * NT], BF16)
    U = sb("U", [128, 47104], BF16)
    ring = [sb("w%d" % i, [128, 4096], BF16) for i in range(3)]
    xblk = [sb("xb%d" % i, [128, 1024], F32) for i in range(2)]
    PT = [sb("pt%d" % i, [128, 512], BF16) for i in range(3)]
    tmpf = [sb("tf%d" % i, [128, 512], F32) for i in range(4)]
    tmpb = [sb("tb%d" % i, [128, 512], BF16) for i in range(4)]
    rstd_bc = sb("rstd_bc", [128, 512], F32)
    oA = sb("oA", [128, 512], F32)
    small = sb("small", [128, 64], F32)
    small2 = sb("small2", [128, 64], F32)
    small3 = sb("small3", [128, 64], F32)
    small4 = sb("small4", [128, 64], F32)
    modraw = [sb("modraw0", [128, 96], F32), sb("modraw1", [128, 96], F32)]
    smalls = [small, small2, small3, small4]
    cols = sb("cols", [128, 384], F32)
    identf = sb("identf_s", [128, 128], F32)
    identb = sb("identb", [128, 128], BF16)
    onesb = sb("onesb", [128, 128], BF16)
    ropeA = sb("ropeA_s", [128, 2 * 16 * 64], F32)
    ropeB = sb("ropeB_s", [128, 2 * 16 * 32], F32)
    poolfix = sb("poolfix_s", [128, 128], F32)
    gains = sb("gains_s", [128, 1344], F32)
    modS = sb("modS", [128, 96], F32)
    gsS = sb("gsS", [128, 32], F32)
    sc = sb("sc", [128, 16], BF16)
    lamt = sb("lamt", [128, 8], F32)
    gsub = sb("gsub", [128, 128], F32)
    ybtok = sb("ybtok", [128, NTT * 128], BF16)
    PS = [es.enter_context(nc.psum_tensor("ps%d" % i, [128, 512], F32)) for i in range(8)]

    hT3 = hT[:].rearrange("p (k t) -> p k t", k=8)
    xT3 = U[:, 0:36864].bitcast(F32).rearrange("p (k t) -> p k t", k=8)
    aT3 = U[:, 36864:46080].rearrange("p (k t) -> p k t", k=4)
    yT3 = U[:, 0:9216].rearrange("p (k t) -> p k t", k=4)
    mix3 = U[:, 9216:27648].rearrange("p (k t) -> p k t", k=8)
    QO = 27648

    def Q(a, b):
        return U[:, QO + a:QO + b]

    ring_i = [0]

    def slot():
        r = ring[ring_i[0] % 3]
        ring_i[0] += 1
        return r

    def wload(dst, src):
        S.add("gpsimd", lambda e, d=dst, s=src: e.dma_start(out=d, in_=s), [src], [dst], dma=True)

    def dma(dst, src, rd=None, wr=None):
        S.add("sync", lambda e, d=dst, s=src: e.dma_start(out=d, in_=s),
              [src] if rd is None else rd, [dst] if wr is None else wr, dma=True)

    def bankreg(out):
        r = S.region(out)
        return (r[0], r[1], r[2], 0, 2048)

    def mm(out, lhsT, rhs, start=True, stop=True, skip=False):
        S.add("tensor", lambda e: e.matmul(out, lhsT, rhs, start=start, stop=stop, skip_group_check=skip),
              [lhsT, rhs], [bankreg(out)])

    def tr(out, in_, ident):
        S.add("tensor", lambda e: e.transpose(out, in_, ident), [in_, ident], [bankreg(out)])

    def act(out, in_, func, bias=None, scale=None):
        rd = [in_]
        kw = {}
        if bias is not None:
            kw["bias"] = bias
            if not isinstance(bias, (int, float)):
                rd.append(bias)
        if scale is not None:
            kw["scale"] = scale
            if not isinstance(scale, (int, float)):
                rd.append(scale)
        S.add("scalar", lambda e: e.activation(out=out, in_=in_, func=func, **kw), rd, [out])

    def tt(out, in0, in1, op, eng="vector"):
        S.add(eng, lambda e: e.tensor_tensor(out=out, in0=in0, in1=in1, op=op), [in0, in1], [out])

    def ts(out, in0, s1, op0, s2=None, op1=None, eng="vector"):
        rd = [in0] + [s for s in (s1, s2) if s is not None and not isinstance(s, (int, float))]
        if op1 is None:
            S.add(eng, lambda e: e.tensor_scalar(out=out, in0=in0, scalar1=s1, scalar2=None, op0=op0), rd, [out])
        else:
            S.add(eng, lambda e: e.tensor_scalar(out=out, in0=in0, scalar1=s1, scalar2=s2, op0=op0, op1=op1), rd, [out])

    def stt(out, in0, scalar, in1, op0, op1):
        rd = [in0, in1] + ([] if isinstance(scalar, (int, float)) else [scalar])
        S.add("vector", lambda e: e.scalar_tensor_tensor(out=out, in0=in0, scalar=scalar, in1=in1, op0=op0, op1=op1),
              rd, [out])

    def cp(out, in_, eng="vector"):
        if eng == "scalar":
            S.add("scalar", lambda e: e.copy(out=out, in_=in_), [in_], [out])
        else:
            S.add(eng, lambda e: e.tensor_copy(out=out, in_=in_), [in_], [out])

    def red(out, in_):
        S.add("vector", lambda e: e.tensor_reduce(out=out, in_=in_, axis=AX.X, op=ALU.add), [in_], [out])

    def recip(out, in_):
        S.add("vector", lambda e: e.reciprocal(out=out, in_=in_), [in_], [out])

    def mset(ap, v, eng="vector"):
        S.add(eng, lambda e: e.memset(ap, v), [], [ap])

    def interleave(gens):
        gens = list(gens)
        while gens:
            for g in list(gens):
                try:
                    next(g)
                except StopIteration:
                    gens.remove(g)

    bank_i = [0]
    bank_list = [list(range(8))]

    def bank():
        b = bank_list[0][bank_i[0] % len(bank_list[0])]
        bank_i[0] += 1
        return PS[b]

    rot = {}

    def rr(lst, key):
        i = rot.get(key, 0)
        rot[key] = i + 1
        return lst[i % len(lst)]

    dma(identf[:], identf_d)
    dma(ropeA[:], ropeA_d)
    dma(ropeB[:], ropeB_d)
    dma(poolfix[:], poolfix_d)
    cp(identb[:], identf[:])
    mset(onesb[:], 1.0)
    for r in range(3):
        xb = xblk[r % 2]
        dma(xb[:, 0:128], rows_d[r])
        pb = bank()
        tr(pb[:, 0:128], xb[:, 0:128], identf[:])
        cp(cols[:, r * 128:(r + 1) * 128], pb[:, 0:128])
    sc3 = sc[:].rearrange("p (k s) -> p k s", s=2)
    act(sc3[:, :, 0], cols[:, 320:328], AF.Silu)
    act(sc3[:, :, 1], cols[:, 328:336], AF.Silu)

    for tt_ in range(NTT):
        xb = rr(xblk, "xin")
        src = ctx_d[tt_ * 128:(tt_ + 1) * 128, :] if tt_ < 2 else x_d[(tt_ - 2) * 128:(tt_ - 1) * 128, :]
        dma(xb[:], src)
        for half in range(2):
            pb = bank()
            for j in range(4):
                k = half * 4 + j
                tr(pb[:, j * 128:(j + 1) * 128], xb[:, k * 128:(k + 1) * 128], identf[:])
            cp(xT3[:, half * 4:half * 4 + 4, tt_ * 128:(tt_ + 1) * 128],
               pb[:].rearrange("p (k t) -> p k t", k=4), eng=("vector" if half == 0 else "scalar"))

    def w_cols(W2d, c0, n, dst, kch=8):
        wload(dst, W2d[:, c0:c0 + n].rearrange("(k p) c -> p k c", p=128))

    def norm_and_h(l, ni, tbs):
        for (t0, n) in tbs:
            st = 1 if t0 == 0 else 0
            pb = bank()
            for k in range(8):
                sq = rr(tmpb, "sq")
                act(sq[:, 0:n], xT3[:, k, t0:t0 + n], AF.Square)
                mm(pb[:, 0:n], onesb[:], sq[:, 0:n], start=(k == 0), stop=(k == 7))
            t = rr(tmpf, "tf")
            act(t[:, 0:n], pb[:, 0:n], AF.Sqrt, bias=EPS, scale=1.0 / D)
            recip(rstd_bc[:, 0:n], t[:, 0:n])
            for k in range(8):
                t = rr(tmpf, "tf")
                tt(t[:, 0:n], xT3[:, k, t0:t0 + n], rstd_bc[:, 0:n], ALU.mult)
                gi = ni * 16 + k * 2 + st
                mi = ((0 if ni == 0 else 24) + k) * 2 + st
                act(hT3[:, k, t0:t0 + n], t[:, 0:n], AF.Identity, bias=modS[:, mi:mi + 1], scale=gsS[:, gi:gi + 1])

    def fpass(wslot3, ncols_chunks, rhs3, kch, tbs, evac):
        for j in ncols_chunks:
            for (t0, n) in tbs:
                pb = bank()
                for k in range(kch):
                    mm(pb[:, 0:n], wslot3[:, k, j * 128:(j + 1) * 128], rhs3[:, k, t0:t0 + n],
                       start=(k == 0), stop=(k == kch - 1))
                evac(j, t0, n, pb)

    for l in range(L):
        last = (l == L - 1)
        lam_init = 0.8 - 0.6 * math.exp(-0.3 * l)
        TBX = TB_ALL[1:] if last else TB_ALL
        W_in = w_in[l]

        dma(gains[:], gains_d[l:l + 1, :].to_broadcast([128, 1344]))
        G_QA, G_KA, G_SUB, G_CQ, G_CKV, G_QB, G_KB, G_LAM = 0, 64, 128, 256, 640, 896, 992, 1088
        t = rr(tmpf, "tf")
        tt(t[:, 0:64], gains[:, G_LAM:G_LAM + 64], gains[:, G_LAM + 64:G_LAM + 128], ALU.mult)
        tt(t[:, 64:128], gains[:, G_LAM + 128:G_LAM + 192], gains[:, G_LAM + 192:G_LAM + 256], ALU.mult)
        red(lamt[:, 0:2], t[:, 0:128].rearrange("p (a b) -> p a b", a=2))
        act(lamt[:, 2:4], lamt[:, 0:2], AF.Exp)
        tt(lamt[:, 5:6], lamt[:, 3:4], lamt[:, 2:3], ALU.subtract)
        ts(lamt[:, 4:5], lamt[:, 5:6], -lam_init, ALU.add)
        ts(gsub[:], gains[:, G_SUB:G_SUB + 128], 1.0 - lam_init, ALU.mult)

        def mod_load(l2, s_):
            wsl = slot()
            w3 = wsl[:].rearrange("p (k c) -> p k c", k=8)
            w_cols(w_mod[l2], s_ * 512, 512, w3)
            return w3

        def mod_mm(l2, s_, w3):
            pbm = bank()
            for m in range(4):
                for k in range(8):
                    mm(pbm[:, m * 2:m * 2 + 2], w3[:, k, m * 128:(m + 1) * 128], sc3[:, k, :],
                       start=(k == 0), stop=(k == 7))
            cp(modraw[l2 % 2][:, s_ * 8:(s_ + 1) * 8], pbm[:, 0:8])
        if l == 0:
            for s_ in range(12):
                mod_mm(0, s_, mod_load(0, s_))
        mps = modraw[l % 2]
        mod3 = modS[:].rearrange("p (m s) -> p m s", s=2)
        tt(mod3, mps[:, 0:96].rearrange("p (m s) -> p m s", s=2),
           cols[:, l * 48:(l + 1) * 48].unsqueeze(2).to_broadcast([128, 48, 2]), ALU.add)
        gs4 = gsS[:].rearrange("p (n k s) -> p n k s", n=2, s=2)
        for ni in range(2):
            so = 8 if ni == 0 else 32
            gcol = (192 if ni == 0 else 224) + l * 8
            ts(gs4[:, ni], mod3[:, so:so + 8, :], 1.0, ALU.add)
            tt(gs4[:, ni], gs4[:, ni], cols[:, gcol:gcol + 8].unsqueeze(2).to_broadcast([128, 8, 2]), ALU.mult)

        norm_and_h(l, 0, TB_ALL)
        for k in range(8):
            dma(xs_d[k], xT3[:, k, :], wr=[("xs", k, k + 1, 0, NT * 4)])

        bank_list[0] = [0]
        sc_banks = [PS[1], PS[2], PS[3]]
        acc_sets = [(PS[4], PS[5]), (PS[6], PS[7])]

        units = []

        def attention(*u):
            units.append(u)

        def run_units():
            steps = [(ui, ki) for ui, u in enumerate(units) for ki in range(len(u[7]))]

            def score(i):
                ui, ki = steps[i]
                qT, kT, vfn, dv, scale, q0, nq, kts, fin = units[ui]
                sb_ = rr(sc_banks, "sc")
                kt = kts[ki]
                mm(sb_[:, 0:nq], kT[:, kt * 128:(kt + 1) * 128], qT[:, q0:q0 + nq])
                return sb_
            accs = None
            issued = []

            def ensure(n_):
                while len(issued) < min(n_, len(steps)):
                    issued.append(score(len(issued)))
            ensure(2)
            for i, (ui, ki) in enumerate(steps):
                qT, kT, vfn, dv, scale, q0, nq, kts, fin = units[ui]
                ensure(i + 3)
                cur = issued[i]
                nqt = nq // 128
                per_bank = 2 if dv == 128 else 4
                if ki == 0:
                    accs = rr(acc_sets, "acc")
                kt = kts[ki]
                p = rr(PT, "pt")
                act(p[:, 0:nq], cur[:, 0:nq], AF.Exp, scale=scale)
                for qt in range(nqt):
                    bk = accs[qt // per_bank]
                    o = (qt % per_bank) * (dv + 1)
                    first = (ki == 0 and qt % per_bank == 0)
                    mm(bk[:, o:o + dv + 1], p[:, qt * 128:(qt + 1) * 128], vfn(kt),
                       start=first, stop=(ki == len(kts) - 1), skip=True)
                if ki == len(kts) - 1:
                    fin(accs, nqt, per_bank)
            del units[:]

        KA3 = Q(0, 4608).rearrange("p (c t) -> p c t", c=2)
        QP4 = Q(4608, 13824).rearrange("p (h m t) -> p h m t", h=2, m=2)
        VA4 = Q(13824, 13824 + NTT * 258).rearrange("p (t h c) -> p t h c", t=NTT, h=2)
        rA = ropeA[:].rearrange("p (s t c) -> p s t c", s=2, t=16)
        qk_unit_done = [0]
        for hp in range(2):
            s1 = slot()
            s2 = slot()
            w1 = s1[:].rearrange("p (k c) -> p k c", k=8)
            w2 = s2[:, 0:2048].rearrange("p (k c) -> p k c", k=8)
            w_cols(W_in, OFF_Q + hp * 256, 256, w1[:, :, 0:256])
            w_cols(W_in, OFF_K + hp * 256, 256, w1[:, :, 256:512])
            w_cols(W_in, OFF_V + hp * 256, 256, w2)
            for hh_ in range(2):
                mset(VA4[:, :, hh_, 128:129], 1.0)
                mset(QP4[64:128, hh_, 0, :], 0.0)
                mset(QP4[0:64, hh_, 1, :], 0.0)
            def a_mm(tt_):
                tk = slice(tt_ * 128, (tt_ + 1) * 128)
                p1 = PS[(tt_ % 3) * 2]
                for k in range(8):
                    mm(p1[:], hT3[:, k, tk], w1[:, k, :], start=(k == 0), stop=(k == 7))
                p2 = PS[(tt_ % 3) * 2 + 1]
                for k in range(8):
                    mm(p2[:, 0:256], hT3[:, k, tk], w2[:, k, :], start=(k == 0), stop=(k == 7))
                return p1, p2

            def a_chain(tt_, p1, p2, par):
                tk = slice(tt_ * 128, (tt_ + 1) * 128)
                sm = smalls[par]
                sq = tmpf[2 * par]
                qn = tmpf[2 * par + 1]
                t1 = xblk[par][:, 0:512]
                t2 = xblk[par][:, 512:1024]
                qkb = tmpb[par]
                act(sq[:], p1[:], AF.Square)
                cp(VA4[:, tt_, :, 0:128], p2[:, 0:256].rearrange("p (h c) -> p h c", h=2), eng="scalar")
                yield
                red(sm[:, 0:8], sq[:].rearrange("p (g c) -> p g c", g=8))
                yield
                act(sm[:, 8:16], sm[:, 0:8], AF.Sqrt, bias=EPS, scale=1.0 / 64)
                yield
                recip(sm[:, 16:24], sm[:, 8:16])
                yield
                qn3 = qn[:].rearrange("p (g c) -> p g c", g=8)
                tt(qn3, p1[:].rearrange("p (g c) -> p g c", g=8),
                   sm[:, 16:24].unsqueeze(2).to_broadcast([128, 8, 64]), ALU.mult)
                yield
                qn4 = qn[:].rearrange("p (a g c) -> p a g c", a=2, g=4)
                tt(qn4, qn4, gains[:, G_QA:G_QA + 128].rearrange("p (a c) -> p a c", a=2).unsqueeze(2)
                   .to_broadcast([128, 2, 4, 64]), ALU.mult)
                yield
                if tt_ >= 2:
                    cc = rA[:, 0, tt_ - 2, :].unsqueeze(1).to_broadcast([128, 8, 64])
                    ss = rA[:, 1, tt_ - 2, :].unsqueeze(1).to_broadcast([128, 8, 64])
                    t13 = t1.rearrange("p (g c) -> p g c", g=8)
                    t23 = t2.rearrange("p (g c) -> p g c", g=8)
                    tt(t13, qn3, cc, ALU.mult)
                    tt(t23, qn3, ss, ALU.mult, eng="gpsimd")
                    yield
                    qkb3 = qkb[:].rearrange("p (g c) -> p g c", g=8)
                    tt(qkb3[:, :, 0:32], t13[:, :, 0:32], t23[:, :, 32:64], ALU.subtract)
                    tt(qkb3[:, :, 32:64], t23[:, :, 0:32], t13[:, :, 32:64], ALU.add, eng="gpsimd")
                    yield
                else:
                    cp(qkb[:], qn[:])
                    yield
                pbk = PS[6 + par]
                pbb = pbk[:].bitcast(BF16)
                for j in range(4):
                    tr(pbb[:, j * 128:(j + 1) * 128], qkb[:, j * 128:(j + 1) * 128], identb[:])
                yield
                pb4 = pbb[:, 0:512].rearrange("p (c t) -> p c t", c=4)
                cp(KA3[:, :, tk], pb4[:, 2:4, :], eng="scalar")
                cp(QP4[0:64, :, 0, tk], pb4[0:64, 0:2, :], eng="scalar")
                cp(QP4[64:128, :, 1, tk], pb4[64:128, 0:2, :], eng="scalar")

            mmd = {0: a_mm(0), 1: a_mm(1)}
            for pi in range(NTT // 2):
                ta, tb_ = 2 * pi, 2 * pi + 1
                if ta + 2 < NTT:
                    mmd[ta + 2] = a_mm(ta + 2)
                interleave([a_chain(ta, mmd[ta][0], mmd[ta][1], 0), a_chain(tb_, mmd[tb_][0], mmd[tb_][1], 1)])
                if tb_ + 2 < NTT:
                    mmd[tb_ + 2] = a_mm(tb_ + 2)
            bank_list[0] = [0]
            for hh in range(2):
                h = hp * 2 + hh
                qblocks = [(256 + i * 512, 512, list(range(NTT))) for i in range(4)]
                if not last:
                    qblocks.append((0, 256, [0, 1]))
                for (q0, nq, kts) in qblocks:
                    def fin0(accs, nqt, per_bank, nq=nq):
                        for b in range((nqt + 1) // 2):
                            a3 = accs[b][:, 0:258].rearrange("p (t c) -> p t c", t=2)
                            recip(small[:, 24 + 2 * b:26 + 2 * b], a3[:, :, 128])
                            tt(oA[:, b * 256:(b + 1) * 256].rearrange("p (t c) -> p t c", t=2), a3[:, :, 0:128],
                               small[:, 24 + 2 * b:26 + 2 * b].unsqueeze(2).to_broadcast([128, 2, 128]), ALU.mult)

                    def fin1(accs, nqt, per_bank, nq=nq, q0=q0, h=h):
                        o = rr(tmpf, "tf")
                        for b in range((nqt + 1) // 2):
                            a3 = accs[b][:, 0:258].rearrange("p (t c) -> p t c", t=2)
                            recip(small[:, 28 + 2 * b:30 + 2 * b], a3[:, :, 128])
                            ts(small[:, 32 + 2 * b:34 + 2 * b], small[:, 28 + 2 * b:30 + 2 * b], lamt[:, 4:5], ALU.mult)
                            o3 = o[:, b * 256:(b + 1) * 256].rearrange("p (t c) -> p t c", t=2)
                            tt(o3, a3[:, :, 0:128],
                               small[:, 32 + 2 * b:34 + 2 * b].unsqueeze(2).to_broadcast([128, 2, 128]), ALU.mult)
                        tt(o[:, 0:nq], o[:, 0:nq], oA[:, 0:nq], ALU.add)
                        sq = rr(tmpf, "tf")
                        act(sq[:, 0:nq], o[:, 0:nq], AF.Square)
                        red(small[:, 36:36 + nqt], sq[:, 0:nq].rearrange("p (t c) -> p t c", c=128))
                        act(small[:, 40:40 + nqt], small[:, 36:36 + nqt], AF.Sqrt, bias=EPS, scale=1.0 / 128)
                        recip(small[:, 44:44 + nqt], small[:, 40:40 + nqt])
                        o3 = o[:, 0:nq].rearrange("p (t c) -> p t c", c=128)
                        tt(o3, o3, small[:, 44:44 + nqt].unsqueeze(2).to_broadcast([128, nqt, 128]), ALU.mult)
                        yb = rr(tmpb, "qkb")
                        tt(yb[:, 0:nq].rearrange("p (t c) -> p t c", c=128), o3,
                           gsub[:].unsqueeze(1).to_broadcast([128, nqt, 128]), ALU.mult)
                        pbk = bank()
                        pbb = pbk[:].bitcast(BF16)
                        for qt in range(nqt):
                            tr(pbb[:, qt * 128:(qt + 1) * 128], yb[:, qt * 128:(qt + 1) * 128], identb[:])
                        cp(yT3[:, h, q0:q0 + nq], pbb[:, 0:nq], eng="scalar")

                    for m in range(2):
                        attention(QP4[:, hh, m, :], KA3[:, hh, :],
                                  lambda kt, hh=hh: VA4[:, kt, hh, :], 128, 0.125, q0, nq, kts,
                                  fin0 if m == 0 else fin1)
            run_units()

        merged = [0]

        def merge(n):
            bank_list[0] = list(range(8))
            g1 = slot()
            g2 = slot()
            wbs = slot()
            gw = [g1[:].rearrange("p (k c) -> p k c", k=8), g2[:].rearrange("p (k c) -> p k c", k=8)]
            w_cols(W_in, OFF_G + n * 1024, 512, gw[0])
            w_cols(W_in, OFF_G + n * 1024 + 512, 512, gw[1])
            wb3 = wbs[:].rearrange("p (k c) -> p k c", k=4)
            wload(wb3, w_branch[l, n].rearrange("(k p) c -> p k c", p=128))
            first = (merged[0] == 0)
            merged[0] += 1
            for j in range(8):
                for (t0, n_) in TBX:
                    pg = bank()
                    for k in range(8):
                        mm(pg[:, 0:n_], gw[j // 4][:, k, (j % 4) * 128:(j % 4 + 1) * 128], hT3[:, k, t0:t0 + n_],
                           start=(k == 0), stop=(k == 7))
                    pp = bank()
                    for k in range(4):
                        mm(pp[:, 0:n_], wb3[:, k, j * 128:(j + 1) * 128], yT3[:, k, t0:t0 + n_],
                           start=(k == 0), stop=(k == 3))
                    sg = rr(tmpf, "tf")
                    act(sg[:, 0:n_], pg[:, 0:n_], AF.Sigmoid)
                    if first:
                        tt(mix3[:, j, t0:t0 + n_], sg[:, 0:n_], pp[:, 0:n_], ALU.mult)
                    else:
                        tt(sg[:, 0:n_], sg[:, 0:n_], pp[:, 0:n_], ALU.mult)
                        tt(mix3[:, j, t0:t0 + n_], sg[:, 0:n_], mix3[:, j, t0:t0 + n_], ALU.add)

        merge(0)

        bank_list[0] = [0, 6, 7]
        acc_sets = [(PS[4],), (PS[5],)]
        CT3 = Q(0, 11520).rearrange("p (c t) -> p c t", c=5)
        QB3 = Q(11520, 16128).rearrange("p (c t) -> p c t", c=2)
        VB3 = Q(16128, 16128 + NTT * 65).rearrange("p (t c) -> p t c", c=65)
        KR3 = Q(17298, 17298 + NTT * 32).rearrange("p (t c) -> p t c", c=32)
        rB = ropeB[:].rearrange("p (s t c) -> p s t c", s=2, t=16)
        s1 = slot()
        s2 = slot()
        w1 = s1[:].rearrange("p (k c) -> p k c", k=8)
        w2 = s2[:, 0:1280].rearrange("p (k c) -> p k c", k=8)
        w_cols(W_in, OFF_CQ, 512, w1)
        w_cols(W_in, OFF_CQ + 512, 160, w2)
        bank_list[0] = list(range(8))

        def b1_mm(tt_):
            tk = slice(tt_ * 128, (tt_ + 1) * 128)
            p1 = bank()
            for k in range(8):
                mm(p1[:], hT3[:, k, tk], w1[:, k, :], start=(k == 0), stop=(k == 7))
            p2 = bank()
            for k in range(8):
                mm(p2[:, 0:160], hT3[:, k, tk], w2[:, k, :], start=(k == 0), stop=(k == 7))
            return p1, p2
        nxt_ = b1_mm(0)
        for tt_ in range(NTT):
            tk = slice(tt_ * 128, (tt_ + 1) * 128)
            p1, p2 = nxt_
            if os.environ.get('NOPIPE_B1') is None:
                nxt_ = b1_mm(tt_ + 1) if tt_ + 1 < NTT else None
            sq = rr(tmpf, "tf")
            sq2 = rr(tmpf, "tf")
            act(sq[:], p1[:], AF.Square)
            act(sq2[:, 0:160], p2[:, 0:160], AF.Square)
            red(small[:, 0:1], sq[:, 0:384])
            red(small[:, 4:5], sq[:, 384:512])
            red(small[:, 5:6], sq2[:, 0:128])
            tt(small[:, 1:2], small[:, 4:5], small[:, 5:6], ALU.add)
            red(small[:, 2:3], sq2[:, 128:160])
            tt(small[:, 8:11], small[:, 0:3], poolfix[:, 32:35], ALU.mult)
            act(small[:, 12:15], small[:, 8:11], AF.Sqrt, bias=EPS)
            recip(small[:, 16:19], small[:, 12:15])
            cb = rr(tmpb, "qkb")
            stt(cb[:, 0:384], p1[:, 0:384], small[:, 16:17], gains[:, G_CQ:G_CQ + 384], ALU.mult, ALU.mult)
            stt(cb[:, 384:512], p1[:, 384:512], small[:, 17:18], gains[:, G_CKV:G_CKV + 128], ALU.mult, ALU.mult)
            cb2 = rr(tmpb, "qkb")
            stt(cb2[:, 0:128], p2[:, 0:128], small[:, 17:18], gains[:, G_CKV + 128:G_CKV + 256], ALU.mult, ALU.mult)
            if tt_ >= 2:
                kr = rr(tmpf, "tf")
                stt(kr[:, 0:32], p2[:, 128:160], small[:, 18:19], gains[:, G_KB + 64:G_KB + 96], ALU.mult, ALU.mult)
                tt(kr[:, 32:64], kr[:, 0:32], rB[:, 0, tt_ - 2, :], ALU.mult)
                tt(kr[:, 64:96], kr[:, 0:32], rB[:, 1, tt_ - 2, :], ALU.mult)
                tt(KR3[:, tt_, 0:16], kr[:, 32:48], kr[:, 80:96], ALU.subtract)
                tt(KR3[:, tt_, 16:32], kr[:, 64:80], kr[:, 48:64], ALU.add)
            else:
                stt(KR3[:, tt_, :], p2[:, 128:160], small[:, 18:19], gains[:, G_KB + 64:G_KB + 96], ALU.mult, ALU.mult)
            pbk = bank()
            pbb = pbk[:].bitcast(BF16)
            for j in range(4):
                tr(pbb[:, j * 128:(j + 1) * 128], cb[:, j * 128:(j + 1) * 128], identb[:])
            tr(pbb[:, 512:640], cb2[:, 0:128], identb[:])
            cp(CT3[:, :, tk], pbb[:, 0:640].rearrange("p (c t) -> p c t", c=5), eng="scalar")
            if os.environ.get('NOPIPE_B1') is not None:
                nxt_ = b1_mm(tt_ + 1) if tt_ + 1 < NTT else None
        bank_list[0] = [0, 6, 7]
        mset(QB3[64:128, :, :], 0.0)
        su = slot()
        sk = slot()
        wuq3 = su[:, 0:2304].rearrange("p (k c) -> p k c", k=3)
        wukv3 = sk[:, 0:2048].rearrange("p (k c) -> p k c", k=2)
        wload(wuq3, w_uq[l].rearrange("(k p) c -> p k c", p=128))
        wload(wukv3, w_ukv[l].rearrange("(k p) c -> p k c", p=128))
        yb4 = ybtok[:].rearrange("p (t h c) -> p t h c", t=NTT, h=2)
        for h in range(8):
            mset(VB3[:, :, 64:65], 1.0)
            def b2_mm(tt_, par, h=h):
                tk = slice(tt_ * 128, (tt_ + 1) * 128)
                pq = PS[par]
                for k in range(2):
                    mm(pq[:, 0:128], CT3[:, 3 + k, tk], wukv3[:, k, h * 128:(h + 1) * 128], start=(k == 0), stop=(k == 1))
                for k in range(3):
                    mm(pq[:, 128:224], CT3[:, k, tk], wuq3[:, k, h * 96:(h + 1) * 96], start=(k == 0), stop=(k == 2),
                       skip=True)
                return pq
            def b2_chain(tt_, pq, par, h=h):
                tk = slice(tt_ * 128, (tt_ + 1) * 128)
                sm = smalls[par]
                sq = tmpf[par]
                kr = xblk[par // 2][:, (par % 2) * 512:(par % 2) * 512 + 512]
                qk = tmpb[par]
                act(sq[:, 0:224], pq[:, 0:224], AF.Square)
                yield
                red(sm[:, 0:7], sq[:, 0:224].rearrange("p (g c) -> p g c", c=32))
                yield
                r8 = sm[:, 0:8].rearrange("p (a b) -> p a b", b=4)
                tt(sm[:, 8:10], r8[:, :, 0], r8[:, :, 1], ALU.add)
                ts(sm[:, 10:11], sm[:, 6:7], 2.0, ALU.mult)
                cp(VB3[:, tt_, 0:64], pq[:, 64:128])
                yield
                act(sm[:, 12:15], sm[:, 8:11], AF.Sqrt, bias=EPS, scale=1.0 / 64)
                yield
                recip(sm[:, 16:19], sm[:, 12:15])
                yield
                stt(qk[:, 0:64], pq[:, 128:192], sm[:, 17:18], gains[:, G_QB:G_QB + 64], ALU.mult, ALU.mult)
                stt(qk[:, 96:160], pq[:, 0:64], sm[:, 16:17], gains[:, G_KB:G_KB + 64], ALU.mult, ALU.mult)
                if tt_ >= 2:
                    stt(kr[:, 0:32], pq[:, 192:224], sm[:, 18:19], gains[:, G_QB + 64:G_QB + 96], ALU.mult, ALU.mult)
                else:
                    stt(qk[:, 64:96], pq[:, 192:224], sm[:, 18:19], gains[:, G_QB + 64:G_QB + 96], ALU.mult, ALU.mult)
                cp(qk[:, 160:192], KR3[:, tt_, :], eng="gpsimd")
                yield
                if tt_ >= 2:
                    tt(kr[:, 32:64], kr[:, 0:32], rB[:, 0, tt_ - 2, :], ALU.mult, eng="gpsimd")
                    tt(kr[:, 64:96], kr[:, 0:32], rB[:, 1, tt_ - 2, :], ALU.mult, eng="gpsimd")
                    yield
                    tt(qk[:, 64:80], kr[:, 32:48], kr[:, 80:96], ALU.subtract, eng="gpsimd")
                    tt(qk[:, 80:96], kr[:, 64:80], kr[:, 48:64], ALU.add, eng="gpsimd")
                    yield
                pbk = PS[4 + par]
                pbb = pbk[:].bitcast(BF16)
                tr(pbb[0:96, 0:128], qk[:, 0:96], identb[:])
                tr(pbb[0:96, 128:256], qk[:, 96:192], identb[:])
                yield
                cp(QB3[0:96, :, tk], pbb[0:96, 0:256].rearrange("p (c t) -> p c t", c=2), eng="scalar")

            for g0 in range(0, NTT, 4):
                grp = list(range(g0, min(g0 + 4, NTT)))
                pqs = [b2_mm(t_, i_) for i_, t_ in enumerate(grp)]
                interleave([b2_chain(t_, pqs[i_], i_) for i_, t_ in enumerate(grp)])
            bank_list[0] = [0, 6, 7]
            qblocks = [(256 + i * 512, 512, list(range(NTT))) for i in range(4)]
            if not last:
                qblocks.append((0, 256, [0, 1]))
            for (q0, nq, kts) in qblocks:
                def finb(accs, nqt, per_bank, q0=q0, h=h):
                    a3 = accs[0][:, 0:nqt * 65].rearrange("p (t c) -> p t c", c=65)
                    recip(small[:, 24:24 + nqt], a3[:, :, 64])
                    tt(yb4[:, q0 // 128:q0 // 128 + nqt, h % 2, :], a3[:, :, 0:64],
                       small[:, 24:24 + nqt].unsqueeze(2).to_broadcast([128, nqt, 64]), ALU.mult)
                attention(QB3[:, 0, :], QB3[:, 1, :], lambda kt: VB3[:, kt, :], 64, 96.0 ** -0.5,
                          q0, nq, kts, finb)
            run_units()
            if h % 2 == 1:
                tiles = list(range(2, NTT)) if last else list(range(NTT))
                for g0 in range(0, len(tiles), 4):
                    grp = tiles[g0:g0 + 4]
                    pbk = bank()
                    pbb = pbk[:].bitcast(BF16)
                    for i, tq in enumerate(grp):
                        tr(pbb[:, i * 128:(i + 1) * 128], ybtok[:, tq * 128:(tq + 1) * 128], identb[:])
                    cp(yT3[:, h // 2, grp[0] * 128:(grp[-1] + 1) * 128], pbb[:, 0:len(grp) * 128], eng="scalar")
        merge(1)

        bank_list[0] = list(range(8))
        UP = [Q(i * 4736, (i + 1) * 4736).bitcast(F32) for i in range(3)]
        DTp = Q(14208, 14208 + PADL)
        su_ = slot()
        wu3 = su_[:].rearrange("p (k c) -> p k c", k=8)
        w_cols(W_in, OFF_U, 512, wu3)
        sp_ = slot()
        wp3 = sp_[:, 0:512].rearrange("p (g d) -> p g d", g=4)
        wload(wp3, w_pool[l].rearrange("g c d -> c g d"))
        for g in range(4):
            w = 2 << g
            A_, B_, C_ = UP
            for (a, b) in ((0, 16), (272, 304), (2352, 2368)):
                mset(A_[:, a:b], 0.0)

            def ev_u(j, t0, n, pb, A_=A_):
                cp(A_[:, pcol(t0):pcol(t0) + n], pb[:, 0:n], eng="scalar")
            fpass(wu3, [g], hT3, 8, TB_ALL, ev_u)
            src, dst = A_, B_
            tt(dst[:, 1:PADL], src[:, 0:PADL - 1], src[:, 1:PADL], ALU.add)
            lo, hi = 1, PADL
            if g >= 1:
                src, dst = B_, C_
                tt(dst[:, 2:PADL - 1], src[:, 1:PADL - 2], src[:, 3:PADL], ALU.add)
            if g >= 2:
                src, dst = C_, B_
                tt(dst[:, 4:PADL - 3], src[:, 2:PADL - 5], src[:, 6:PADL - 1], ALU.add)
            if g >= 3:
                src, dst = B_, C_
                tt(dst[:, 8:PADL - 7], src[:, 4:PADL - 11], src[:, 12:PADL - 3], ALU.add)
            Lg = dst
            ts(Lg[:, 16:2352], Lg[:, 16:2352], 1.0 / w, ALU.mult)
            for (s0, sl) in ((16, 256), (304, 2048)):
                tt(Lg[:, s0:s0 + 8], Lg[:, s0:s0 + 8], poolfix[:, g * 8:g * 8 + 8], ALU.mult)
                tt(Lg[:, s0 + sl - 8:s0 + sl], Lg[:, s0 + sl - 8:s0 + sl], poolfix[:, 40 + g * 8:48 + g * 8], ALU.mult)
            tt(DTp[:, 16:2352], Lg[:, 16:2352], A_[:, 16:2352], ALU.subtract)
            for (t0, n) in TB_ALL:
                pb = bank()
                mm(pb[:, 0:n], wp3[:, g, :], DTp[:, pcol(t0):pcol(t0) + n])
                act(yT3[:, g, t0:t0 + n], pb[:, 0:n], AF.Identity, scale=cols[:, 256 + l * 4 + g:257 + l * 4 + g])
        merge(2)

        PCs, U2, YS = UP
        for g in range(4):
            sd = slot()
            wd4 = sd[:, 0:3072].rearrange("p (k s c) -> p k s c", k=8, s=3)
            for si, off in enumerate((OFF_PC, OFF_PX, OFF_PB)):
                w_cols(W_in, off + g * 128, 128, wd4[:, :, si, :])
            wm_ = mod_load(l + 1, 8 + g) if not last else None
            for (a, b) in ((0, 16), (272, 304), (2352, 2368)):
                mset(U2[:, a:b], 0.0)

            def ev_pc(j, t0, n, pb):
                cp(PCs[:, pcol(t0):pcol(t0) + n], pb[:, 0:n], eng="scalar")

            def ev_px(j, t0, n, pb):
                tt(U2[:, pcol(t0):pcol(t0) + n], pb[:, 0:n], PCs[:, pcol(t0):pcol(t0) + n], ALU.mult)

            def ev_pb(j, t0, n, pb, g=g):
                tt(yT3[:, g, t0:t0 + n], pb[:, 0:n], YS[:, pcol(t0):pcol(t0) + n], ALU.mult)
            fpass(wd4[:, :, 0, :], [0], hT3, 8, TB_ALL, ev_pc)
            fpass(wd4[:, :, 1, :], [0], hT3, 8, TB_ALL, ev_px)
            if wm_ is not None:
                mod_mm(l + 1, 8 + g, wm_)
            wc = 272 + l * 12
            ts(YS[:, 16:2352], U2[:, 16:2352], cols[:, wc + 4 + g:wc + 5 + g], ALU.mult)
            stt(YS[:, 16:2352], U2[:, 15:2351], cols[:, wc + g:wc + g + 1], YS[:, 16:2352], ALU.mult, ALU.add)
            stt(YS[:, 16:2352], U2[:, 17:2353], cols[:, wc + 8 + g:wc + 9 + g], YS[:, 16:2352], ALU.mult, ALU.add)
            fpass(wd4[:, :, 2, :], [0], hT3, 8, TB_ALL, ev_pb)
        merge(3)

        so1 = slot()
        so2 = slot()
        wo = [so1[:].rearrange("p (k c) -> p k c", k=8), so2[:].rearrange("p (k c) -> p k c", k=8)]
        w_cols(w_o[l], 0, 512, wo[0])
        w_cols(w_o[l], 512, 512, wo[1])
        bufs8 = [xblk[0][:, 0:512], xblk[0][:, 512:1024], xblk[1][:, 0:512], xblk[1][:, 512:1024]] + [t_[:] for t_ in tmpf]
        bank_list[0] = list(range(7))
        for (t0, n) in TBX:
            st = 1 if t0 == 0 else 0
            pbn = PS[7]
            for i in range(8):
                dma(bufs8[i][:, 0:n], xs_d[i][:, t0:t0 + n], rd=[("xs", i, i + 1, t0 * 4, (t0 + n) * 4)])
            for i in range(8):
                xb = bufs8[i]
                pb = bank()
                for k in range(8):
                    mm(pb[:, 0:n], wo[i // 4][:, k, (i % 4) * 128:(i % 4 + 1) * 128], mix3[:, k, t0:t0 + n],
                       start=(k == 0), stop=(k == 7))
                gi = (16 + i) * 2 + st
                stt(xb[:, 0:n], pb[:, 0:n], modS[:, gi:gi + 1], xb[:, 0:n], ALU.mult, ALU.add)
                dma(xs_d[i][:, t0:t0 + n], xb[:, 0:n], wr=[("xs", i, i + 1, t0 * 4, (t0 + n) * 4)])
                sq = rr(tmpb, "sq")
                act(sq[:, 0:n], xb[:, 0:n], AF.Square)
                mm(pbn[:, 0:n], onesb[:], sq[:, 0:n], start=(i == 0), stop=(i == 7))
            act(oA[:, 0:n], pbn[:, 0:n], AF.Sqrt, bias=EPS, scale=1.0 / D)
            recip(rstd_bc[:, 0:n], oA[:, 0:n])
            for i in range(8):
                xb = bufs8[i]
                tt(xb[:, 0:n], xb[:, 0:n], rstd_bc[:, 0:n], ALU.mult)
                act(hT3[:, i, t0:t0 + n], xb[:, 0:n], AF.Identity, bias=modS[:, (24 + i) * 2 + st:(24 + i) * 2 + st + 1],
                    scale=gsS[:, 16 + i * 2 + st:16 + i * 2 + st + 1])
        bank_list[0] = list(range(8))
        for k in range(8):
            dma(xT3[:, k, :], xs_d[k], rd=[("xs", k, k + 1, 0, NT * 4)])

        for g in range(8):
            s1 = slot()
            s2 = slot()
            w13 = s1[:].rearrange("p (k c) -> p k c", k=8)
            w23 = s2[:].rearrange("p (k c) -> p k c", k=4)
            w_cols(w_ff1[l], g * 512, 512, w13)
            wload(w23, w_ff2[l][g * 512:(g + 1) * 512, :].rearrange("(k p) c -> p k c", p=128))
            wm_ = mod_load(l + 1, g) if not last else None

            def ev_a(j, t0, n, pb):
                r = rr(tmpf, "tf")
                act(r[:, 0:n], pb[:, 0:n], AF.Relu)
                tt(aT3[:, j, t0:t0 + n], r[:, 0:n], r[:, 0:n], ALU.mult)
            fpass(w13, range(4), hT3, 8, TBX, ev_a)
            if wm_ is not None:
                mod_mm(l + 1, g, wm_)

            def ev_x(i, t0, n, pb):
                gi = (40 + i) * 2 + (1 if t0 == 0 else 0)
                stt(xT3[:, i, t0:t0 + n], pb[:, 0:n], modS[:, gi:gi + 1], xT3[:, i, t0:t0 + n], ALU.mult, ALU.add)
            fpass(w23, range(8), aT3, 4, TBX, ev_x)

    for tt_ in range(2, NTT):
        xb = rr(xblk, "xin")
        for half in range(2):
            pb = bank()
            for j in range(4):
                k = half * 4 + j
                tr(pb[:, j * 128:(j + 1) * 128], xT3[:, k, tt_ * 128:(tt_ + 1) * 128], identf[:])
            cp(xb[:, half * 512:(half + 1) * 512], pb[:], eng=("vector" if half == 0 else "scalar"))
        dma(y_d[(tt_ - 2) * 128:(tt_ - 1) * 128, :], xb[:])
    S.add("sync", lambda e: e.nop(), [y_d], [])
    S.emit()
    es.close()
    return nc


def _consts():
    identf = np.eye(128, dtype=np.float32)

    def rope(rot_dim):
        rows = S_LAT // 64
        row = np.repeat(np.arange(rows, dtype=np.float32), 64)
        col = np.tile(np.arange(64, dtype=np.float32), rows)
        n_freq = rot_dim // 4
        inv = (np.float32(10000.0) ** (-np.arange(n_freq, dtype=np.float32) / np.float32(n_freq))).astype(np.float32)
        ang = np.concatenate([row[:, None] * inv, col[:, None] * inv], axis=-1).astype(np.float32)
        cs, sn = np.cos(ang).astype(np.float32), np.sin(ang).astype(np.float32)
        half = rot_dim // 2
        cc = np.concatenate([cs, cs], -1).reshape(16, 128, 2 * half).transpose(1, 0, 2)
        ss = np.concatenate([sn, sn], -1).reshape(16, 128, 2 * half).transpose(1, 0, 2)
        return np.ascontiguousarray(np.stack([cc, ss], 1).reshape(128, -1)).astype(np.float32)

    fix = np.ones((128,), np.float32)
    for g, w in enumerate((2, 4, 8, 16)):
        for t in range(w // 2):
            fix[g * 8 + t] = w / (t + w // 2)
        for j in range(1, w // 2):
            fix[40 + g * 8 + (8 - j)] = w / (j + w // 2)
    fix[32:35] = (1.0 / 384, 1.0 / 256, 1.0 / 32)
    poolfix = np.tile(fix[None, :], (128, 1)).astype(np.float32)
    return identf, rope(64), rope(32), poolfix


_NC_CACHE = {}


def kernel(x, c, ctx, c_ctx, w_mod, b_mod, g_norm1, g_norm2, w_in, gq_a, gk_a, lam_a, g_sub_a,
           g_cq, w_uq, g_ckv, w_ukv, gq_b, gk_b, w_pool, s_pool, w_conv, w_branch, w_o, w_ff1, w_ff2,
           _cores=8):
    f = lambda a: np.ascontiguousarray(np.asarray(a, dtype=np.float32))
    x, c, ctx, c_ctx = f(x), f(c), f(ctx), f(c_ctx)
    L = int(np.asarray(w_mod).shape[0])
    identf, ropeA, ropeB, poolfix = _consts()
    gains = np.concatenate([f(gq_a), f(gk_a), f(g_sub_a), f(g_cq), f(g_ckv), f(gq_b), f(gk_b),
                            f(lam_a).reshape(L, 256)], axis=1)
    shared = dict(w_mod=f(w_mod), w_in=f(w_in), gains=np.ascontiguousarray(gains), w_uq=f(w_uq), w_ukv=f(w_ukv),
                  w_pool=f(w_pool), w_branch=f(w_branch), w_o=f(w_o), w_ff1=f(w_ff1), w_ff2=f(w_ff2),
                  identf=identf, ropeA=ropeA, ropeB=ropeB, poolfix=poolfix)
    rows = np.zeros((384, 128), np.float32)
    rows[0:L * 48] = f(b_mod).reshape(L * 48, 128)
    rows[192:192 + L * 8] = f(g_norm1).reshape(L * 8, 128)
    rows[224:224 + L * 8] = f(g_norm2).reshape(L * 8, 128)
    rows[256:256 + L * 4] = f(s_pool).reshape(L * 4, 128)
    rows[272:272 + L * 12] = f(w_conv).reshape(L * 12, 128)
    rows[328:336] = c_ctx.reshape(8, 128)
    if L not in _NC_CACHE:
        _NC_CACHE[L] = build(L)
    nc = _NC_CACHE[L]
    in_maps = []
    for b in range(_cores):
        r = rows.copy()
        r[320:328] = c[b].reshape(8, 128)
        m = dict(shared)
        m.update(x=x[b], ctx=ctx[b], rows=np.ascontiguousarray(r.reshape(3, 128, 128)))
        in_maps.append(m)
    res = run_bass_kernel_spmd(nc, in_maps, core_ids=list(range(_cores)))
    return np.stack([np.asarray(res.results[b]["y"], dtype=np.float32) for b in range(_cores)], axis=0)
```
